# Optimizing a Trainium2 kernel written in Bass

```python
import jax, jax.numpy as jnp
from jax import lax
import numpy as np

D_MODEL = 2048
BATCH = 2
SEQ = 8192
DEPTH = 4

MEM_LEN = 256
EPS = 1e-6
NEG = -1e30
ATT_HEADS = 8
ATT_HEAD_DIM = 128
ATT_WIDTH = ATT_HEADS * ATT_HEAD_DIM
MOBA_BLOCK = 256
MOBA_TOPK = 3
MOBA_QCHUNK = 64
ROPE_THETA = 10000.0
SSD_WIDTH = D_MODEL
SSD_HEAD_DIM = 64
SSD_HEADS = SSD_WIDTH // SSD_HEAD_DIM
SSD_GROUPS = 4
SSD_STATE = 128
SSD_CONV = 4
SSD_CHUNK = 256
SSD_CONV_DIM = SSD_WIDTH + 2 * SSD_GROUPS * SSD_STATE
MEM_HEADS = 4
MEM_HEAD_DIM = 256
MEM_WIDTH = MEM_HEADS * MEM_HEAD_DIM
D_FF = 5632
FFN_CONV = 3
IN_SIZES = (ATT_WIDTH, ATT_WIDTH, ATT_WIDTH, SSD_WIDTH, SSD_CONV_DIM, SSD_HEADS, MEM_WIDTH, D_MODEL, D_MODEL, D_MODEL)
N_IN = 3 * ATT_WIDTH + SSD_WIDTH + SSD_CONV_DIM + SSD_HEADS + MEM_WIDTH + 3 * D_MODEL

kernel_name = 'hybrid_moba_ssd_memory_gated_block'


def _split(t, sizes):
    idx = []
    acc = 0
    for s in sizes[:-1]:
        acc += s
        idx.append(acc)
    return jnp.split(t, idx, axis=-1)


def rmsnorm(x, w):
    xf = x.astype(jnp.float32)
    y = xf * lax.rsqrt(jnp.mean(xf * xf, axis=-1, keepdims=True) + EPS) * w.astype(jnp.float32)
    return y.astype(x.dtype)


def group_rmsnorm(y, w, groups):
    shp = y.shape
    yf = y.astype(jnp.float32).reshape(shp[:-1] + (groups, shp[-1] // groups))
    yf = yf * lax.rsqrt(jnp.mean(yf * yf, axis=-1, keepdims=True) + EPS)
    return yf.reshape(shp) * w.astype(jnp.float32)


def causal_dwconv(x, w, b):
    W, C = w.shape
    y = lax.conv_general_dilated(x, w[:, None, :].astype(x.dtype), window_strides=(1,),
                                 padding=[(W - 1, 0)], dimension_numbers=('NWC', 'WIO', 'NWC'),
                                 feature_group_count=C)
    return y + b.astype(x.dtype)


def rope(t, pos):
    half = t.shape[-1] // 2
    inv = ROPE_THETA ** (-jnp.arange(half, dtype=jnp.float32) / half)
    ang = pos[:, None] * inv[None, :]
    cos, sin = jnp.cos(ang), jnp.sin(ang)
    tf = t.astype(jnp.float32)
    t1, t2 = tf[..., :half], tf[..., half:]
    return jnp.concatenate([t1 * cos - t2 * sin, t2 * cos + t1 * sin], axis=-1).astype(t.dtype)


def moba_attention(q, k, v):
    B, H, S, Dh = q.shape
    nb = S // MOBA_BLOCK
    scale = Dh ** -0.5
    kb = k.reshape(B, H, nb, MOBA_BLOCK, Dh)
    vb = v.reshape(B, H, nb, MOBA_BLOCK, Dh)
    kmean = jnp.mean(kb.astype(jnp.float32), axis=3)
    gate = jnp.einsum('bhsd,bhnd->bhsn', q.astype(jnp.float32), kmean)
    qblk = jnp.arange(S) // MOBA_BLOCK
    past = jnp.arange(nb)[None, :] < qblk[:, None]
    gate = jnp.where(past[None, None], gate, NEG)
    k_sel = min(MOBA_TOPK, max(nb - 1, 1))
    _, idx = lax.top_k(gate, k_sel)
    valid = idx < qblk[None, None, :, None]
    bi = jnp.arange(B)[:, None, None, None]
    hi = jnp.arange(H)[None, :, None, None]
    QC = MOBA_QCHUNK

    def chunk(c):
        s0 = c * QC
        qc = lax.dynamic_slice_in_dim(q, s0, QC, axis=2)
        ic = lax.dynamic_slice_in_dim(idx, s0, QC, axis=2)
        vc = lax.dynamic_slice_in_dim(valid, s0, QC, axis=2)
        kg = kb[bi, hi, ic]
        vg = vb[bi, hi, ic]
        s_sel = jnp.einsum('bhqd,bhqnjd->bhqnj', qc, kg).astype(jnp.float32) * scale
        s_sel = jnp.where(vc[..., None], s_sel, NEG)
        blk = s0 // MOBA_BLOCK
        k_own = lax.dynamic_index_in_dim(kb, blk, axis=2, keepdims=False)
        v_own = lax.dynamic_index_in_dim(vb, blk, axis=2, keepdims=False)
        s_own = jnp.einsum('bhqd,bhjd->bhqj', qc, k_own).astype(jnp.float32) * scale
        qpos = s0 + jnp.arange(QC)
        kpos = blk * MOBA_BLOCK + jnp.arange(MOBA_BLOCK)
        s_own = jnp.where(kpos[None, :] <= qpos[:, None], s_own, NEG)
        logits = jnp.concatenate([s_sel.reshape(B, H, QC, k_sel * MOBA_BLOCK), s_own], axis=-1)
        p = jax.nn.softmax(logits, axis=-1).astype(v.dtype)
        p_sel = p[..., :k_sel * MOBA_BLOCK].reshape(B, H, QC, k_sel, MOBA_BLOCK)
        p_own = p[..., k_sel * MOBA_BLOCK:]
        return (jnp.einsum('bhqnj,bhqnjd->bhqd', p_sel, vg)
                + jnp.einsum('bhqj,bhjd->bhqd', p_own, v_own))

    outs = lax.map(chunk, jnp.arange(S // QC))
    return outs.transpose(1, 0, 3, 2, 4).reshape(B, S, H * Dh)


def ssd_scan(X, A, Bm, Cm):
    b, S, g, j, p = X.shape
    n = Bm.shape[-1]
    Q = SSD_CHUNK
    nc = S // Q
    X = X.reshape(b, nc, Q, g, j, p)
    A = A.reshape(b, nc, Q, g, j).transpose(0, 3, 4, 1, 2)
    Bc = Bm.reshape(b, nc, Q, g, n)
    Cc = Cm.reshape(b, nc, Q, g, n)
    A_cs = jnp.cumsum(A, axis=-1)
    causal = jnp.tril(jnp.ones((Q, Q), dtype=bool))
    Lmat = jnp.exp(jnp.where(causal, A_cs[..., :, None] - A_cs[..., None, :], -jnp.inf))
    CB = jnp.einsum('bclgn,bcsgn->bcgls', Cc, Bc)
    Y_diag = jnp.einsum('bcgls,bgjcls,bcsgjp->bclgjp', CB, Lmat, X)
    decay_states = jnp.exp(A_cs[..., -1:] - A_cs)
    states = jnp.einsum('bclgn,bgjcl,bclgjp->bcgjpn', Bc, decay_states, X)
    chunk_decay = jnp.exp(A_cs[..., -1])

    def step(h, inp):
        st, dec = inp
        return h * dec[..., None, None] + st, h

    h0 = jnp.zeros_like(states[:, 0])
    _, prev = lax.scan(step, h0, (states.transpose(1, 0, 2, 3, 4, 5), chunk_decay.transpose(3, 0, 1, 2)))
    prev = prev.transpose(1, 0, 2, 3, 4, 5)
    Y_off = jnp.einsum('bclgn,bcgjpn,bgjcl->bclgjp', Cc, prev, jnp.exp(A_cs))
    return (Y_diag + Y_off).reshape(b, S, g, j, p)


def ssd_mixer(z, xbc, dt_raw, conv_w, conv_b, dt_bias, a_log, d_skip, norm_w):
    B, S, _ = xbc.shape
    hpg = SSD_HEADS // SSD_GROUPS
    xbc = jax.nn.silu(causal_dwconv(xbc, conv_w, conv_b))
    xs, Bm, Cm = _split(xbc, (SSD_WIDTH, SSD_GROUPS * SSD_STATE, SSD_GROUPS * SSD_STATE))
    xs = xs.reshape(B, S, SSD_GROUPS, hpg, SSD_HEAD_DIM)
    Bm = Bm.reshape(B, S, SSD_GROUPS, SSD_STATE)
    Cm = Cm.reshape(B, S, SSD_GROUPS, SSD_STATE)
    dt = jax.nn.softplus(dt_raw.astype(jnp.float32) + dt_bias.astype(jnp.float32))
    A = -jnp.exp(a_log.astype(jnp.float32))
    dt_g = dt.reshape(B, S, SSD_GROUPS, hpg)
    y = ssd_scan(xs * dt_g[..., None], dt_g * A.reshape(SSD_GROUPS, hpg), Bm, Cm)
    y = y + xs * d_skip.reshape(SSD_GROUPS, hpg, 1)
    y = y.reshape(B, S, SSD_WIDTH) * jax.nn.silu(z.astype(jnp.float32))
    return group_rmsnorm(y, norm_w, SSD_GROUPS)


def memory_attention(q_m, mem_n, w_mem_kv):
    B, S, _ = q_m.shape
    q = q_m.reshape(B, S, MEM_HEADS, MEM_HEAD_DIM)
    k, v = jnp.split(mem_n @ w_mem_kv, 2, axis=-1)
    k = k.reshape(B, -1, MEM_HEADS, MEM_HEAD_DIM)
    v = v.reshape(B, -1, MEM_HEADS, MEM_HEAD_DIM)
    s = jnp.einsum('bshd,bmhd->bhsm', q, k).astype(jnp.float32) * (MEM_HEAD_DIM ** -0.5)
    p = jax.nn.softmax(s, axis=-1).astype(v.dtype)
    return jnp.einsum('bhsm,bmhd->bshd', p, v).reshape(B, S, MEM_WIDTH)


def token_mixing(h, mem_n, w_in, conv_ssd_w, conv_ssd_b, dt_bias, a_log, d_skip, ssd_norm,
                 w_mem_kv, w_br_attn, w_br_ssd, w_br_mem, w_out):
    B, S, _ = h.shape
    Sp = -(-S // MOBA_BLOCK) * MOBA_BLOCK
    hp = jnp.pad(h, ((0, 0), (0, Sp - S), (0, 0)))
    q_a, k_a, v_a, z, xbc, dt_raw, q_m, g_a, g_s, g_m = _split(hp @ w_in, IN_SIZES)
    heads = lambda t: t.reshape(B, Sp, ATT_HEADS, ATT_HEAD_DIM).transpose(0, 2, 1, 3)
    pos = jnp.arange(Sp, dtype=jnp.float32)
    y_a = moba_attention(rope(heads(q_a), pos), rope(heads(k_a), pos), heads(v_a)).astype(h.dtype)
    y_s = ssd_mixer(z, xbc, dt_raw, conv_ssd_w, conv_ssd_b, dt_bias, a_log, d_skip, ssd_norm).astype(h.dtype)
    y_m = memory_attention(q_m, mem_n, w_mem_kv)
    gate = lambda t: jax.nn.sigmoid(t.astype(jnp.float32)).astype(h.dtype)
    merged = (gate(g_a) * (y_a @ w_br_attn) + gate(g_s) * (y_s @ w_br_ssd)
              + gate(g_m) * (y_m @ w_br_mem))
    return (merged @ w_out)[:, :S]


def conv_glu_ffn(h, w_up, conv_w, conv_b, w_down):
    u = causal_dwconv(h @ w_up, conv_w, conv_b)
    a, g = jnp.split(u, 2, axis=-1)
    return (jax.nn.gelu(a, approximate=True) * g) @ w_down


def setup_inputs(seed: int = 0) -> dict:
    key = jax.random.key(seed)
    ks = jax.random.split(key, 24)
    f32 = jnp.float32
    L = DEPTH
    nrm = lambda k, shp, s: jax.random.normal(k, shp, f32) * s
    gain = lambda k: 1.0 + 0.05 * jax.random.normal(k, (L, D_MODEL), f32)
    dt0 = jnp.exp(jax.random.uniform(ks[10], (L, SSD_HEADS), f32, np.log(1e-3), np.log(1e-1)))
    return {
        'x': nrm(ks[0], (BATCH, SEQ, D_MODEL), 1.0),
        'mem': nrm(ks[1], (BATCH, MEM_LEN, D_MODEL), 1.0),
        'norm_mix_pre': gain(ks[2]),
        'norm_mix_post': gain(ks[3]),
        'norm_ffn_pre': gain(ks[4]),
        'norm_ffn_post': gain(ks[5]),
        'norm_mem': gain(ks[6]),
        'w_in': nrm(ks[7], (L, D_MODEL, N_IN), D_MODEL ** -0.5),
        'conv_ssd_w': nrm(ks[8], (L, SSD_CONV, SSD_CONV_DIM), SSD_CONV ** -0.5),
        'conv_ssd_b': nrm(ks[9], (L, SSD_CONV_DIM), 0.02),
        'dt_bias': dt0 + jnp.log(-jnp.expm1(-dt0)),
        'a_log': jnp.log(jax.random.uniform(ks[11], (L, SSD_HEADS), f32, 1.0, 16.0)),
        'd_skip': 1.0 + 0.1 * jax.random.normal(ks[12], (L, SSD_HEADS), f32),
        'ssd_norm': 1.0 + 0.05 * jax.random.normal(ks[13], (L, SSD_WIDTH), f32),
        'w_mem_kv': nrm(ks[14], (L, D_MODEL, 2 * MEM_WIDTH), D_MODEL ** -0.5),
        'w_br_attn': nrm(ks[15], (L, ATT_WIDTH, D_MODEL), ATT_WIDTH ** -0.5),
        'w_br_ssd': nrm(ks[16], (L, SSD_WIDTH, D_MODEL), SSD_WIDTH ** -0.5),
        'w_br_mem': nrm(ks[17], (L, MEM_WIDTH, D_MODEL), MEM_WIDTH ** -0.5),
        'w_out': nrm(ks[18], (L, D_MODEL, D_MODEL), D_MODEL ** -0.5),
        'w_up': nrm(ks[19], (L, D_MODEL, 2 * D_FF), D_MODEL ** -0.5),
        'conv_ffn_w': nrm(ks[20], (L, FFN_CONV, 2 * D_FF), FFN_CONV ** -0.5),
        'conv_ffn_b': nrm(ks[21], (L, 2 * D_FF), 0.02),
        'w_down': nrm(ks[22], (L, D_FF, D_MODEL), D_FF ** -0.5),
    }


def reference(x, mem, norm_mix_pre, norm_mix_post, norm_ffn_pre, norm_ffn_post, norm_mem,
              w_in, conv_ssd_w, conv_ssd_b, dt_bias, a_log, d_skip, ssd_norm, w_mem_kv,
              w_br_attn, w_br_ssd, w_br_mem, w_out, w_up, conv_ffn_w, conv_ffn_b, w_down):
    for l in range(DEPTH):
        h = rmsnorm(x, norm_mix_pre[l])
        mem_n = rmsnorm(mem, norm_mem[l])
        y = token_mixing(h, mem_n, w_in[l], conv_ssd_w[l], conv_ssd_b[l], dt_bias[l], a_log[l],
                         d_skip[l], ssd_norm[l], w_mem_kv[l], w_br_attn[l], w_br_ssd[l],
                         w_br_mem[l], w_out[l])
        x = x + rmsnorm(y, norm_mix_post[l]).astype(x.dtype)
        h = rmsnorm(x, norm_ffn_pre[l])
        y = conv_glu_ffn(h, w_up[l], conv_ffn_w[l], conv_ffn_b[l], w_down[l])
        x = x + rmsnorm(y, norm_ffn_post[l]).astype(x.dtype)
    return x
```

```python
import contextlib
import numpy as np
import ml_dtypes
import concourse.bass as bass
import concourse.mybir as mybir
from concourse.bass_utils import run_bass_kernel_spmd

F32 = mybir.dt.float32
BF16 = mybir.dt.bfloat16
AF = mybir.ActivationFunctionType
ALU = mybir.AluOpType
AX = mybir.AxisListType
NPBF = ml_dtypes.bfloat16

D = 2048
KC = D // 128
NCORES = 8
EPS = 1e-6
ENGS = ("pe", "act", "dve", "pool", "sp")


class Trk:
    __slots__ = ("W", "R", "name")

    def __init__(self, name=""):
        self.W = {}
        self.R = {}
        self.name = name


class Prog:
    def __init__(self, nc, stack):
        self.nc = nc
        self.stack = stack
        self.ops = {e: [] for e in ENGS}
        self.cnt = {}
        self.seen = {e: {} for e in ENGS}
        self.esem = {}
        for e in ENGS:
            if e != "sp":
                s = stack.enter_context(nc.semaphore("c_" + e))
                self.esem[e] = s
                self.cnt[id(s)] = 0
        self.dsems = {}
        self.semobj = {}
        self.nuniq = 0

    def sbuf(self, shape, dt, name=None):
        self.nuniq += 1
        return self.stack.enter_context(self.nc.sbuf_tensor(name or f"sb{self.nuniq}", list(shape), dt))

    def psum(self, shape, dt, name=None):
        self.nuniq += 1
        return self.stack.enter_context(self.nc.psum_tensor(name or f"ps{self.nuniq}", list(shape), dt))

    def dma_sem(self, key):
        if key not in self.dsems:
            s = self.stack.enter_context(self.nc.semaphore("d_" + str(key)))
            self.dsems[key] = s
            self.cnt[id(s)] = 0
        return self.dsems[key]

    def _waits(self, eng, reads, writes):
        need = {}
        objs = {}
        for t in reads:
            for k, (s, v) in t.W.items():
                if need.get(k, 0) < v:
                    need[k] = v
                    objs[k] = s
        for t in writes:
            for dct in (t.W, t.R):
                for k, (s, v) in dct.items():
                    if need.get(k, 0) < v:
                        need[k] = v
                        objs[k] = s
        out = []
        seen = self.seen[eng]
        own = id(self.esem[eng]) if eng in self.esem else None
        for k, v in need.items():
            if eng == "pe" and k == own:
                continue
            if seen.get(k, 0) >= v:
                continue
            seen[k] = v
            out.append((objs[k], v))
        return out

    def _record(self, sem, val, reads, writes):
        k = id(sem)
        for t in reads:
            t.R[k] = (sem, val)
        for t in writes:
            t.W[k] = (sem, val)

    def op(self, eng, fn, reads=(), writes=()):
        waits = self._waits(eng, reads, writes)
        sem = self.esem[eng]
        self.cnt[id(sem)] += 1
        val = self.cnt[id(sem)]
        self._record(sem, val, reads, writes)
        self.ops[eng].append((waits, fn, sem, 1))

    def dma(self, q, out, in_, reads=(), writes=(), key="d", **kw):
        waits = self._waits(q, reads, writes)
        sem = self.dma_sem(key)
        self.cnt[id(sem)] += 16
        val = self.cnt[id(sem)]
        self._record(sem, val, reads, writes)
        self.ops[q].append((waits, (lambda e: e.dma_start(out=out, in_=in_, **kw)), sem, 16))

    def finish(self, trackers, eng="sp"):
        waits = self._waits(eng, trackers, ())
        self.ops[eng].append((waits, None, None, 0))

    def emit(self):
        nc = self.nc
        allsems = list(self.esem.values()) + list(self.dsems.values())
        with nc.Block() as b0:
            def clr(e):
                for s in allsems:
                    e.sem_clear(s)
            b0.sync(clr)
        with nc.Block() as block:
            def mk(ename):
                def body(e):
                    for waits, fn, sem, inc in self.ops[ename]:
                        for (s, v) in waits:
                            e.wait_ge(s, v)
                        if fn is not None:
                            ins = fn(e)
                            ins.then_inc(sem, inc)
                return body
            block.tensor(mk("pe"))
            block.scalar(mk("act"))
            block.vector(mk("dve"))
            block.gpsimd(mk("pool"))
            block.sync(mk("sp"))


class Rot:
    def __init__(self, bufs):
        self.bufs = bufs
        self.trk = [Trk() for _ in bufs]
        self.i = -1

    def next(self):
        self.i = (self.i + 1) % len(self.bufs)
        return self.bufs[self.i], self.trk[self.i]


def cast_w_dram(P, src, dst, rows, cols, trk_dst, key="wc", rstep=512):
    for r0 in range(0, rows, rstep):
        r1 = min(rows, r0 + rstep)
        for c0 in range(0, cols, 2048):
            c1 = min(cols, c0 + 2048)
            P.dma("pool", dst[r0:r1, c0:c1], src[r0:r1, c0:c1], writes=(trk_dst,), key=key)


def rms_stats(P, xt, xt_trk, ncols, ones_f, ones_f_trk, sq_rot, ps_stat, ps_trk, rstd, rstd_trk, eps_sb, nchunks=KC, dim=D):
    for kc in range(nchunks):
        sq, sqt = sq_rot.next()
        P.op("act", (lambda e, sq=sq, kc=kc: e.activation(out=sq[:, :ncols], in_=xt[:, kc, :ncols], func=AF.Square)),
             reads=(xt_trk,), writes=(sqt,))
        P.op("pe", (lambda e, sq=sq, kc=kc: e.matmul(ps_stat[:, :ncols], ones_f[:, :], sq[:, :ncols],
                                                       start=(kc == 0), stop=(kc == nchunks - 1))),
             reads=(sqt, ones_f_trk), writes=(ps_trk,))
    P.op("act", (lambda e: e.activation(out=rstd[:, :ncols], in_=ps_stat[:, :ncols], func=AF.Sqrt,
                                        bias=eps_sb[:, 0:1], scale=1.0 / dim)),
         reads=(ps_trk, ones_f_trk), writes=(rstd_trk,))
    P.op("dve", (lambda e: e.reciprocal(out=rstd[:, :ncols], in_=rstd[:, :ncols])),
         reads=(rstd_trk,), writes=(rstd_trk,))


SEG_Q, SEG_K, SEG_V, SEG_Z, SEG_XBC, SEG_DT, SEG_QM, SEG_G = 0, 1024, 2048, 3072, 5120, 8192, 8224, 9248
N_IN = 15392


def build_p1(T, ncols_total=N_IN):
    nc = bass.Bass("TRN2", target_bir_lowering=False)
    xT = nc.dram_tensor("xT", [D, T], F32, kind="ExternalInput").ap()
    nw = nc.dram_tensor("nw", [128, KC], F32, kind="ExternalInput").ap()
    w_in = nc.dram_tensor("w_in", [D, N_IN], F32, kind="ExternalInput").ap()
    o_qk = nc.dram_tensor("o_qk", [2048, T], BF16, kind="ExternalOutput").ap()
    o_v = nc.dram_tensor("o_v", [T, 1024], BF16, kind="ExternalOutput").ap()
    o_z = nc.dram_tensor("o_z", [2048, T], BF16, kind="ExternalOutput").ap()
    o_xbc = nc.dram_tensor("o_xbc", [3072, T], BF16, kind="ExternalOutput").ap()
    o_dt = nc.dram_tensor("o_dt", [32, T], F32, kind="ExternalOutput").ap()
    o_qm = nc.dram_tensor("o_qm", [1024, T], BF16, kind="ExternalOutput").ap()
    o_g = nc.dram_tensor("o_g", [6144, T], BF16, kind="ExternalOutput").ap()
    w_bf = nc.dram_tensor("w_bf", [D, N_IN], BF16).ap()
    NTG = T // 512
    with contextlib.ExitStack() as st:
        P = Prog(nc, st)
        t_wbf = Trk("wbf")
        cast_w_dram(P, w_in, w_bf, D, N_IN, t_wbf)

        ones_f = P.sbuf([128, 128], F32)
        t_ones = Trk()
        P.op("pool", lambda e: e.memset(ones_f[:, :], 1.0), writes=(t_ones,))
        eps_sb = P.sbuf([128, 1], F32)
        P.op("pool", lambda e: e.memset(eps_sb[:, :], EPS), writes=(t_ones,))
        nw_sb = P.sbuf([128, KC], F32)
        t_nw = Trk()
        P.dma("sp", nw_sb[:, :], nw[:, :], writes=(t_nw,), key="c0")
        hT = P.sbuf([128, KC, T], BF16)
        t_hT = Trk()
        xt_rot = Rot([P.sbuf([128, KC, 512], F32) for _ in range(1)])
        sq_rot = Rot([P.sbuf([128, 512], F32) for _ in range(3)])
        rstd = P.sbuf([128, 512], F32)
        t_rstd = Trk()
        ps_stat = P.psum([128, 512], F32)
        t_pss = Trk()
        xTv = xT.rearrange("(kc p) t -> p kc t", p=128)
        for tg in range(NTG):
            xt, t_xt = xt_rot.next()
            P.dma("sp", xt[:, :, :], xTv[:, :, tg * 512:(tg + 1) * 512], writes=(t_xt,), key=f"x{xt_rot.i}")
            rms_stats(P, xt, t_xt, 512, ones_f, t_ones, sq_rot, ps_stat, t_pss, rstd, t_rstd, eps_sb)
            for kc in range(KC):
                P.op("dve", (lambda e, xt=xt, kc=kc, tg=tg: e.scalar_tensor_tensor(
                    out=hT[:, kc, tg * 512:(tg + 1) * 512], in0=xt[:, kc, :], scalar=nw_sb[:, kc:kc + 1],
                    in1=rstd[:, :], op0=ALU.mult, op1=ALU.mult)),
                    reads=(t_xt, t_rstd, t_nw), writes=(t_hT,))

        wv = w_bf.rearrange("(kc p) n -> p kc n", p=128)
        w_rot = Rot([P.sbuf([128, KC, 512], BF16) for _ in range(2)])
        ps_rot = Rot([P.psum([128, 512], F32) for _ in range(4)])
        stg_rot = Rot([P.sbuf([128, 2048], BF16) for _ in range(3)])
        stgf_rot = Rot([P.sbuf([128, 2048], F32) for _ in range(1)])
        t_out = Trk("out")
        evac_i = [0]

        def evac(dst, src, rd, wr, func=None):
            if func is not None:
                P.op("act", lambda e: e.activation(out=dst, in_=src, func=func), reads=rd, writes=wr)
                return
            evac_i[0] += 1
            if evac_i[0] % 2 == 0:
                P.op("act", lambda e: e.activation(out=dst, in_=src, func=AF.Copy), reads=rd, writes=wr)
            else:
                P.op("dve", lambda e: e.tensor_copy(out=dst, in_=src), reads=rd, writes=wr)

        def load_w(c0, ncol):
            wt, t_w = w_rot.next()
            P.dma("sp", wt[:, :, :ncol], wv[:, :, c0:c0 + ncol], reads=(t_wbf,), writes=(t_w,), key=f"w{w_rot.i}")
            return wt, t_w

        def fm_group(c0, ncol, dst, drow0, func=None, f32out=False):
            wt, t_w = load_w(c0, ncol)
            for s0 in range(0, ncol, 128):
                sn = min(128, ncol - s0)
                stg, t_stg = (stgf_rot if f32out else stg_rot).next()
                for tg in range(NTG):
                    ps, t_ps = ps_rot.next()

                    def mm(e, wt=wt, s0=s0, sn=sn, tg=tg, ps=ps):
                        for kc in range(KC):
                            ins = e.matmul(ps[:sn, :], wt[:, kc, s0:s0 + sn], hT[:, kc, tg * 512:(tg + 1) * 512],
                                           start=(kc == 0), stop=(kc == KC - 1))
                        return ins
                    P.op("pe", mm, reads=(t_w, t_hT), writes=(t_ps,))
                    evac(stg[:sn, tg * 512:(tg + 1) * 512], ps[:sn, :], (t_ps,), (t_stg,), func)
                P.dma("pool", dst[drow0 + s0:drow0 + s0 + sn, :], stg[:sn, :T], reads=(t_stg,), writes=(t_out,),
                      key=f"o{(stgf_rot if f32out else stg_rot).i}{int(f32out)}")

        def tm_group(c0, ncol, dst, dcol0):
            wt, t_w = load_w(c0, ncol)
            for tt in range(T // 128):
                ps, t_ps = ps_rot.next()

                def mm(e, wt=wt, tt=tt, ps=ps):
                    for kc in range(KC):
                        ins = e.matmul(ps[:, :ncol], hT[:, kc, tt * 128:(tt + 1) * 128], wt[:, kc, :ncol],
                                       start=(kc == 0), stop=(kc == KC - 1))
                    return ins
                P.op("pe", mm, reads=(t_w, t_hT), writes=(t_ps,))
                stg, t_stg = stg_rot.next()
                evac(stg[:, :ncol], ps[:, :ncol], (t_ps,), (t_stg,))
                P.dma("pool", dst[tt * 128:(tt + 1) * 128, dcol0:dcol0 + ncol], stg[:, :ncol], reads=(t_stg,),
                      writes=(t_out,), key=f"o{stg_rot.i}0")

        for c0 in range(0, 2048, 512):
            fm_group(SEG_Q + c0, 512, o_qk, c0)
        for c0 in range(0, 1024, 512):
            tm_group(SEG_V + c0, 512, o_v, c0)
        for c0 in range(0, 2048, 512):
            fm_group(SEG_Z + c0, 512, o_z, c0)
        for c0 in range(0, 3072, 512):
            fm_group(SEG_XBC + c0, 512, o_xbc, c0)
        fm_group(SEG_DT, 32, o_dt, 0, f32out=True)
        for c0 in range(0, 1024, 512):
            fm_group(SEG_QM + c0, 512, o_qm, c0)
        for c0 in range(0, 6144, 512):
            fm_group(SEG_G + c0, 512, o_g, c0, func=AF.Sigmoid)
        P.finish((t_out,))
        P.emit()
    return nc


def load_vec16(P, dram_ap, key="c0"):
    n = dram_ap.shape[1]
    t = P.sbuf([128, n], F32)
    trk = Trk()
    P.dma("sp", t[:, :], dram_ap[:, :], writes=(trk,), key=key)
    return t, trk


def mm_group(P, ps, ps_trk, pairs, reads, M=128, N=None):
    def mm(e):
        n = len(pairs)
        for i, (l, r) in enumerate(pairs):
            ins = e.matmul(ps, l, r, start=(i == 0), stop=(i == n - 1))
        return ins
    P.op("pe", mm, reads=reads, writes=(ps_trk,))


def build_p3a(T, TG=256):
    nc = bass.Bass("TRN2", target_bir_lowering=False)
    di = lambda n, s, dt=F32: nc.dram_tensor(n, s, dt, kind="ExternalInput").ap()
    xT = di("xT", [D, T])
    yaT = di("yaT", [1024, T], BF16)
    ysT = di("ysT", [2048, T], BF16)
    qmT = di("qmT", [1024, T], BF16)
    gT = di("gT", [6144, T], BF16)
    memT = di("memT", [D, 256])
    nw_mem = di("nw_mem", [128, KC])
    nw_post = di("nw_post", [128, KC])
    nw_pre = di("nw_pre", [128, KC])
    w_kv = di("w_kv", [D, 2048])
    w_ba = di("w_ba", [1024, D])
    w_bs = di("w_bs", [2048, D])
    w_bm = di("w_bm", [1024, D])
    w_o = di("w_o", [D, D])
    o_xm = nc.dram_tensor("o_xm", [D, T], F32, kind="ExternalOutput").ap()
    o_h2 = nc.dram_tensor("o_h2", [D, T], BF16, kind="ExternalOutput").ap()
    b_kv = nc.dram_tensor("b_kv", [D, 2048], BF16).ap()
    b_ba = nc.dram_tensor("b_ba", [1024, D], BF16).ap()
    b_bs = nc.dram_tensor("b_bs", [2048, D], BF16).ap()
    b_bm = nc.dram_tensor("b_bm", [1024, D], BF16).ap()
    b_o = nc.dram_tensor("b_o", [D, D], BF16).ap()
    NTG = T // TG
    with contextlib.ExitStack() as st:
        P = Prog(nc, st)
        t_wb = Trk("wb")
        cast_w_dram(P, w_kv, b_kv, D, 2048, t_wb)
        cast_w_dram(P, w_ba, b_ba, 1024, D, t_wb)
        cast_w_dram(P, w_bs, b_bs, 2048, D, t_wb)
        cast_w_dram(P, w_bm, b_bm, 1024, D, t_wb)
        cast_w_dram(P, w_o, b_o, D, D, t_wb)
        t_c = Trk("consts")
        ones_f = P.sbuf([128, 128], F32)
        P.op("pool", lambda e: e.memset(ones_f[:, :], 1.0), writes=(t_c,))
        ones_b = P.sbuf([128, 128], BF16)
        P.op("pool", lambda e: e.memset(ones_b[:, :], 1.0), writes=(t_c,))
        eps_sb = P.sbuf([128, 1], F32)
        P.op("pool", lambda e: e.memset(eps_sb[:, :], EPS), writes=(t_c,))
        nwm_sb, t_nwm = load_vec16(P, nw_mem)
        nwpo_sb, t_nwpo = load_vec16(P, nw_post)
        nwpr_sb, t_nwpr = load_vec16(P, nw_pre)
        sq_rot = Rot([P.sbuf([128, 256], F32) for _ in range(3)])
        rstd = P.sbuf([128, 256], F32)
        t_rstd = Trk()
        ps_stat = P.psum([128, 256], F32)
        t_pss = Trk()
        ps_rot = Rot([P.psum([128, 512], F32) for _ in range(5)])

        big_rot = Rot([P.sbuf([128, KC, 256], F32) for _ in range(2)])
        mt_sb, t_mt = big_rot.next()
        P.dma("sp", mt_sb[:, :, :], memT.rearrange("(kc p) m -> p kc m", p=128), writes=(t_mt,), key="big0")
        rms_stats(P, mt_sb, t_mt, 256, ones_f, t_c, sq_rot, ps_stat, t_pss, rstd, t_rstd, eps_sb)
        mnT = P.sbuf([128, KC, 256], BF16)
        t_mn = Trk()
        for kc in range(KC):
            P.op("dve", (lambda e, kc=kc: e.scalar_tensor_tensor(
                out=mnT[:, kc, :], in0=mt_sb[:, kc, :], scalar=nwm_sb[:, kc:kc + 1], in1=rstd[:, :],
                op0=ALU.mult, op1=ALU.mult)), reads=(t_mt, t_rstd, t_nwm), writes=(t_mn,))
        KmT = P.sbuf([128, 8, 256], BF16)
        Vm = P.sbuf([128, 2, 1024], BF16)
        t_kv = Trk()
        wkv_rot = Rot([P.sbuf([128, KC, 512], BF16) for _ in range(2)])
        kvv = b_kv.rearrange("(kc p) n -> p kc n", p=128)
        for cg in range(4):
            wt, t_w = wkv_rot.next()
            P.dma("sp", wt[:, :, :], kvv[:, :, cg * 512:(cg + 1) * 512], reads=(t_wb,), writes=(t_w,),
                  key=f"wkv{wkv_rot.i}")
            if cg < 2:
                for s in range(4):
                    ps, t_ps = ps_rot.next()
                    mm_group(P, ps[:, :256], t_ps, [(wt[:, kc, s * 128:(s + 1) * 128], mnT[:, kc, :]) for kc in range(KC)],
                             reads=(t_w, t_mn))
                    P.op("act", (lambda e, ps=ps, c=cg * 4 + s: e.activation(out=KmT[:, c, :], in_=ps[:, :256], func=AF.Copy)),
                         reads=(t_ps,), writes=(t_kv,))
            else:
                for mt in range(2):
                    ps, t_ps = ps_rot.next()
                    mm_group(P, ps[:, :], t_ps, [(mnT[:, kc, mt * 128:(mt + 1) * 128], wt[:, kc, :]) for kc in range(KC)],
                             reads=(t_w, t_mn))
                    P.op("act", (lambda e, ps=ps, mt=mt, c0=(cg - 2) * 512: e.activation(
                        out=Vm[:, mt, c0:c0 + 512], in_=ps[:, :], func=AF.Copy)), reads=(t_ps,), writes=(t_kv,))

        qm_sb = P.sbuf([128, 8, TG], BF16); t_qm = Trk()
        ya_sb = P.sbuf([128, 8, TG], BF16); t_ya = Trk()
        ys_sb = P.sbuf([128, 16, TG], BF16); t_ys = Trk()
        ym_sb = P.sbuf([128, 8, TG], BF16); t_ym = Trk()
        mg_sb = P.sbuf([128, KC, TG], BF16); t_mg = Trk()
        pT_rot = Rot([P.sbuf([128, TG], BF16) for _ in range(4)])
        rs_sb = P.sbuf([128, TG], F32); t_rs = Trk()
        g_rot = Rot([P.sbuf([128, 3, 2, TG], BF16) for _ in range(2)])
        wb_rot = Rot([P.sbuf([128, 32, 256], BF16) for _ in range(2)])
        wo_rot = Rot([P.sbuf([128, KC, 256], BF16) for _ in range(2)])
        macc_rot = Rot([P.sbuf([128, TG], F32) for _ in range(2)])
        mtmp_rot = Rot([P.sbuf([128, TG], F32) for _ in range(2)])
        tmp_rot = Rot([P.sbuf([128, TG], F32) for _ in range(2)])
        h2_rot = Rot([P.sbuf([128, KC, TG], BF16) for _ in range(1)])
        t_out = Trk("out")
        xv = xT.rearrange("(kc p) t -> p kc t", p=128)
        oxv = o_xm.rearrange("(kc p) t -> p kc t", p=128)
        ohv = o_h2.rearrange("(kc p) t -> p kc t", p=128)
        bav = b_ba.rearrange("(kc p) n -> p kc n", p=128)
        bsv = b_bs.rearrange("(kc p) n -> p kc n", p=128)
        bmv = b_bm.rearrange("(kc p) n -> p kc n", p=128)
        bov = b_o.rearrange("(kc p) n -> p kc n", p=128)
        gv = gT.rearrange("(g c p) t -> p g c t", p=128, g=3)
        for tg in range(NTG):
            ts = slice(tg * TG, (tg + 1) * TG)
            P.dma("sp", qm_sb[:, :, :], qmT.rearrange("(c p) t -> p c t", p=128)[:, :, ts], writes=(t_qm,), key="qm")
            P.dma("sp", ya_sb[:, :, :], yaT.rearrange("(c p) t -> p c t", p=128)[:, :, ts], writes=(t_ya,), key="ya")
            P.dma("sp", ys_sb[:, :, :], ysT.rearrange("(c p) t -> p c t", p=128)[:, :, ts], writes=(t_ys,), key="ys")
            for hh in range(4):
                pts = []
                for mt in range(2):
                    ps, t_ps = ps_rot.next()
                    mm_group(P, ps[:, :TG], t_ps,
                             [(KmT[:, hh * 2 + dc, mt * 128:(mt + 1) * 128], qm_sb[:, hh * 2 + dc, :]) for dc in range(2)],
                             reads=(t_kv, t_qm))
                    pT, t_pT = pT_rot.next()
                    P.op("act", (lambda e, ps=ps, pT=pT: e.activation(out=pT[:, :], in_=ps[:, :TG], func=AF.Exp, scale=1.0 / 16.0)),
                         reads=(t_ps,), writes=(t_pT,))
                    pts.append((pT, t_pT))
                ps, t_ps = ps_rot.next()
                mm_group(P, ps[:, :TG], t_ps, [(ones_b[:, :], pT[:, :]) for (pT, _) in pts],
                         reads=(t_c,) + tuple(t for _, t in pts))
                P.op("dve", (lambda e, ps=ps: e.reciprocal(out=rs_sb[:, :], in_=ps[:, :TG])), reads=(t_ps,), writes=(t_rs,))
                for dc in range(2):
                    ps, t_ps = ps_rot.next()
                    mm_group(P, ps[:, :TG], t_ps,
                             [(Vm[:, mt, hh * 256 + dc * 128: hh * 256 + (dc + 1) * 128], pts[mt][0][:, :]) for mt in range(2)],
                             reads=(t_kv,) + tuple(t for _, t in pts))
                    P.op("dve", (lambda e, ps=ps, c=hh * 2 + dc: e.tensor_tensor(out=ym_sb[:, c, :], in0=ps[:, :TG], in1=rs_sb[:, :],
                                                                                op=ALU.mult)),
                         reads=(t_ps, t_rs), writes=(t_ym,))
            for cg in range(8):
                wb, t_w = wb_rot.next()
                cs = slice(cg * 256, (cg + 1) * 256)
                k = f"wb{wb_rot.i}"
                P.dma("sp", wb[:, 0:8, :], bav[:, :, cs], reads=(t_wb,), writes=(t_w,), key=k)
                P.dma("sp", wb[:, 8:24, :], bsv[:, :, cs], reads=(t_wb,), writes=(t_w,), key=k)
                P.dma("sp", wb[:, 24:32, :], bmv[:, :, cs], reads=(t_wb,), writes=(t_w,), key=k)
                gt, t_g = g_rot.next()
                for g3 in range(3):
                    P.dma("sp", gt[:, g3, :, :], gv[:, g3, cg * 2:cg * 2 + 2, ts], writes=(t_g,), key=f"g{g_rot.i}")
                for ct in range(2):
                    c = cg * 2 + ct
                    macc, t_ma = macc_rot.next()
                    specs = ((0, 8, ya_sb, t_ya, 0), (8, 16, ys_sb, t_ys, 1), (24, 8, ym_sb, t_ym, 2))
                    for bi, (k0, nk, src, t_src, g3) in enumerate(specs):
                        ps, t_ps = ps_rot.next()
                        mm_group(P, ps[:, :TG], t_ps,
                                 [(wb[:, k0 + kc, ct * 128:(ct + 1) * 128], src[:, kc, :]) for kc in range(nk)],
                                 reads=(t_w, t_src))
                        if bi == 0:
                            P.op("dve", (lambda e, ps=ps, macc=macc, gt=gt, g3=g3, ct=ct: e.tensor_tensor(
                                out=macc[:, :], in0=ps[:, :TG], in1=gt[:, g3, ct, :], op=ALU.mult)),
                                reads=(t_ps, t_g), writes=(t_ma,))
                        else:
                            mtmp, t_mtmp = mtmp_rot.next()
                            P.op("dve", (lambda e, ps=ps, mtmp=mtmp, gt=gt, g3=g3, ct=ct: e.tensor_tensor(
                                out=mtmp[:, :], in0=ps[:, :TG], in1=gt[:, g3, ct, :], op=ALU.mult)),
                                reads=(t_ps, t_g), writes=(t_mtmp,))
                            last = (bi == 2)
                            dst = mg_sb[:, c, :] if last else macc[:, :]
                            P.op("pool", (lambda e, mtmp=mtmp, macc=macc, dst=dst: e.tensor_tensor(
                                out=dst, in0=mtmp[:, :], in1=macc[:, :], op=ALU.add)),
                                reads=(t_mtmp, t_ma), writes=((t_mg,) if last else (t_ma,)))
            yt, t_yt = big_rot.next()
            for cg in range(8):
                wo, t_w = wo_rot.next()
                P.dma("sp", wo[:, :, :], bov[:, :, cg * 256:(cg + 1) * 256], reads=(t_wb,), writes=(t_w,), key=f"wo{wo_rot.i}")
                for ct in range(2):
                    c = cg * 2 + ct
                    ps, t_ps = ps_rot.next()
                    mm_group(P, ps[:, :TG], t_ps, [(wo[:, kc, ct * 128:(ct + 1) * 128], mg_sb[:, kc, :]) for kc in range(KC)],
                             reads=(t_w, t_mg))
                    P.op("act", (lambda e, ps=ps, c=c, yt=yt: e.activation(out=yt[:, c, :], in_=ps[:, :TG], func=AF.Copy)),
                         reads=(t_ps,), writes=(t_yt,))
            xt, t_xt = big_rot.next()
            P.dma("sp", xt[:, :, :], xv[:, :, ts], writes=(t_xt,), key=f"big{big_rot.i}")
            rms_stats(P, yt, t_yt, TG, ones_f, t_c, sq_rot, ps_stat, t_pss, rstd, t_rstd, eps_sb)
            for kc in range(KC):
                tmp, t_tmp = tmp_rot.next()
                P.op("dve", (lambda e, kc=kc, tmp=tmp, yt=yt: e.scalar_tensor_tensor(
                    out=tmp[:, :], in0=yt[:, kc, :], scalar=nwpo_sb[:, kc:kc + 1], in1=rstd[:, :], op0=ALU.mult, op1=ALU.mult)),
                    reads=(t_yt, t_rstd, t_nwpo), writes=(t_tmp,))
                P.op("pool", (lambda e, kc=kc, tmp=tmp, xt=xt: e.tensor_tensor(out=xt[:, kc, :], in0=tmp[:, :], in1=xt[:, kc, :],
                                                                               op=ALU.add)),
                     reads=(t_tmp, t_xt), writes=(t_xt,))
            P.dma("pool", oxv[:, :, ts], xt[:, :, :], reads=(t_xt,), writes=(t_out,), key="oxm")
            rms_stats(P, xt, t_xt, TG, ones_f, t_c, sq_rot, ps_stat, t_pss, rstd, t_rstd, eps_sb)
            h2, t_h2 = h2_rot.next()
            for kc in range(KC):
                P.op("dve", (lambda e, kc=kc, xt=xt, h2=h2: e.scalar_tensor_tensor(
                    out=h2[:, kc, :], in0=xt[:, kc, :], scalar=nwpr_sb[:, kc:kc + 1], in1=rstd[:, :], op0=ALU.mult, op1=ALU.mult)),
                    reads=(t_xt, t_rstd, t_nwpr), writes=(t_h2,))
            P.dma("pool", ohv[:, :, ts], h2[:, :, :], reads=(t_h2,), writes=(t_out,), key="oh2")
        P.finish((t_out,))
        P.emit()
    return nc


DFF = 5632
NF = DFF // 128
GELU_C = 0.7978845608028654


def build_p3b(T, TG=256, gelu_native=True):
    nc = bass.Bass("TRN2", target_bir_lowering=False)
    di = lambda n, s, dt=F32: nc.dram_tensor(n, s, dt, kind="ExternalInput").ap()
    h2T = di("h2T", [D, T + 2], BF16)
    xmT = di("xmT", [D, T])
    w_up = di("w_up", [D, 2 * DFF])
    w_dn = di("w_dn", [DFF, D])
    cw = di("cw", [128, 2 * NF * 3])
    cb = di("cb", [128, 2 * NF])
    nw = di("nw", [128, KC])
    o_x = nc.dram_tensor("o_x", [D, T], F32, kind="ExternalOutput").ap()
    b_up = nc.dram_tensor("b_up", [D, 2 * DFF], BF16).ap()
    b_dn = nc.dram_tensor("b_dn", [DFF, D], BF16).ap()
    NTG = T // TG
    NE = TG + 2
    with contextlib.ExitStack() as st:
        P = Prog(nc, st)
        t_wb = Trk("wb")
        cast_w_dram(P, w_up, b_up, D, 2 * DFF, t_wb)
        cast_w_dram(P, w_dn, b_dn, DFF, D, t_wb)
        t_c = Trk("consts")
        ones_f = P.sbuf([128, 128], F32)
        P.op("pool", lambda e: e.memset(ones_f[:, :], 1.0), writes=(t_c,))
        eps_sb = P.sbuf([128, 1], F32)
        P.op("pool", lambda e: e.memset(eps_sb[:, :], EPS), writes=(t_c,))
        cw_sb, t_cw = load_vec16(P, cw)
        cb_sb, t_cb = load_vec16(P, cb)
        nw_sb, t_nw = load_vec16(P, nw)
        sq_rot = Rot([P.sbuf([128, TG], F32) for _ in range(3)])
        rstd = P.sbuf([128, TG], F32); t_rstd = Trk()
        ps_stat = P.psum([128, TG], F32); t_pss = Trk()
        ps_rot = Rot([P.psum([128, 512], F32) for _ in range(6)])
        big_rot = Rot([P.sbuf([128, KC, TG], F32) for _ in range(2)])
        h2_rot = Rot([P.sbuf([128, KC, NE], BF16) for _ in range(2)])
        act_sb = P.sbuf([128, NF, TG], BF16); t_act = Trk()
        wu_rot = Rot([P.sbuf([128, 2, KC, 256], BF16) for _ in range(3)])
        wd_rot = Rot([P.sbuf([128, NF, 256], BF16) for _ in range(2)])
        u_rot = Rot([P.sbuf([128, NE], F32) for _ in range(4)])
        t_rot = Rot([P.sbuf([128, TG], F32) for _ in range(4)])
        ga_rot = Rot([P.sbuf([128, TG], F32) for _ in range(2)])
        tmp_rot = Rot([P.sbuf([128, TG], F32) for _ in range(2)])
        t_out = Trk("out")
        hv = h2T.rearrange("(kc p) t -> p kc t", p=128)
        xv = xmT.rearrange("(kc p) t -> p kc t", p=128)
        ov = o_x.rearrange("(kc p) t -> p kc t", p=128)
        buv = b_up.rearrange("(kc p) n -> p kc n", p=128)
        bdv = b_dn.rearrange("(f p) n -> p f n", p=128)

        def conv_chain(ps, t_ps, ch):
            u, t_u = u_rot.next()
            P.op("act", (lambda e: e.activation(out=u[:, :], in_=ps[:, :NE], func=AF.Copy)), reads=(t_ps,), writes=(t_u,))
            t, t_t = t_rot.next()
            P.op("pool", (lambda e: e.tensor_scalar(out=t[:, :], in0=u[:, 0:TG], scalar1=cw_sb[:, ch * 3:ch * 3 + 1],
                                                    scalar2=cb_sb[:, ch:ch + 1], op0=ALU.mult, op1=ALU.add)),
                 reads=(t_u, t_cw, t_cb), writes=(t_t,))
            for k in (1, 2):
                P.op("dve", (lambda e, k=k: e.scalar_tensor_tensor(out=t[:, :], in0=u[:, k:k + TG],
                                                                   scalar=cw_sb[:, ch * 3 + k:ch * 3 + k + 1], in1=t[:, :],
                                                                   op0=ALU.mult, op1=ALU.add)),
                     reads=(t_u, t_cw, t_t), writes=(t_t,))
            return t, t_t

        for tg in range(NTG):
            ts = slice(tg * TG, (tg + 1) * TG)
            h2, t_h2 = h2_rot.next()
            P.dma("sp", h2[:, :, :], hv[:, :, tg * TG:tg * TG + NE], writes=(t_h2,), key=f"h2{h2_rot.i}")
            for fg in range(NF // 2):
                wu, t_w = wu_rot.next()
                k = f"wu{wu_rot.i}"
                P.dma("sp", wu[:, 0, :, :], buv[:, :, fg * 256:(fg + 1) * 256], reads=(t_wb,), writes=(t_w,), key=k)
                P.dma("sp", wu[:, 1, :, :], buv[:, :, DFF + fg * 256:DFF + (fg + 1) * 256], reads=(t_wb,), writes=(t_w,), key=k)
                for ft in range(2):
                    f = fg * 2 + ft
                    res = []
                    for half in range(2):
                        ps, t_ps = ps_rot.next()
                        mm_group(P, ps[:, :NE], t_ps, [(wu[:, half, kc, ft * 128:(ft + 1) * 128], h2[:, kc, :]) for kc in range(KC)],
                                 reads=(t_w, t_h2))
                        res.append(conv_chain(ps, t_ps, half * NF + f))
                    (ta, t_ta), (tgg, t_tg) = res
                    ga, t_ga = ga_rot.next()
                    if gelu_native:
                        P.op("act", (lambda e, ta=ta, ga=ga: e.activation(out=ga[:, :], in_=ta[:, :], func=AF.Gelu_apprx_tanh)),
                             reads=(t_ta,), writes=(t_ga,))
                    else:
                        P.op("act", (lambda e, ta=ta, ga=ga: e.activation(out=ga[:, :], in_=ta[:, :], func=AF.Square)),
                             reads=(t_ta,), writes=(t_ga,))
                        P.op("pool", (lambda e, ga=ga: e.tensor_scalar(out=ga[:, :], in0=ga[:, :], scalar1=0.044715, scalar2=1.0,
                                                                      op0=ALU.mult, op1=ALU.add)), reads=(t_ga,), writes=(t_ga,))
                        P.op("pool", (lambda e, ta=ta, ga=ga: e.tensor_tensor(out=ga[:, :], in0=ga[:, :], in1=ta[:, :], op=ALU.mult)),
                             reads=(t_ga, t_ta), writes=(t_ga,))
                        P.op("act", (lambda e, ga=ga: e.activation(out=ga[:, :], in_=ga[:, :], func=AF.Sigmoid, scale=2.0 * GELU_C)),
                             reads=(t_ga,), writes=(t_ga,))
                        P.op("pool", (lambda e, ta=ta, ga=ga: e.tensor_tensor(out=ga[:, :], in0=ga[:, :], in1=ta[:, :], op=ALU.mult)),
                             reads=(t_ga, t_ta), writes=(t_ga,))
                    P.op("dve", (lambda e, ga=ga, tgg=tgg, f=f: e.tensor_tensor(out=act_sb[:, f, :], in0=ga[:, :], in1=tgg[:, :],
                                                                                op=ALU.mult)),
                         reads=(t_ga, t_tg), writes=(t_act,))
            yt, t_yt = big_rot.next()
            for cg in range(8):
                wd, t_w = wd_rot.next()
                P.dma("sp", wd[:, :, :], bdv[:, :, cg * 256:(cg + 1) * 256], reads=(t_wb,), writes=(t_w,), key=f"wd{wd_rot.i}")
                for ct in range(2):
                    c = cg * 2 + ct
                    ps, t_ps = ps_rot.next()
                    mm_group(P, ps[:, :TG], t_ps, [(wd[:, f, ct * 128:(ct + 1) * 128], act_sb[:, f, :]) for f in range(NF)],
                             reads=(t_w, t_act))
                    P.op("act", (lambda e, ps=ps, c=c, yt=yt: e.activation(out=yt[:, c, :], in_=ps[:, :TG], func=AF.Copy)),
                         reads=(t_ps,), writes=(t_yt,))
            xt, t_xt = big_rot.next()
            P.dma("sp", xt[:, :, :], xv[:, :, ts], writes=(t_xt,), key=f"big{big_rot.i}")
            rms_stats(P, yt, t_yt, TG, ones_f, t_c, sq_rot, ps_stat, t_pss, rstd, t_rstd, eps_sb)
            for kc in range(KC):
                tmp, t_tmp = tmp_rot.next()
                P.op("dve", (lambda e, kc=kc, tmp=tmp, yt=yt: e.scalar_tensor_tensor(
                    out=tmp[:, :], in0=yt[:, kc, :], scalar=nw_sb[:, kc:kc + 1], in1=rstd[:, :], op0=ALU.mult, op1=ALU.mult)),
                    reads=(t_yt, t_rstd, t_nw), writes=(t_tmp,))
                P.op("pool", (lambda e, kc=kc, tmp=tmp, xt=xt: e.tensor_tensor(out=xt[:, kc, :], in0=tmp[:, :], in1=xt[:, kc, :],
                                                                               op=ALU.add)),
                     reads=(t_tmp, t_xt), writes=(t_xt,))
            P.dma("pool", ov[:, :, ts], xt[:, :, :], reads=(t_xt,), writes=(t_out,), key="ox")
        P.finish((t_out,))
        P.emit()
    return nc


BIGR = 3000.0
NEGF = -1.0e30


def attn_tables(S):
    nb = S // 256
    j = np.arange(nb)[None, :]
    n = np.arange(nb)[:, None]
    past = (j < n)
    pastb = np.where(past, 0.0, NEGF).astype(np.float32).reshape(1, nb * nb)
    past01 = past.astype(np.float32).reshape(1, nb * nb)
    own01 = (j == n).astype(np.float32).reshape(1, nb * nb)
    rep = lambda a: np.ascontiguousarray(np.broadcast_to(a, (128, a.shape[1])))
    k = np.arange(128)[:, None]
    q = np.arange(256)[None, :]
    cmA = np.where(k <= q, 0.0, -BIGR)
    cmB = np.where(128 + k <= q, 0.0, -BIGR)
    cm = np.concatenate([cmA, cmB], axis=1).astype(NPBF)
    oh = np.zeros((32, nb, 128), np.float32)
    for jj in range(nb):
        oh[jj, jj, :] = 1.0
    half = 64
    inv = (10000.0 ** (-np.arange(half, dtype=np.float32) / half)).astype(np.float32)
    ang = np.arange(S, dtype=np.float32)[None, :] * inv[:, None]
    cos = np.cos(ang).astype(np.float32)
    sin = np.sin(ang).astype(np.float32)
    cosT = np.concatenate([cos, cos], axis=0)
    sinT = np.concatenate([-sin, sin], axis=0)
    return dict(pastb=rep(pastb), past01=rep(past01), own01=rep(own01), cm=np.ascontiguousarray(cm),
                onehot=np.ascontiguousarray(oh.reshape(32, nb * 128).astype(NPBF)),
                ident_f=np.eye(128, dtype=np.float32), ident_b=np.eye(128, dtype=np.float32).astype(NPBF),
                cosT=np.ascontiguousarray(cosT), sinT=np.ascontiguousarray(sinT))


def build_p2a(S):
    nb = S // 256
    NT = S // 128
    RC = min(S, 2048)
    nc = bass.Bass("TRN2", target_bir_lowering=False)
    di = lambda n, s, dt=F32: nc.dram_tensor(n, s, dt, kind="ExternalInput").ap()
    qk = di("qk", [4, 128, S], BF16)
    qks = di("qks", [4, 128, S], BF16)
    v = di("v", [2, S, 128], BF16)
    cosT = di("cosT", [128, S])
    sinT = di("sinT", [128, S])
    pastb = di("pastb", [128, nb * nb])
    past01 = di("past01", [128, nb * nb])
    own01 = di("own01", [128, nb * nb])
    cm = di("cm", [128, 512], BF16)
    onehot = di("onehot", [32, nb * 128], BF16)
    ident_f = di("ident_f", [128, 128])
    ident_b = di("ident_b", [128, 128], BF16)
    o_ya = nc.dram_tensor("o_ya", [256, S], BF16, kind="ExternalOutput").ap()
    scale = 128.0 ** -0.5
    with contextlib.ExitStack() as st:
        P = Prog(nc, st)
        t_c = Trk("consts")

        def cload(ap, shape, dt):
            t = P.sbuf(shape, dt)
            P.dma("sp", t[:, :], ap[:, :], writes=(t_c,), key="c0")
            return t
        pastb_sb = cload(pastb, [128, nb * nb], F32)
        past01_sb = cload(past01, [128, nb * nb], F32)
        own01_sb = cload(own01, [128, nb * nb], F32)
        cm_sb = cload(cm, [128, 512], BF16)
        oh_sb = cload(onehot, [32, nb * 128], BF16)
        idf_sb = cload(ident_f, [128, 128], F32)
        idb_sb = cload(ident_b, [128, 128], BF16)
        ones_b = P.sbuf([128, 128], BF16)
        P.op("pool", lambda e: e.memset(ones_b[:, :], 1.0), writes=(t_c,))

        QR = P.sbuf([128, S], BF16); t_QR = Trk()
        KR = P.sbuf([128, S], BF16); t_KR = Trk()
        V = P.sbuf([128, NT, 128], BF16); t_V = Trk()
        maskbT = P.sbuf([32, S], BF16); t_mb = Trk()
        outb = P.sbuf([128, S], BF16); t_ob = Trk()
        raw_rot = Rot([P.sbuf([128, 2, RC], BF16) for _ in range(2)])
        cs_sb = P.sbuf([128, 2, RC], F32); t_cs = Trk()
        r1_rot = Rot([P.sbuf([128, RC], F32) for _ in range(1)])
        r2_rot = Rot([P.sbuf([128, RC], F32) for _ in range(1)])
        kmf = P.sbuf([128, nb], F32); t_kmf = Trk()
        kmT = P.sbuf([128, nb], BF16); t_km = Trk()
        gm_rot = Rot([P.sbuf([128, nb], F32) for _ in range(2)])
        m8_rot = Rot([P.sbuf([128, 8], F32) for _ in range(2)])
        sel_rot = Rot([P.sbuf([128, 32], F32) for _ in range(2)])
        pT_rot = Rot([P.sbuf([128, 256], BF16) for _ in range(4)])
        rs_rot = Rot([P.sbuf([128, 256], F32) for _ in range(2)])
        ps_s_rot = Rot([P.psum([128, 512], F32) for _ in range(3)])
        ps_o_rot = Rot([P.psum([128, 512], F32) for _ in range(2)])
        ps_m_rot = Rot([P.psum([128, 512], F32) for _ in range(2)])
        ps_g = P.psum([128, 512], F32); t_psg = Trk()
        t_out = Trk("out")
        if nb < 32:
            for b in sel_rot.bufs:
                P.op("pool", (lambda e, b=b: e.memset(b[:, :], 0.0)), writes=(t_c,))

        for h in range(2):
            for which, dst, t_dst in ((0, QR, t_QR), (1, KR, t_KR)):
                src = which * 2 + h
                for c0 in range(0, S, RC):
                    raw, t_raw = raw_rot.next()
                    kk = f"raw{raw_rot.i}"
                    P.dma("sp", raw[:, 0, :], qk[src, :, c0:c0 + RC], writes=(t_raw,), key=kk)
                    P.dma("sp", raw[:, 1, :], qks[src, :, c0:c0 + RC], writes=(t_raw,), key=kk)
                    P.dma("sp", cs_sb[:, 0, :], cosT[:, c0:c0 + RC], writes=(t_cs,), key="cs")
                    P.dma("sp", cs_sb[:, 1, :], sinT[:, c0:c0 + RC], writes=(t_cs,), key="cs")
                    r1, t_r1 = r1_rot.next()
                    r2, t_r2 = r2_rot.next()
                    P.op("dve", (lambda e, raw=raw, r1=r1: e.tensor_tensor(out=r1[:, :], in0=raw[:, 0, :], in1=cs_sb[:, 0, :], op=ALU.mult)),
                         reads=(t_raw, t_cs), writes=(t_r1,))
                    P.op("pool", (lambda e, raw=raw, r2=r2: e.tensor_tensor(out=r2[:, :], in0=raw[:, 1, :], in1=cs_sb[:, 1, :], op=ALU.mult)),
                         reads=(t_raw, t_cs), writes=(t_r2,))
                    P.op("dve", (lambda e, r1=r1, r2=r2, dst=dst, c0=c0: e.tensor_tensor(out=dst[:, c0:c0 + RC], in0=r1[:, :], in1=r2[:, :],
                                                                                        op=ALU.add)),
                         reads=(t_r1, t_r2), writes=(t_dst,))
            P.dma("sp", V[:, :, :], v[h].rearrange("(n p) d -> p n d", p=128), writes=(t_V,), key="v")
            P.op("dve", (lambda e: e.tensor_reduce(out=kmf[:, :], in_=KR[:, :].rearrange("p (n k) -> p n k", k=256), axis=AX.X, op=ALU.add)),
                 reads=(t_KR,), writes=(t_kmf,))
            P.op("act", (lambda e: e.activation(out=kmT[:, :], in_=kmf[:, :], func=AF.Copy, scale=1.0 / 256.0)),
                 reads=(t_kmf,), writes=(t_km,))
            for qt in range(NT):
                n = qt // 2
                P.op("pe", (lambda e, qt=qt: e.matmul(ps_g[:, :nb], QR[:, qt * 128:(qt + 1) * 128], kmT[:, :], start=True, stop=True)),
                     reads=(t_QR, t_km), writes=(t_psg,))
                gm, t_gm = gm_rot.next()
                P.op("dve", (lambda e, gm=gm, n=n: e.tensor_tensor(out=gm[:, :], in0=ps_g[:, :nb], in1=pastb_sb[:, n * nb:(n + 1) * nb],
                                                                   op=ALU.add)), reads=(t_psg, t_c), writes=(t_gm,))
                m8, t_m8 = m8_rot.next()
                if nb >= 8:
                    P.op("dve", (lambda e, gm=gm, m8=m8: e.max(out=m8[:, :], in_=gm[:, :])), reads=(t_gm,), writes=(t_m8,))
                else:
                    raise NotImplementedError
                sel, t_sel = sel_rot.next()
                P.op("dve", (lambda e, gm=gm, m8=m8, sel=sel: e.tensor_scalar(out=sel[:, :nb], in0=gm[:, :], scalar1=m8[:, 2:3], scalar2=None,
                                                                              op0=ALU.is_ge)), reads=(t_gm, t_m8), writes=(t_sel,))
                P.op("dve", (lambda e, sel=sel, n=n: e.tensor_tensor(out=sel[:, :nb], in0=sel[:, :nb], in1=past01_sb[:, n * nb:(n + 1) * nb],
                                                                     op=ALU.mult)), reads=(t_sel, t_c), writes=(t_sel,))
                P.op("dve", (lambda e, sel=sel, n=n: e.tensor_tensor(out=sel[:, :nb], in0=sel[:, :nb], in1=own01_sb[:, n * nb:(n + 1) * nb],
                                                                     op=ALU.add)), reads=(t_sel, t_c), writes=(t_sel,))
                P.op("dve", (lambda e, sel=sel: e.tensor_scalar(out=sel[:, :nb], in0=sel[:, :nb], scalar1=BIGR, scalar2=-BIGR,
                                                                op0=ALU.mult, op1=ALU.add)), reads=(t_sel,), writes=(t_sel,))
                P.op("pe", (lambda e, sel=sel: e.transpose(ps_g[:32, 256:384], sel[:, :], idf_sb[:, :])),
                     reads=(t_sel, t_c, t_gm), writes=(t_psg,))
                P.op("act", (lambda e, qt=qt: e.activation(out=maskbT[:, qt * 128:(qt + 1) * 128], in_=ps_g[:32, 256:384], func=AF.Copy)),
                     reads=(t_psg,), writes=(t_mb,))
            for n in range(nb):
                qs = slice(n * 256, (n + 1) * 256)
                ps_o, t_po = ps_o_rot.next()
                ps_m, t_pm = ps_m_rot.next()
                nkt = 2 * n + 2
                for kt in range(nkt):
                    j = kt // 2
                    ps_s, t_pss = ps_s_rot.next()
                    pairs = [(KR[:, kt * 128:(kt + 1) * 128], QR[:, qs]),
                             (oh_sb[:, j * 128:(j + 1) * 128], maskbT[:, qs])]
                    if j == n:
                        pairs.append((idb_sb[:, :], cm_sb[:, (kt % 2) * 256:(kt % 2 + 1) * 256]))
                    mm_group(P, ps_s[:, :256], t_pss, pairs, reads=(t_KR, t_QR, t_mb, t_c))
                    pT, t_pT = pT_rot.next()
                    P.op("act", (lambda e, ps_s=ps_s, pT=pT: e.activation(out=pT[:, :], in_=ps_s[:, :256], func=AF.Exp, scale=scale)),
                         reads=(t_pss,), writes=(t_pT,))

                    def mm2(e, kt=kt, pT=pT, ps_o=ps_o, ps_m=ps_m, nkt=nkt):
                        e.matmul(ps_o[:, :256], V[:, kt, :], pT[:, :], start=(kt == 0), stop=(kt == nkt - 1))
                        return e.matmul(ps_m[:, :256], ones_b[:, :], pT[:, :], start=(kt == 0), stop=(kt == nkt - 1))
                    P.op("pe", mm2, reads=(t_V, t_pT, t_c), writes=(t_po, t_pm))
                rs, t_rs = rs_rot.next()
                P.op("dve", (lambda e, rs=rs, ps_m=ps_m: e.reciprocal(out=rs[:, :], in_=ps_m[:, :256])), reads=(t_pm,), writes=(t_rs,))
                P.op("dve", (lambda e, rs=rs, ps_o=ps_o, qs=qs: e.tensor_tensor(out=outb[:, qs], in0=ps_o[:, :256], in1=rs[:, :], op=ALU.mult)),
                     reads=(t_po, t_rs), writes=(t_ob,))
            P.dma("pool", o_ya[h * 128:(h + 1) * 128, :], outb[:, :], reads=(t_ob,), writes=(t_out,), key="oya")
        P.finish((t_out,))
        P.emit()
    return nc


def ssd_tables():
    oh = np.zeros((8, 8, 128), np.float32)
    for h in range(8):
        oh[h, h, :] = 1.0
    s = np.arange(128)[:, None]
    l = np.arange(128)[None, :]
    tri = (l >= s).astype(np.float32)
    return dict(oh8=np.ascontiguousarray(oh.reshape(8, 1024)), tri=tri, ident_f=np.eye(128, dtype=np.float32),
                ident_b=np.eye(128, dtype=np.float32).astype(NPBF))


def build_p2b(S, SC=1024, dbg=99):
    nc = bass.Bass("TRN2", target_bir_lowering=False)
    di = lambda n, s, dt=F32: nc.dram_tensor(n, s, dt, kind="ExternalInput").ap()
    xbc = di("xbc", [768, S + 4], BF16)
    h_in = di("h_in", [128, 512])
    zT = di("zT", [512, S], BF16)
    dtT = di("dtT", [8, S])
    cw = di("cw", [128, 24])
    cb = di("cb", [128, 6])
    dtb = di("dtb", [128, 2])
    alog = di("alog", [128, 2])
    dsk = di("dsk", [128, 4])
    nw = di("nw", [128, 4])
    oh8 = di("oh8", [8, 1024])
    tri = di("tri", [128, 128])
    ident_f = di("ident_f", [128, 128])
    ident_b = di("ident_b", [128, 128], BF16)
    o_ys = nc.dram_tensor("o_ys", [512, S], BF16, kind="ExternalOutput").ap()
    o_h = nc.dram_tensor("o_h", [128, 512], F32, kind="ExternalOutput").ap()
    SC = min(SC, S)
    NSC = S // SC
    NCH = SC // 128
    with contextlib.ExitStack() as st:
        P = Prog(nc, st)
        t_c = Trk("consts")

        def cload(ap, shape, dt):
            t = P.sbuf(shape, dt)
            P.dma("sp", t[:, :], ap[:, :], writes=(t_c,), key="c0")
            return t
        cw_sb = cload(cw, [128, 24], F32)
        cb_sb = cload(cb, [128, 6], F32)
        dtb_sb = cload(dtb, [128, 2], F32)
        alog_sb = cload(alog, [128, 2], F32)
        dsk_sb = cload(dsk, [128, 4], F32)
        nw_sb = cload(nw, [128, 4], F32)
        oh8_sb = cload(oh8, [8, 1024], F32)
        tri_sb = cload(tri, [128, 128], F32)
        idf_sb = cload(ident_f, [128, 128], F32)
        idb_sb = cload(ident_b, [128, 128], BF16)
        ones_f = P.sbuf([128, 128], F32)
        P.op("pool", lambda e: e.memset(ones_f[:, :], 1.0), writes=(t_c,))
        eps_sb = P.sbuf([128, 1], F32)
        P.op("pool", lambda e: e.memset(eps_sb[:, :], EPS), writes=(t_c,))
        one_sb = P.sbuf([128, 1], F32)
        P.op("pool", lambda e: e.memset(one_sb[:, :], 1.0), writes=(t_c,))
        zero_sb = P.sbuf([128, 1], F32)
        P.op("pool", lambda e: e.memset(zero_sb[:, :], 0.0), writes=(t_c,))
        A_sb = P.sbuf([128, 2], F32)
        P.op("act", lambda e: e.activation(out=A_sb[:, :], in_=alog_sb[:, :], func=AF.Exp), reads=(t_c,), writes=(t_c,))
        P.op("dve", lambda e: e.tensor_scalar(out=A_sb[:, :], in0=A_sb[:, :], scalar1=-1.0, scalar2=None, op0=ALU.mult),
             reads=(t_c,), writes=(t_c,))

        raw = P.sbuf([128, 6, SC + 4], BF16); t_raw = Trk()
        zr = P.sbuf([128, 4, SC], BF16); t_zr = Trk()
        dtr = P.sbuf([8, SC], F32); t_dtr = Trk()
        xc = P.sbuf([128, 6, SC], BF16); t_xc = Trk()
        sz = P.sbuf([128, 4, SC], BF16); t_sz = Trk()
        dts = P.sbuf([8, SC], F32); t_dts = Trk()
        aT = P.sbuf([8, SC], F32); t_aT = Trk()
        ct_rot = Rot([P.sbuf([128, SC], F32) for _ in range(2)])
        acs_rot = Rot([P.sbuf([8, 128], F32) for _ in range(2)])
        datm_rot = Rot([P.sbuf([128, 16], F32) for _ in range(2)])
        E_rot = Rot([P.sbuf([128, 8, 128], F32) for _ in range(2)])
        Dp_rot = Rot([P.sbuf([128, 8, 128], F32) for _ in range(2)])
        cbm_rot = Rot([P.sbuf([128, 128], F32) for _ in range(2)])
        MT_rot = Rot([P.sbuf([128, 8, 128], BF16) for _ in range(2)])
        ChT_rot = Rot([P.sbuf([128, 8, 128], BF16) for _ in range(2)])
        X_rot = Rot([P.sbuf([128, 512], BF16) for _ in range(2)])
        Xd_rot = Rot([P.sbuf([128, 512], BF16) for _ in range(2)])
        Btm_rot = Rot([P.sbuf([128, 128], BF16) for _ in range(2)])
        yg_rot = Rot([P.sbuf([128, 4, 128], F32) for _ in range(2)])
        sq_rot = Rot([P.sbuf([128, 128], F32) for _ in range(3)])
        rstd_rot = Rot([P.sbuf([128, 128], F32) for _ in range(2)])
        H = P.sbuf([128, 512], F32); t_H = Trk()
        Hbf = P.sbuf([128, 512], BF16); t_Hbf = Trk()
        yo_rot = Rot([P.sbuf([128, 4, SC], BF16) for _ in range(2)])
        ps_bc = [P.psum([128, 512], F32) for _ in range(2)]; t_bc = Trk()
        ps_tr = P.psum([128, 1024], BF16); t_tr = Trk()
        ps_sm = P.psum([128, 512], F32); t_sm = Trk()
        ps_y_rot = Rot([P.psum([128, 512], F32) for _ in range(2)])
        ps_st = P.psum([128, 512], F32); t_st = Trk()
        ps_n = P.psum([128, 512], F32); t_n = Trk()
        t_out = Trk("out")
        xv = xbc.rearrange("(c p) t -> p c t", p=128)
        zv = zT.rearrange("(c p) t -> p c t", p=128)
        ov = o_ys.rearrange("(c p) t -> p c t", p=128)

        first = False
        P.dma("sp", H[:, :], h_in[:, :], writes=(t_H,), key="hin")
        P.op("act", (lambda e: e.activation(out=Hbf[:, :], in_=H[:, :], func=AF.Copy)), reads=(t_H,), writes=(t_Hbf,))
        for sc in range(NSC):
            t0 = sc * SC
            P.dma("sp", raw[:, :, :], xv[:, :, t0:t0 + SC + 4], writes=(t_raw,), key="raw")
            P.dma("sp", zr[:, :, :], zv[:, :, t0:t0 + SC], writes=(t_zr,), key="zr")
            P.dma("sp", dtr[:, :], dtT[:, t0:t0 + SC], writes=(t_dtr,), key="dtr")
            for ch in range(6):
                ct, t_ct = ct_rot.next()
                P.op("pool", (lambda e, ct=ct, ch=ch: e.tensor_scalar(out=ct[:, :], in0=raw[:, ch, 1:1 + SC], scalar1=cw_sb[:, ch * 4:ch * 4 + 1],
                                                                       scalar2=cb_sb[:, ch:ch + 1], op0=ALU.mult, op1=ALU.add)),
                     reads=(t_raw, t_c), writes=(t_ct,))
                for k in (1, 2, 3):
                    P.op("dve", (lambda e, ct=ct, ch=ch, k=k: e.scalar_tensor_tensor(
                        out=ct[:, :], in0=raw[:, ch, 1 + k:1 + k + SC], scalar=cw_sb[:, ch * 4 + k:ch * 4 + k + 1], in1=ct[:, :],
                        op0=ALU.mult, op1=ALU.add)), reads=(t_raw, t_c, t_ct), writes=(t_ct,))
                P.op("act", (lambda e, ct=ct, ch=ch: e.activation(out=xc[:, ch, :], in_=ct[:, :], func=AF.Silu)),
                     reads=(t_ct,), writes=(t_xc,))
            for pc in range(4):
                P.op("act", (lambda e, pc=pc: e.activation(out=sz[:, pc, :], in_=zr[:, pc, :], func=AF.Silu)),
                     reads=(t_zr,), writes=(t_sz,))
            P.op("act", (lambda e: e.activation(out=dts[:, :], in_=dtr[:, :], func=AF.Exp, bias=dtb_sb[:8, 0:1])),
                 reads=(t_dtr, t_c), writes=(t_dts,))
            P.op("act", (lambda e: e.activation(out=dts[:, :], in_=dts[:, :], func=AF.Ln, bias=one_sb[:8, 0:1])),
                 reads=(t_dts, t_c), writes=(t_dts,))
            P.op("dve", (lambda e: e.tensor_scalar(out=aT[:, :], in0=dts[:, :], scalar1=A_sb[:8, 0:1], scalar2=None, op0=ALU.mult)),
                 reads=(t_dts, t_c), writes=(t_aT,))
            yo, t_yo = yo_rot.next()
            for c in range(NCH):
                o = c * 128
                cs = slice(o, o + 128)
                if dbg <= 1:
                    continue
                acs, t_acs = acs_rot.next()
                P.op("dve", (lambda e, acs=acs, cs=cs: e.tensor_tensor_scan(out=acs[:, :], data0=ones_f[:8, :], data1=aT[:, cs],
                                                                           initial=0.0, op0=ALU.mult, op1=ALU.add)),
                     reads=(t_aT, t_c), writes=(t_acs,))
                if dbg <= 2:
                    continue
                def trs(e, acs=acs, cs=cs):
                    e.transpose(ps_sm[:, 0:8], dts[:, cs], idf_sb[:8, :8])
                    return e.transpose(ps_sm[:, 8:16], acs[:, :], idf_sb[:8, :8])
                P.op("pe", trs, reads=(t_dts, t_acs, t_c), writes=(t_sm,))
                datm, t_datm = datm_rot.next()
                P.op("act", (lambda e, datm=datm: e.activation(out=datm[:, :], in_=ps_sm[:, 0:16], func=AF.Copy)),
                     reads=(t_sm,), writes=(t_datm,))
                if dbg <= 3:
                    continue
                def bcs(e, acs=acs):
                    for hh in range(8):
                        ins = e.matmul(ps_bc[hh // 4][:, (hh % 4) * 128:(hh % 4 + 1) * 128], oh8_sb[:, hh * 128:(hh + 1) * 128],
                                       acs[:, :], start=True, stop=True)
                    return ins
                P.op("pe", bcs, reads=(t_acs, t_c), writes=(t_bc,))
                if dbg <= 4:
                    continue
                E, t_E = E_rot.next()
                for hh in range(8):
                    P.op("act", (lambda e, E=E, hh=hh: e.activation(out=E[:, hh, :],
                                                                    in_=ps_bc[hh // 4][:, (hh % 4) * 128:(hh % 4 + 1) * 128], func=AF.Exp)),
                         reads=(t_bc,), writes=(t_E,))
                Dp, t_Dp = Dp_rot.next()
                for hh in range(8):
                    P.op("dve", (lambda e, Dp=Dp, hh=hh, datm=datm: e.tensor_scalar(
                        out=Dp[:, hh, :], in0=ps_bc[hh // 4][:, (hh % 4) * 128:(hh % 4 + 1) * 128], scalar1=datm[:, 8 + hh:9 + hh],
                        scalar2=None, op0=ALU.subtract)), reads=(t_bc, t_datm, t_c), writes=(t_Dp,))
                    P.op("dve", (lambda e, Dp=Dp, hh=hh: e.tensor_scalar(out=Dp[:, hh, :], in0=Dp[:, hh, :], scalar1=0.0, scalar2=None,
                                                                        op0=ALU.min)), reads=(t_Dp,), writes=(t_Dp,))
                    P.op("act", (lambda e, Dp=Dp, hh=hh: e.activation(out=Dp[:, hh, :], in_=Dp[:, hh, :], func=AF.Exp)),
                         reads=(t_Dp,), writes=(t_Dp,))
                if dbg <= 5:
                    continue
                P.op("pe", (lambda e, cs=cs: e.matmul(ps_sm[:, 128:256], xc[:, 4, cs], xc[:, 5, cs], start=True, stop=True)),
                     reads=(t_xc, t_datm), writes=(t_sm,))
                cbm, t_cbm = cbm_rot.next()
                P.op("dve", (lambda e, cbm=cbm: e.tensor_tensor(out=cbm[:, :], in0=ps_sm[:, 128:256], in1=tri_sb[:, :], op=ALU.mult)),
                     reads=(t_sm, t_c), writes=(t_cbm,))
                if dbg <= 6:
                    continue
                MT, t_MT = MT_rot.next()
                ChT, t_ChT = ChT_rot.next()
                for hh in range(8):
                    P.op("pool", (lambda e, MT=MT, Dp=Dp, cbm=cbm, hh=hh: e.tensor_tensor(out=MT[:, hh, :], in0=Dp[:, hh, :], in1=cbm[:, :],
                                                                                         op=ALU.mult)),
                         reads=(t_Dp, t_cbm), writes=(t_MT,))
                    P.op("pool", (lambda e, ChT=ChT, E=E, hh=hh, cs=cs: e.tensor_tensor(out=ChT[:, hh, :], in0=E[:, hh, :], in1=xc[:, 5, cs],
                                                                                       op=ALU.mult)),
                         reads=(t_E, t_xc), writes=(t_ChT,))
                if dbg <= 7:
                    continue
                def trx(e, cs=cs):
                    for pc in range(4):
                        e.transpose(ps_tr[:, pc * 128:(pc + 1) * 128], xc[:, pc, cs], idb_sb[:, :])
                    return e.transpose(ps_tr[:, 512:640], xc[:, 4, cs], idb_sb[:, :])
                P.op("pe", trx, reads=(t_xc, t_c), writes=(t_tr,))
                X, t_X = X_rot.next()
                Xd, t_Xd = Xd_rot.next()
                Btm, t_Btm = Btm_rot.next()
                P.op("act", (lambda e, Btm=Btm: e.activation(out=Btm[:, :], in_=ps_tr[:, 512:640], func=AF.Copy)),
                     reads=(t_tr,), writes=(t_Btm,))
                for hh in range(8):
                    hs = slice(hh * 64, (hh + 1) * 64)
                    P.op("dve", (lambda e, X=X, hs=hs, hh=hh, datm=datm: e.tensor_scalar(
                        out=X[:, hs], in0=ps_tr[:, hs], scalar1=datm[:, hh:hh + 1], scalar2=None, op0=ALU.mult)),
                        reads=(t_tr, t_datm), writes=(t_X,))
                    P.op("dve", (lambda e, Xd=Xd, hs=hs, hh=hh, datm=datm, Dp=Dp: e.tensor_scalar(
                        out=Xd[:, hs], in0=ps_tr[:, hs], scalar1=datm[:, hh:hh + 1], scalar2=Dp[:, hh, 127:128],
                        op0=ALU.mult, op1=ALU.mult)), reads=(t_tr, t_datm, t_Dp), writes=(t_Xd,))
                if dbg <= 8:
                    continue
                ps_y, t_y = ps_y_rot.next()

                def ymm(e, X=X, MT=MT, ChT=ChT, ps_y=ps_y, first=first):
                    for hh in range(8):
                        hp, ee = hh // 2, hh % 2
                        out = ps_y[ee * 64:(ee + 1) * 64, hp * 128:(hp + 1) * 128]
                        ins = e.matmul(out, X[:, hh * 64:(hh + 1) * 64], MT[:, hh, :], start=True, stop=first)
                        if not first:
                            ins = e.matmul(out, Hbf[:, hh * 64:(hh + 1) * 64], ChT[:, hh, :], start=False, stop=True)
                    return ins
                P.op("pe", ymm, reads=(t_X, t_MT, t_ChT, t_Hbf), writes=(t_y,))
                if dbg <= 9:
                    continue
                P.op("pe", (lambda e, Btm=Btm, Xd=Xd: e.matmul(ps_st[:, :], Btm[:, :], Xd[:, :], start=True, stop=True)),
                     reads=(t_Btm, t_Xd), writes=(t_st,))
                for hh in range(8):
                    hs = slice(hh * 64, (hh + 1) * 64)
                    if first:
                        P.op("dve", (lambda e, hs=hs: e.tensor_copy(out=H[:, hs], in_=ps_st[:, hs])), reads=(t_st, t_y), writes=(t_H,))
                    else:
                        P.op("dve", (lambda e, hs=hs, hh=hh, E=E: e.scalar_tensor_tensor(
                            out=H[:, hs], in0=H[:, hs], scalar=E[:, hh, 127:128], in1=ps_st[:, hs], op0=ALU.mult, op1=ALU.add)),
                            reads=(t_st, t_E, t_H, t_y), writes=(t_H,))
                P.op("act", (lambda e: e.activation(out=Hbf[:, :], in_=H[:, :], func=AF.Copy)), reads=(t_H, t_y), writes=(t_Hbf,))
                if dbg <= 10:
                    continue
                yg, t_yg = yg_rot.next()
                for hp in range(4):
                    P.op("dve", (lambda e, yg=yg, hp=hp, cs=cs, ps_y=ps_y: e.scalar_tensor_tensor(
                        out=yg[:, hp, :], in0=xc[:, hp, cs], scalar=dsk_sb[:, hp:hp + 1], in1=ps_y[:, hp * 128:(hp + 1) * 128],
                        op0=ALU.mult, op1=ALU.add)), reads=(t_xc, t_y, t_c), writes=(t_yg,))
                    P.op("pool", (lambda e, yg=yg, hp=hp, cs=cs: e.tensor_tensor(out=yg[:, hp, :], in0=yg[:, hp, :], in1=sz[:, hp, cs], op=ALU.mult)),
                         reads=(t_yg, t_sz), writes=(t_yg,))
                for hp in range(4):
                    sq, t_sq = sq_rot.next()
                    P.op("pool", (lambda e, sq=sq, yg=yg, hp=hp: e.tensor_tensor(out=sq[:, :], in0=yg[:, hp, :], in1=yg[:, hp, :], op=ALU.mult)),
                         reads=(t_yg,), writes=(t_sq,))
                    P.op("pe", (lambda e, sq=sq, hp=hp: e.matmul(ps_n[:, :128], ones_f[:, :], sq[:, :], start=(hp == 0), stop=(hp == 3))),
                         reads=(t_sq, t_c), writes=(t_n,))
                rstd, t_rstd = rstd_rot.next()
                P.op("act", (lambda e, rstd=rstd: e.activation(out=rstd[:, :], in_=ps_n[:, :128], func=AF.Ln, bias=eps_sb[:, 0:1], scale=1.0 / 512.0)),
                     reads=(t_n, t_c), writes=(t_rstd,))
                P.op("act", (lambda e, rstd=rstd: e.activation(out=rstd[:, :], in_=rstd[:, :], func=AF.Exp, scale=-0.5)),
                     reads=(t_rstd,), writes=(t_rstd,))
                for hp in range(4):
                    P.op("dve", (lambda e, yg=yg, hp=hp, cs=cs, rstd=rstd, yo=yo: e.scalar_tensor_tensor(
                        out=yo[:, hp, cs], in0=yg[:, hp, :], scalar=nw_sb[:, hp:hp + 1], in1=rstd[:, :], op0=ALU.mult, op1=ALU.mult)),
                        reads=(t_yg, t_rstd, t_c), writes=(t_yo,))
                first = False
            P.dma("pool", ov[:, :, t0:t0 + SC], yo[:, :, :], reads=(t_yo,), writes=(t_out,), key=f"oys{yo_rot.i}")
        P.dma("pool", o_h[:, :], H[:, :], reads=(t_H,), writes=(t_out,), key="oh")
        P.finish((t_out,))
        P.emit()
    return nc


_PROGS = {}


def _prog(name, fn):
    if name not in _PROGS:
        _PROGS[name] = fn()
    return _PROGS[name]


def _lay(v):
    return np.ascontiguousarray(np.asarray(v, np.float32).reshape(-1, 128).T)


def _run(nc, in_maps):
    res = run_bass_kernel_spmd(nc, in_maps, core_ids=list(range(NCORES)))
    return res.results


def kernel(x, mem, norm_mix_pre, norm_mix_post, norm_ffn_pre, norm_ffn_post, norm_mem,
           w_in, conv_ssd_w, conv_ssd_b, dt_bias, a_log, d_skip, ssd_norm, w_mem_kv,
           w_br_attn, w_br_ssd, w_br_mem, w_out, w_up, conv_ffn_w, conv_ffn_b, w_down):
    f32 = lambda a: np.asarray(a, np.float32)
    x = f32(x)
    mem = f32(mem)
    B, S, _ = x.shape
    T = S * B // NCORES
    J = S // T
    depth = w_in.shape[0]
    p1 = _prog("p1", lambda: build_p1(T))
    p2a = _prog("p2a", lambda: build_p2a(S))
    p2b = _prog("p2b", lambda: build_p2b(2048, SC=512))
    p3a = _prog("p3a", lambda: build_p3a(T))
    p3b = _prog("p3b", lambda: build_p3b(T))
    atab = attn_tables(S)
    stab = ssd_tables()
    cores = [(b, j) for b in range(B) for j in range(J)]
    xT = [np.ascontiguousarray(x[b, j * T:(j + 1) * T, :].T) for (b, j) in cores]
    memT = [np.ascontiguousarray(mem[b].T) for b in range(B)]
    pad8 = lambda v: np.ascontiguousarray(np.broadcast_to(np.pad(f32(v), (0, 120))[:, None], (128, 2)))
    for l in range(depth):
        w_in_l = f32(w_in[l])
        nw = _lay(norm_mix_pre[l])
        r1 = _run(p1, [dict(xT=xT[c], nw=nw, w_in=w_in_l) for c in range(NCORES)])
        del w_in_l
        full = lambda key, b: np.concatenate([r1[b * J + j][key] for j in range(J)], axis=1)
        qk_f = [full("o_qk", b) for b in range(B)]
        z_f = [full("o_z", b) for b in range(B)]
        xbc_f = [full("o_xbc", b) for b in range(B)]
        dt_f = [full("o_dt", b) for b in range(B)]
        v_f = [np.concatenate([r1[b * J + j]["o_v"] for j in range(J)], axis=0) for b in range(B)]
        maps = []
        for (b, g) in cores:
            hs = (2 * g, 2 * g + 1)
            qk = np.stack([qk_f[b][h * 128:(h + 1) * 128] for h in hs] + [qk_f[b][1024 + h * 128:1024 + (h + 1) * 128] for h in hs])
            qks = np.ascontiguousarray(np.concatenate([qk[:, 64:], qk[:, :64]], axis=1))
            v = np.ascontiguousarray(np.stack([v_f[b][:, h * 128:(h + 1) * 128] for h in hs]))
            maps.append(dict(qk=np.ascontiguousarray(qk), qks=qks, v=v, **atab))
        r2a = _run(p2a, maps)
        cw_l = f32(conv_ssd_w[l])
        cb_l = f32(conv_ssd_b[l])
        SEG = 2048
        hst = [np.zeros((128, 512), np.float32) for _ in range(NCORES)]
        ys_parts = [[] for _ in range(NCORES)]
        for sg in range(S // SEG):
            maps = []
            for c, (b, g) in enumerate(cores):
                ch = np.concatenate([np.arange(g * 512, (g + 1) * 512), 2048 + np.arange(g * 128, (g + 1) * 128),
                                     2560 + np.arange(g * 128, (g + 1) * 128)])
                cw = np.ascontiguousarray(cw_l[:, ch].T.reshape(6, 128, 4).transpose(1, 0, 2).reshape(128, 24))
                seg = xbc_f[b][ch][:, sg * SEG:(sg + 1) * SEG]
                halo = xbc_f[b][ch][:, sg * SEG - 4:sg * SEG] if sg > 0 else np.zeros((768, 4), seg.dtype)
                maps.append(dict(xbc=np.ascontiguousarray(np.concatenate([halo, seg], axis=1)), h_in=hst[c],
                                 zT=np.ascontiguousarray(z_f[b][g * 512:(g + 1) * 512, sg * SEG:(sg + 1) * SEG]),
                                 dtT=np.ascontiguousarray(dt_f[b][g * 8:(g + 1) * 8, sg * SEG:(sg + 1) * SEG]), cw=cw, cb=_lay(cb_l[ch]),
                                 dtb=pad8(dt_bias[l][g * 8:(g + 1) * 8]), alog=pad8(a_log[l][g * 8:(g + 1) * 8]),
                                 dsk=_lay(np.repeat(f32(d_skip[l][g * 8:(g + 1) * 8]), 64)),
                                 nw=_lay(f32(ssd_norm[l])[g * 512:(g + 1) * 512]), **stab))
            rr = _run(p2b, maps)
            for c in range(NCORES):
                hst[c] = np.ascontiguousarray(rr[c]["o_h"])
                ys_parts[c].append(rr[c]["o_ys"])
        r2b = [dict(o_ys=np.concatenate(ys_parts[c], axis=1)) for c in range(NCORES)]
        ya_f = [np.concatenate([r2a[b * J + g]["o_ya"] for g in range(J)], axis=0) for b in range(B)]
        ys_f = [np.concatenate([r2b[b * J + g]["o_ys"] for g in range(J)], axis=0) for b in range(B)]
        del qk_f, z_f, xbc_f, dt_f, v_f
        wts = dict(w_kv=f32(w_mem_kv[l]), w_ba=f32(w_br_attn[l]), w_bs=f32(w_br_ssd[l]), w_bm=f32(w_br_mem[l]), w_o=f32(w_out[l]),
                   nw_mem=_lay(norm_mem[l]), nw_post=_lay(norm_mix_post[l]), nw_pre=_lay(norm_ffn_pre[l]))
        maps = []
        for c, (b, j) in enumerate(cores):
            ts = slice(j * T, (j + 1) * T)
            maps.append(dict(xT=xT[c], yaT=np.ascontiguousarray(ya_f[b][:, ts]), ysT=np.ascontiguousarray(ys_f[b][:, ts]),
                             qmT=r1[c]["o_qm"], gT=r1[c]["o_g"], memT=memT[b], **wts))
        r3a = _run(p3a, maps)
        del r1, r2a, r2b, ya_f, ys_f, wts, maps
        cwf = f32(conv_ffn_w[l])
        cw = np.ascontiguousarray(cwf.T.reshape(2 * NF, 128, 3).transpose(1, 0, 2).reshape(128, 2 * NF * 3))
        wts = dict(w_up=f32(w_up[l]), w_dn=f32(w_down[l]), cw=cw, cb=_lay(conv_ffn_b[l]), nw=_lay(norm_ffn_post[l]))
        maps = []
        for c, (b, j) in enumerate(cores):
            h2 = r3a[c]["o_h2"]
            halo = r3a[c - 1]["o_h2"][:, -2:] if j > 0 else np.zeros((D, 2), h2.dtype)
            maps.append(dict(h2T=np.ascontiguousarray(np.concatenate([halo, h2], axis=1)), xmT=r3a[c]["o_xm"], **wts))
        r3b = _run(p3b, maps)
        xT = [np.ascontiguousarray(r3b[c]["o_x"]) for c in range(NCORES)]
        del r3a, r3b, wts, maps
    out = np.empty((B, S, D), np.float32)
    for c, (b, j) in enumerate(cores):
        out[b, j * T:(j + 1) * T, :] = xT[c].T
    return out
```

```python
import contextlib
import numpy as np
import ml_dtypes
import concourse.bass as bass
import concourse.mybir as mybir
from concourse.bass_utils import run_bass_kernel_spmd

F32 = mybir.dt.float32
BF16 = mybir.dt.bfloat16
AF = mybir.ActivationFunctionType
ALU = mybir.AluOpType
AX = mybir.AxisListType
NPBF = ml_dtypes.bfloat16

D = 2048
KC = D // 128
NCORES = 8
EPS = 1e-6
ENGS = ("pe", "act", "dve", "pool", "sp")


class Trk:
    __slots__ = ("W", "R", "name")

    def __init__(self, name=""):
        self.W = {}
        self.R = {}
        self.name = name


class Prog:
    def __init__(self, nc, stack):
        self.nc = nc
        self.stack = stack
        self.ops = {e: [] for e in ENGS}
        self.cnt = {}
        self.seen = {e: {} for e in ENGS}
        self.esem = {}
        for e in ENGS:
            if e != "sp":
                s = stack.enter_context(nc.semaphore("c_" + e))
                self.esem[e] = s
                self.cnt[id(s)] = 0
        self.dsems = {}
        self.semobj = {}
        self.nuniq = 0

    def sbuf(self, shape, dt, name=None):
        self.nuniq += 1
        return self.stack.enter_context(self.nc.sbuf_tensor(name or f"sb{self.nuniq}", list(shape), dt))

    def psum(self, shape, dt, name=None):
        self.nuniq += 1
        return self.stack.enter_context(self.nc.psum_tensor(name or f"ps{self.nuniq}", list(shape), dt))

    def dma_sem(self, key):
        if key not in self.dsems:
            s = self.stack.enter_context(self.nc.semaphore("d_" + str(key)))
            self.dsems[key] = s
            self.cnt[id(s)] = 0
        return self.dsems[key]

    def _waits(self, eng, reads, writes):
        need = {}
        objs = {}
        for t in reads:
            for k, (s, v) in t.W.items():
                if need.get(k, 0) < v:
                    need[k] = v
                    objs[k] = s
        for t in writes:
            for dct in (t.W, t.R):
                for k, (s, v) in dct.items():
                    if need.get(k, 0) < v:
                        need[k] = v
                        objs[k] = s
        out = []
        seen = self.seen[eng]
        own = id(self.esem[eng]) if eng in self.esem else None
        for k, v in need.items():
            if eng == "pe" and k == own:
                continue
            if seen.get(k, 0) >= v:
                continue
            seen[k] = v
            out.append((objs[k], v))
        return out

    def _record(self, sem, val, reads, writes):
        k = id(sem)
        for t in reads:
            t.R[k] = (sem, val)
        for t in writes:
            t.W[k] = (sem, val)

    def op(self, eng, fn, reads=(), writes=()):
        waits = self._waits(eng, reads, writes)
        sem = self.esem[eng]
        self.cnt[id(sem)] += 1
        val = self.cnt[id(sem)]
        self._record(sem, val, reads, writes)
        self.ops[eng].append((waits, fn, sem, 1))

    def dma(self, q, out, in_, reads=(), writes=(), key="d", **kw):
        waits = self._waits(q, reads, writes)
        sem = self.dma_sem(key)
        self.cnt[id(sem)] += 16
        val = self.cnt[id(sem)]
        self._record(sem, val, reads, writes)
        self.ops[q].append((waits, (lambda e: e.dma_start(out=out, in_=in_, **kw)), sem, 16))

    def finish(self, trackers, eng="sp"):
        waits = self._waits(eng, trackers, ())
        self.ops[eng].append((waits, None, None, 0))

    def emit(self):
        nc = self.nc
        allsems = list(self.esem.values()) + list(self.dsems.values())
        with nc.Block() as b0:
            def clr(e):
                for s in allsems:
                    e.sem_clear(s)
            b0.sync(clr)
        with nc.Block() as block:
            def mk(ename):
                def body(e):
                    for waits, fn, sem, inc in self.ops[ename]:
                        for (s, v) in waits:
                            e.wait_ge(s, v)
                        if fn is not None:
                            ins = fn(e)
                            ins.then_inc(sem, inc)
                return body
            block.tensor(mk("pe"))
            block.scalar(mk("act"))
            block.vector(mk("dve"))
            block.gpsimd(mk("pool"))
            block.sync(mk("sp"))


class Rot:
    def __init__(self, bufs):
        self.bufs = bufs
        self.trk = [Trk() for _ in bufs]
        self.i = -1

    def next(self):
        self.i = (self.i + 1) % len(self.bufs)
        return self.bufs[self.i], self.trk[self.i]


class CastTrk:
    def __init__(self):
        self.blocks = []

    def add(self, c0, c1, trk):
        self.blocks.append((c0, c1, trk))

    def get(self, c0=None, c1=None):
        if c0 is None:
            return tuple(t for (_, _, t) in self.blocks)
        return tuple(t for (a, b, t) in self.blocks if a < c1 and c0 < b)


def cast_w_dram(P, src, dst, rows, cols, ct=None, key="wc", rstep=512):
    ct = ct if ct is not None else CastTrk()
    for c0 in range(0, cols, 2048):
        c1 = min(cols, c0 + 2048)
        trk = Trk()
        for r0 in range(0, rows, rstep):
            r1 = min(rows, r0 + rstep)
            P.dma("pool", dst[r0:r1, c0:c1], src[r0:r1, c0:c1], writes=(trk,), key=key)
        ct.add(c0, c1, trk)
    return ct


def rms_stats(P, xt, xt_trk, ncols, ones_f, ones_f_trk, sq_rot, ps_stat, ps_trk, rstd, rstd_trk, eps_sb, nchunks=KC, dim=D):
    for kc in range(nchunks):
        sq, sqt = sq_rot.next()
        P.op("act", (lambda e, sq=sq, kc=kc: e.activation(out=sq[:, :ncols], in_=xt[:, kc, :ncols], func=AF.Square)),
             reads=(xt_trk,), writes=(sqt,))
        P.op("pe", (lambda e, sq=sq, kc=kc: e.matmul(ps_stat[:, :ncols], ones_f[:, :], sq[:, :ncols],
                                                       start=(kc == 0), stop=(kc == nchunks - 1))),
             reads=(sqt, ones_f_trk), writes=(ps_trk,))
    P.op("act", (lambda e: e.activation(out=rstd[:, :ncols], in_=ps_stat[:, :ncols], func=AF.Sqrt,
                                        bias=eps_sb[:, 0:1], scale=1.0 / dim)),
         reads=(ps_trk, ones_f_trk), writes=(rstd_trk,))
    P.op("dve", (lambda e: e.reciprocal(out=rstd[:, :ncols], in_=rstd[:, :ncols])),
         reads=(rstd_trk,), writes=(rstd_trk,))


SEG_Q, SEG_K, SEG_V, SEG_Z, SEG_XBC, SEG_DT, SEG_QM, SEG_G = 0, 1024, 2048, 3072, 5120, 8192, 8224, 9248
N_IN = 15392


def build_p1(T, ncols_total=N_IN):
    nc = bass.Bass("TRN2", target_bir_lowering=False)
    xT = nc.dram_tensor("xT", [D, T], F32, kind="ExternalInput").ap()
    nw = nc.dram_tensor("nw", [128, KC], F32, kind="ExternalInput").ap()
    w_in = nc.dram_tensor("w_in", [D, N_IN], F32, kind="ExternalInput").ap()
    o_qk = nc.dram_tensor("o_qk", [2048, T], BF16, kind="ExternalOutput").ap()
    o_v = nc.dram_tensor("o_v", [T, 1024], BF16, kind="ExternalOutput").ap()
    o_z = nc.dram_tensor("o_z", [2048, T], BF16, kind="ExternalOutput").ap()
    o_xbc = nc.dram_tensor("o_xbc", [3072, T], BF16, kind="ExternalOutput").ap()
    o_dt = nc.dram_tensor("o_dt", [32, T], F32, kind="ExternalOutput").ap()
    o_qm = nc.dram_tensor("o_qm", [1024, T], BF16, kind="ExternalOutput").ap()
    o_g = nc.dram_tensor("o_g", [6144, T], BF16, kind="ExternalOutput").ap()
    w_bf = nc.dram_tensor("w_bf", [D, N_IN], BF16).ap()
    NTG = T // 512
    with contextlib.ExitStack() as st:
        P = Prog(nc, st)
        ct_w = cast_w_dram(P, w_in, w_bf, D, N_IN)

        ones_f = P.sbuf([128, 128], F32)
        t_ones = Trk()
        P.op("pool", lambda e: e.memset(ones_f[:, :], 1.0), writes=(t_ones,))
        eps_sb = P.sbuf([128, 1], F32)
        P.op("pool", lambda e: e.memset(eps_sb[:, :], EPS), writes=(t_ones,))
        nw_sb = P.sbuf([128, KC], F32)
        t_nw = Trk()
        P.dma("sp", nw_sb[:, :], nw[:, :], writes=(t_nw,), key="c0")
        hT = P.sbuf([128, KC, T], BF16)
        t_hT = Trk()
        xt_rot = Rot([P.sbuf([128, KC, 512], F32) for _ in range(1)])
        sq_rot = Rot([P.sbuf([128, 512], F32) for _ in range(3)])
        rstd = P.sbuf([128, 512], F32)
        t_rstd = Trk()
        ps_stat = P.psum([128, 512], F32)
        t_pss = Trk()
        xTv = xT.rearrange("(kc p) t -> p kc t", p=128)
        for tg in range(NTG):
            xt, t_xt = xt_rot.next()
            P.dma("sp", xt[:, :, :], xTv[:, :, tg * 512:(tg + 1) * 512], writes=(t_xt,), key=f"x{xt_rot.i}")
            rms_stats(P, xt, t_xt, 512, ones_f, t_ones, sq_rot, ps_stat, t_pss, rstd, t_rstd, eps_sb)
            for kc in range(KC):
                P.op("dve", (lambda e, xt=xt, kc=kc, tg=tg: e.scalar_tensor_tensor(
                    out=hT[:, kc, tg * 512:(tg + 1) * 512], in0=xt[:, kc, :], scalar=nw_sb[:, kc:kc + 1],
                    in1=rstd[:, :], op0=ALU.mult, op1=ALU.mult)),
                    reads=(t_xt, t_rstd, t_nw), writes=(t_hT,))

        wv = w_bf.rearrange("(kc p) n -> p kc n", p=128)
        w_rot = Rot([P.sbuf([128, KC, 512], BF16) for _ in range(2)])
        ps_rot = Rot([P.psum([128, 512], F32) for _ in range(4)])
        stg_rot = Rot([P.sbuf([128, 2048], BF16) for _ in range(3)])
        stgf_rot = Rot([P.sbuf([128, 2048], F32) for _ in range(1)])
        t_out = Trk("out")
        evac_i = [0]

        def evac(dst, src, rd, wr, func=None):
            if func is not None:
                P.op("act", lambda e: e.activation(out=dst, in_=src, func=func), reads=rd, writes=wr)
                return
            evac_i[0] += 1
            if evac_i[0] % 2 == 0:
                P.op("act", lambda e: e.activation(out=dst, in_=src, func=AF.Copy), reads=rd, writes=wr)
            else:
                P.op("dve", lambda e: e.tensor_copy(out=dst, in_=src), reads=rd, writes=wr)

        def load_w(c0, ncol):
            wt, t_w = w_rot.next()
            P.dma("sp", wt[:, :, :ncol], wv[:, :, c0:c0 + ncol], reads=ct_w.get(c0, c0 + ncol), writes=(t_w,), key=f"w{w_rot.i}")
            return wt, t_w

        def fm_group(c0, ncol, dst, drow0, func=None, f32out=False):
            wt, t_w = load_w(c0, ncol)
            for s0 in range(0, ncol, 128):
                sn = min(128, ncol - s0)
                stg, t_stg = (stgf_rot if f32out else stg_rot).next()
                for tg in range(NTG):
                    ps, t_ps = ps_rot.next()

                    def mm(e, wt=wt, s0=s0, sn=sn, tg=tg, ps=ps):
                        for kc in range(KC):
                            ins = e.matmul(ps[:sn, :], wt[:, kc, s0:s0 + sn], hT[:, kc, tg * 512:(tg + 1) * 512],
                                           start=(kc == 0), stop=(kc == KC - 1))
                        return ins
                    P.op("pe", mm, reads=(t_w, t_hT), writes=(t_ps,))
                    evac(stg[:sn, tg * 512:(tg + 1) * 512], ps[:sn, :], (t_ps,), (t_stg,), func)
                P.dma("pool", dst[drow0 + s0:drow0 + s0 + sn, :], stg[:sn, :T], reads=(t_stg,), writes=(t_out,),
                      key=f"o{(stgf_rot if f32out else stg_rot).i}{int(f32out)}")

        def tm_group(c0, ncol, dst, dcol0):
            wt, t_w = load_w(c0, ncol)
            for tt in range(T // 128):
                ps, t_ps = ps_rot.next()

                def mm(e, wt=wt, tt=tt, ps=ps):
                    for kc in range(KC):
                        ins = e.matmul(ps[:, :ncol], hT[:, kc, tt * 128:(tt + 1) * 128], wt[:, kc, :ncol],
                                       start=(kc == 0), stop=(kc == KC - 1))
                    return ins
                P.op("pe", mm, reads=(t_w, t_hT), writes=(t_ps,))
                stg, t_stg = stg_rot.next()
                evac(stg[:, :ncol], ps[:, :ncol], (t_ps,), (t_stg,))
                P.dma("pool", dst[tt * 128:(tt + 1) * 128, dcol0:dcol0 + ncol], stg[:, :ncol], reads=(t_stg,),
                      writes=(t_out,), key=f"o{stg_rot.i}0")

        for c0 in range(0, 2048, 512):
            fm_group(SEG_Q + c0, 512, o_qk, c0)
        for c0 in range(0, 1024, 512):
            tm_group(SEG_V + c0, 512, o_v, c0)
        for c0 in range(0, 2048, 512):
            fm_group(SEG_Z + c0, 512, o_z, c0)
        for c0 in range(0, 3072, 512):
            fm_group(SEG_XBC + c0, 512, o_xbc, c0)
        fm_group(SEG_DT, 32, o_dt, 0, f32out=True)
        for c0 in range(0, 1024, 512):
            fm_group(SEG_QM + c0, 512, o_qm, c0)
        for c0 in range(0, 6144, 512):
            fm_group(SEG_G + c0, 512, o_g, c0, func=AF.Sigmoid)
        P.finish((t_out,))
        P.emit()
    return nc


def load_vec16(P, dram_ap, key="c0"):
    n = dram_ap.shape[1]
    t = P.sbuf([128, n], F32)
    trk = Trk()
    P.dma("sp", t[:, :], dram_ap[:, :], writes=(trk,), key=key)
    return t, trk


def mm_group(P, ps, ps_trk, pairs, reads, M=128, N=None):
    def mm(e):
        n = len(pairs)
        for i, (l, r) in enumerate(pairs):
            ins = e.matmul(ps, l, r, start=(i == 0), stop=(i == n - 1))
        return ins
    P.op("pe", mm, reads=reads, writes=(ps_trk,))


def build_p3a(T, TG=256):
    nc = bass.Bass("TRN2", target_bir_lowering=False)
    di = lambda n, s, dt=F32: nc.dram_tensor(n, s, dt, kind="ExternalInput").ap()
    xT = di("xT", [D, T])
    yaT = di("yaT", [1024, T], BF16)
    ysT = di("ysT", [2048, T], BF16)
    qmT = di("qmT", [1024, T], BF16)
    gT = di("gT", [6144, T], BF16)
    memT = di("memT", [D, 256])
    nw_mem = di("nw_mem", [128, KC])
    nw_post = di("nw_post", [128, KC])
    nw_pre = di("nw_pre", [128, KC])
    w_kv = di("w_kv", [D, 2048])
    w_ba = di("w_ba", [1024, D])
    w_bs = di("w_bs", [2048, D])
    w_bm = di("w_bm", [1024, D])
    w_o = di("w_o", [D, D])
    o_xm = nc.dram_tensor("o_xm", [D, T], F32, kind="ExternalOutput").ap()
    o_h2 = nc.dram_tensor("o_h2", [D, T], BF16, kind="ExternalOutput").ap()
    b_kv = nc.dram_tensor("b_kv", [D, 2048], BF16).ap()
    b_ba = nc.dram_tensor("b_ba", [1024, D], BF16).ap()
    b_bs = nc.dram_tensor("b_bs", [2048, D], BF16).ap()
    b_bm = nc.dram_tensor("b_bm", [1024, D], BF16).ap()
    b_o = nc.dram_tensor("b_o", [D, D], BF16).ap()
    NTG = T // TG
    with contextlib.ExitStack() as st:
        P = Prog(nc, st)
        ct_kv = cast_w_dram(P, w_kv, b_kv, D, 2048)
        ct_ba = cast_w_dram(P, w_ba, b_ba, 1024, D)
        ct_bs = cast_w_dram(P, w_bs, b_bs, 2048, D)
        ct_bm = cast_w_dram(P, w_bm, b_bm, 1024, D)
        ct_o = cast_w_dram(P, w_o, b_o, D, D)
        t_c = Trk("consts")
        ones_f = P.sbuf([128, 128], F32)
        P.op("pool", lambda e: e.memset(ones_f[:, :], 1.0), writes=(t_c,))
        ones_b = P.sbuf([128, 128], BF16)
        P.op("pool", lambda e: e.memset(ones_b[:, :], 1.0), writes=(t_c,))
        eps_sb = P.sbuf([128, 1], F32)
        P.op("pool", lambda e: e.memset(eps_sb[:, :], EPS), writes=(t_c,))
        nwm_sb, t_nwm = load_vec16(P, nw_mem)
        nwpo_sb, t_nwpo = load_vec16(P, nw_post)
        nwpr_sb, t_nwpr = load_vec16(P, nw_pre)
        sq_rot = Rot([P.sbuf([128, 256], F32) for _ in range(3)])
        rstd = P.sbuf([128, 256], F32)
        t_rstd = Trk()
        ps_stat = P.psum([128, 256], F32)
        t_pss = Trk()
        ps_rot = Rot([P.psum([128, 512], F32) for _ in range(5)])

        big_rot = Rot([P.sbuf([128, KC, 256], F32) for _ in range(2)])
        mt_sb, t_mt = big_rot.next()
        P.dma("sp", mt_sb[:, :, :], memT.rearrange("(kc p) m -> p kc m", p=128), writes=(t_mt,), key="big0")
        rms_stats(P, mt_sb, t_mt, 256, ones_f, t_c, sq_rot, ps_stat, t_pss, rstd, t_rstd, eps_sb)
        mnT = P.sbuf([128, KC, 256], BF16)
        t_mn = Trk()
        for kc in range(KC):
            P.op("dve", (lambda e, kc=kc: e.scalar_tensor_tensor(
                out=mnT[:, kc, :], in0=mt_sb[:, kc, :], scalar=nwm_sb[:, kc:kc + 1], in1=rstd[:, :],
                op0=ALU.mult, op1=ALU.mult)), reads=(t_mt, t_rstd, t_nwm), writes=(t_mn,))
        KmT = P.sbuf([128, 8, 256], BF16)
        Vm = P.sbuf([128, 2, 1024], BF16)
        t_kv = Trk()
        wkv_rot = Rot([P.sbuf([128, KC, 512], BF16) for _ in range(2)])
        kvv = b_kv.rearrange("(kc p) n -> p kc n", p=128)
        for cg in range(4):
            wt, t_w = wkv_rot.next()
            P.dma("sp", wt[:, :, :], kvv[:, :, cg * 512:(cg + 1) * 512], reads=ct_kv.get(), writes=(t_w,),
                  key=f"wkv{wkv_rot.i}")
            if cg < 2:
                for s in range(4):
                    ps, t_ps = ps_rot.next()
                    mm_group(P, ps[:, :256], t_ps, [(wt[:, kc, s * 128:(s + 1) * 128], mnT[:, kc, :]) for kc in range(KC)],
                             reads=(t_w, t_mn))
                    P.op("act", (lambda e, ps=ps, c=cg * 4 + s: e.activation(out=KmT[:, c, :], in_=ps[:, :256], func=AF.Copy)),
                         reads=(t_ps,), writes=(t_kv,))
            else:
                for mt in range(2):
                    ps, t_ps = ps_rot.next()
                    mm_group(P, ps[:, :], t_ps, [(mnT[:, kc, mt * 128:(mt + 1) * 128], wt[:, kc, :]) for kc in range(KC)],
                             reads=(t_w, t_mn))
                    P.op("act", (lambda e, ps=ps, mt=mt, c0=(cg - 2) * 512: e.activation(
                        out=Vm[:, mt, c0:c0 + 512], in_=ps[:, :], func=AF.Copy)), reads=(t_ps,), writes=(t_kv,))

        qm_sb = P.sbuf([128, 8, TG], BF16); t_qm = Trk()
        ya_sb = P.sbuf([128, 8, TG], BF16); t_ya = Trk()
        ys_sb = P.sbuf([128, 16, TG], BF16); t_ys = Trk()
        ym_sb = P.sbuf([128, 8, TG], BF16); t_ym = Trk()
        mg_sb = P.sbuf([128, KC, TG], BF16); t_mg = Trk()
        pT_rot = Rot([P.sbuf([128, TG], BF16) for _ in range(4)])
        rs_sb = P.sbuf([128, TG], F32); t_rs = Trk()
        g_rot = Rot([P.sbuf([128, 3, 2, TG], BF16) for _ in range(2)])
        wb_rot = Rot([P.sbuf([128, 32, 256], BF16) for _ in range(2)])
        wo_rot = Rot([P.sbuf([128, KC, 256], BF16) for _ in range(2)])
        macc_rot = Rot([P.sbuf([128, TG], F32) for _ in range(2)])
        mtmp_rot = Rot([P.sbuf([128, TG], F32) for _ in range(2)])
        tmp_rot = Rot([P.sbuf([128, TG], F32) for _ in range(2)])
        h2_rot = Rot([P.sbuf([128, KC, TG], BF16) for _ in range(1)])
        t_out = Trk("out")
        xv = xT.rearrange("(kc p) t -> p kc t", p=128)
        oxv = o_xm.rearrange("(kc p) t -> p kc t", p=128)
        ohv = o_h2.rearrange("(kc p) t -> p kc t", p=128)
        bav = b_ba.rearrange("(kc p) n -> p kc n", p=128)
        bsv = b_bs.rearrange("(kc p) n -> p kc n", p=128)
        bmv = b_bm.rearrange("(kc p) n -> p kc n", p=128)
        bov = b_o.rearrange("(kc p) n -> p kc n", p=128)
        gv = gT.rearrange("(g c p) t -> p g c t", p=128, g=3)
        for tg in range(NTG):
            ts = slice(tg * TG, (tg + 1) * TG)
            P.dma("sp", qm_sb[:, :, :], qmT.rearrange("(c p) t -> p c t", p=128)[:, :, ts], writes=(t_qm,), key="qm")
            P.dma("sp", ya_sb[:, :, :], yaT.rearrange("(c p) t -> p c t", p=128)[:, :, ts], writes=(t_ya,), key="ya")
            P.dma("sp", ys_sb[:, :, :], ysT.rearrange("(c p) t -> p c t", p=128)[:, :, ts], writes=(t_ys,), key="ys")
            for hh in range(4):
                pts = []
                for mt in range(2):
                    ps, t_ps = ps_rot.next()
                    mm_group(P, ps[:, :TG], t_ps,
                             [(KmT[:, hh * 2 + dc, mt * 128:(mt + 1) * 128], qm_sb[:, hh * 2 + dc, :]) for dc in range(2)],
                             reads=(t_kv, t_qm))
                    pT, t_pT = pT_rot.next()
                    P.op("act", (lambda e, ps=ps, pT=pT: e.activation(out=pT[:, :], in_=ps[:, :TG], func=AF.Exp, scale=1.0 / 16.0)),
                         reads=(t_ps,), writes=(t_pT,))
                    pts.append((pT, t_pT))
                ps, t_ps = ps_rot.next()
                mm_group(P, ps[:, :TG], t_ps, [(ones_b[:, :], pT[:, :]) for (pT, _) in pts],
                         reads=(t_c,) + tuple(t for _, t in pts))
                P.op("dve", (lambda e, ps=ps: e.reciprocal(out=rs_sb[:, :], in_=ps[:, :TG])), reads=(t_ps,), writes=(t_rs,))
                for dc in range(2):
                    ps, t_ps = ps_rot.next()
                    mm_group(P, ps[:, :TG], t_ps,
                             [(Vm[:, mt, hh * 256 + dc * 128: hh * 256 + (dc + 1) * 128], pts[mt][0][:, :]) for mt in range(2)],
                             reads=(t_kv,) + tuple(t for _, t in pts))
                    P.op("dve", (lambda e, ps=ps, c=hh * 2 + dc: e.tensor_tensor(out=ym_sb[:, c, :], in0=ps[:, :TG], in1=rs_sb[:, :],
                                                                                op=ALU.mult)),
                         reads=(t_ps, t_rs), writes=(t_ym,))
            for cg in range(8):
                wb, t_w = wb_rot.next()
                cs = slice(cg * 256, (cg + 1) * 256)
                k = f"wb{wb_rot.i}"
                P.dma("sp", wb[:, 0:8, :], bav[:, :, cs], reads=ct_ba.get(), writes=(t_w,), key=k)
                P.dma("sp", wb[:, 8:24, :], bsv[:, :, cs], reads=ct_bs.get(), writes=(t_w,), key=k)
                P.dma("sp", wb[:, 24:32, :], bmv[:, :, cs], reads=ct_bm.get(), writes=(t_w,), key=k)
                gt, t_g = g_rot.next()
                for g3 in range(3):
                    P.dma("sp", gt[:, g3, :, :], gv[:, g3, cg * 2:cg * 2 + 2, ts], writes=(t_g,), key=f"g{g_rot.i}")
                for ct in range(2):
                    c = cg * 2 + ct
                    macc, t_ma = macc_rot.next()
                    specs = ((0, 8, ya_sb, t_ya, 0), (8, 16, ys_sb, t_ys, 1), (24, 8, ym_sb, t_ym, 2))
                    for bi, (k0, nk, src, t_src, g3) in enumerate(specs):
                        ps, t_ps = ps_rot.next()
                        mm_group(P, ps[:, :TG], t_ps,
                                 [(wb[:, k0 + kc, ct * 128:(ct + 1) * 128], src[:, kc, :]) for kc in range(nk)],
                                 reads=(t_w, t_src))
                        if bi == 0:
                            P.op("dve", (lambda e, ps=ps, macc=macc, gt=gt, g3=g3, ct=ct: e.tensor_tensor(
                                out=macc[:, :], in0=ps[:, :TG], in1=gt[:, g3, ct, :], op=ALU.mult)),
                                reads=(t_ps, t_g), writes=(t_ma,))
                        else:
                            mtmp, t_mtmp = mtmp_rot.next()
                            P.op("dve", (lambda e, ps=ps, mtmp=mtmp, gt=gt, g3=g3, ct=ct: e.tensor_tensor(
                                out=mtmp[:, :], in0=ps[:, :TG], in1=gt[:, g3, ct, :], op=ALU.mult)),
                                reads=(t_ps, t_g), writes=(t_mtmp,))
                            last = (bi == 2)
                            dst = mg_sb[:, c, :] if last else macc[:, :]
                            P.op("pool", (lambda e, mtmp=mtmp, macc=macc, dst=dst: e.tensor_tensor(
                                out=dst, in0=mtmp[:, :], in1=macc[:, :], op=ALU.add)),
                                reads=(t_mtmp, t_ma), writes=((t_mg,) if last else (t_ma,)))
            yt, t_yt = big_rot.next()
            for cg in range(8):
                wo, t_w = wo_rot.next()
                P.dma("sp", wo[:, :, :], bov[:, :, cg * 256:(cg + 1) * 256], reads=ct_o.get(), writes=(t_w,), key=f"wo{wo_rot.i}")
                for ct in range(2):
                    c = cg * 2 + ct
                    ps, t_ps = ps_rot.next()
                    mm_group(P, ps[:, :TG], t_ps, [(wo[:, kc, ct * 128:(ct + 1) * 128], mg_sb[:, kc, :]) for kc in range(KC)],
                             reads=(t_w, t_mg))
                    P.op("act", (lambda e, ps=ps, c=c, yt=yt: e.activation(out=yt[:, c, :], in_=ps[:, :TG], func=AF.Copy)),
                         reads=(t_ps,), writes=(t_yt,))
            xt, t_xt = big_rot.next()
            P.dma("sp", xt[:, :, :], xv[:, :, ts], writes=(t_xt,), key=f"big{big_rot.i}")
            rms_stats(P, yt, t_yt, TG, ones_f, t_c, sq_rot, ps_stat, t_pss, rstd, t_rstd, eps_sb)
            for kc in range(KC):
                tmp, t_tmp = tmp_rot.next()
                P.op("dve", (lambda e, kc=kc, tmp=tmp, yt=yt: e.scalar_tensor_tensor(
                    out=tmp[:, :], in0=yt[:, kc, :], scalar=nwpo_sb[:, kc:kc + 1], in1=rstd[:, :], op0=ALU.mult, op1=ALU.mult)),
                    reads=(t_yt, t_rstd, t_nwpo), writes=(t_tmp,))
                P.op("pool", (lambda e, kc=kc, tmp=tmp, xt=xt: e.tensor_tensor(out=xt[:, kc, :], in0=tmp[:, :], in1=xt[:, kc, :],
                                                                               op=ALU.add)),
                     reads=(t_tmp, t_xt), writes=(t_xt,))
            P.dma("pool", oxv[:, :, ts], xt[:, :, :], reads=(t_xt,), writes=(t_out,), key="oxm")
            rms_stats(P, xt, t_xt, TG, ones_f, t_c, sq_rot, ps_stat, t_pss, rstd, t_rstd, eps_sb)
            h2, t_h2 = h2_rot.next()
            for kc in range(KC):
                P.op("dve", (lambda e, kc=kc, xt=xt, h2=h2: e.scalar_tensor_tensor(
                    out=h2[:, kc, :], in0=xt[:, kc, :], scalar=nwpr_sb[:, kc:kc + 1], in1=rstd[:, :], op0=ALU.mult, op1=ALU.mult)),
                    reads=(t_xt, t_rstd, t_nwpr), writes=(t_h2,))
            P.dma("pool", ohv[:, :, ts], h2[:, :, :], reads=(t_h2,), writes=(t_out,), key="oh2")
        P.finish((t_out,))
        P.emit()
    return nc


DFF = 5632
NF = DFF // 128
GELU_C = 0.7978845608028654


def build_p3b(T, TG=256, gelu_native=True):
    nc = bass.Bass("TRN2", target_bir_lowering=False)
    di = lambda n, s, dt=F32: nc.dram_tensor(n, s, dt, kind="ExternalInput").ap()
    h2T = di("h2T", [D, T + 2], BF16)
    xmT = di("xmT", [D, T])
    w_up = di("w_up", [D, 2 * DFF])
    w_dn = di("w_dn", [DFF, D])
    cw = di("cw", [128, 2 * NF * 3])
    cb = di("cb", [128, 2 * NF])
    nw = di("nw", [128, KC])
    o_x = nc.dram_tensor("o_x", [D, T], F32, kind="ExternalOutput").ap()
    b_up = nc.dram_tensor("b_up", [D, 2 * DFF], BF16).ap()
    b_dn = nc.dram_tensor("b_dn", [DFF, D], BF16).ap()
    NTG = T // TG
    NE = TG + 2
    with contextlib.ExitStack() as st:
        P = Prog(nc, st)
        ct_up = cast_w_dram(P, w_up, b_up, D, 2 * DFF)
        ct_dn = cast_w_dram(P, w_dn, b_dn, DFF, D)
        t_c = Trk("consts")
        ones_f = P.sbuf([128, 128], F32)
        P.op("pool", lambda e: e.memset(ones_f[:, :], 1.0), writes=(t_c,))
        eps_sb = P.sbuf([128, 1], F32)
        P.op("pool", lambda e: e.memset(eps_sb[:, :], EPS), writes=(t_c,))
        cw_sb, t_cw = load_vec16(P, cw)
        cb_sb, t_cb = load_vec16(P, cb)
        nw_sb, t_nw = load_vec16(P, nw)
        sq_rot = Rot([P.sbuf([128, TG], F32) for _ in range(3)])
        rstd = P.sbuf([128, TG], F32); t_rstd = Trk()
        ps_stat = P.psum([128, TG], F32); t_pss = Trk()
        ps_rot = Rot([P.psum([128, 512], F32) for _ in range(6)])
        big_rot = Rot([P.sbuf([128, KC, TG], F32) for _ in range(2)])
        h2_rot = Rot([P.sbuf([128, KC, NE], BF16) for _ in range(2)])
        act_sb = P.sbuf([128, NF, TG], BF16); t_act = Trk()
        wu_rot = Rot([P.sbuf([128, 2, KC, 256], BF16) for _ in range(3)])
        wd_rot = Rot([P.sbuf([128, NF, 256], BF16) for _ in range(2)])
        u_rot = Rot([P.sbuf([128, NE], F32) for _ in range(4)])
        t_rot = Rot([P.sbuf([128, TG], F32) for _ in range(4)])
        ga_rot = Rot([P.sbuf([128, TG], F32) for _ in range(2)])
        tmp_rot = Rot([P.sbuf([128, TG], F32) for _ in range(2)])
        t_out = Trk("out")
        hv = h2T.rearrange("(kc p) t -> p kc t", p=128)
        xv = xmT.rearrange("(kc p) t -> p kc t", p=128)
        ov = o_x.rearrange("(kc p) t -> p kc t", p=128)
        buv = b_up.rearrange("(kc p) n -> p kc n", p=128)
        bdv = b_dn.rearrange("(f p) n -> p f n", p=128)

        def conv_chain(ps, t_ps, ch):
            u, t_u = u_rot.next()
            P.op("act", (lambda e: e.activation(out=u[:, :], in_=ps[:, :NE], func=AF.Copy)), reads=(t_ps,), writes=(t_u,))
            t, t_t = t_rot.next()
            P.op("pool", (lambda e: e.tensor_scalar(out=t[:, :], in0=u[:, 0:TG], scalar1=cw_sb[:, ch * 3:ch * 3 + 1],
                                                    scalar2=cb_sb[:, ch:ch + 1], op0=ALU.mult, op1=ALU.add)),
                 reads=(t_u, t_cw, t_cb), writes=(t_t,))
            for k in (1, 2):
                P.op("dve", (lambda e, k=k: e.scalar_tensor_tensor(out=t[:, :], in0=u[:, k:k + TG],
                                                                   scalar=cw_sb[:, ch * 3 + k:ch * 3 + k + 1], in1=t[:, :],
                                                                   op0=ALU.mult, op1=ALU.add)),
                     reads=(t_u, t_cw, t_t), writes=(t_t,))
            return t, t_t

        for tg in range(NTG):
            ts = slice(tg * TG, (tg + 1) * TG)
            h2, t_h2 = h2_rot.next()
            P.dma("sp", h2[:, :, :], hv[:, :, tg * TG:tg * TG + NE], writes=(t_h2,), key=f"h2{h2_rot.i}")
            for fg in range(NF // 2):
                wu, t_w = wu_rot.next()
                k = f"wu{wu_rot.i}"
                P.dma("sp", wu[:, 0, :, :], buv[:, :, fg * 256:(fg + 1) * 256], reads=ct_up.get(fg * 256, (fg + 1) * 256), writes=(t_w,), key=k)
                P.dma("sp", wu[:, 1, :, :], buv[:, :, DFF + fg * 256:DFF + (fg + 1) * 256], reads=ct_up.get(DFF + fg * 256, DFF + (fg + 1) * 256), writes=(t_w,), key=k)
                for ft in range(2):
                    f = fg * 2 + ft
                    res = []
                    for half in range(2):
                        ps, t_ps = ps_rot.next()
                        mm_group(P, ps[:, :NE], t_ps, [(wu[:, half, kc, ft * 128:(ft + 1) * 128], h2[:, kc, :]) for kc in range(KC)],
                                 reads=(t_w, t_h2))
                        res.append(conv_chain(ps, t_ps, half * NF + f))
                    (ta, t_ta), (tgg, t_tg) = res
                    ga, t_ga = ga_rot.next()
                    if gelu_native:
                        P.op("act", (lambda e, ta=ta, ga=ga: e.activation(out=ga[:, :], in_=ta[:, :], func=AF.Gelu_apprx_tanh)),
                             reads=(t_ta,), writes=(t_ga,))
                    else:
                        P.op("act", (lambda e, ta=ta, ga=ga: e.activation(out=ga[:, :], in_=ta[:, :], func=AF.Square)),
                             reads=(t_ta,), writes=(t_ga,))
                        P.op("pool", (lambda e, ga=ga: e.tensor_scalar(out=ga[:, :], in0=ga[:, :], scalar1=0.044715, scalar2=1.0,
                                                                      op0=ALU.mult, op1=ALU.add)), reads=(t_ga,), writes=(t_ga,))
                        P.op("pool", (lambda e, ta=ta, ga=ga: e.tensor_tensor(out=ga[:, :], in0=ga[:, :], in1=ta[:, :], op=ALU.mult)),
                             reads=(t_ga, t_ta), writes=(t_ga,))
                        P.op("act", (lambda e, ga=ga: e.activation(out=ga[:, :], in_=ga[:, :], func=AF.Sigmoid, scale=2.0 * GELU_C)),
                             reads=(t_ga,), writes=(t_ga,))
                        P.op("pool", (lambda e, ta=ta, ga=ga: e.tensor_tensor(out=ga[:, :], in0=ga[:, :], in1=ta[:, :], op=ALU.mult)),
                             reads=(t_ga, t_ta), writes=(t_ga,))
                    P.op("dve", (lambda e, ga=ga, tgg=tgg, f=f: e.tensor_tensor(out=act_sb[:, f, :], in0=ga[:, :], in1=tgg[:, :],
                                                                                op=ALU.mult)),
                         reads=(t_ga, t_tg), writes=(t_act,))
            yt, t_yt = big_rot.next()
            for cg in range(8):
                wd, t_w = wd_rot.next()
                P.dma("sp", wd[:, :, :], bdv[:, :, cg * 256:(cg + 1) * 256], reads=ct_dn.get(), writes=(t_w,), key=f"wd{wd_rot.i}")
                for ct in range(2):
                    c = cg * 2 + ct
                    ps, t_ps = ps_rot.next()
                    mm_group(P, ps[:, :TG], t_ps, [(wd[:, f, ct * 128:(ct + 1) * 128], act_sb[:, f, :]) for f in range(NF)],
                             reads=(t_w, t_act))
                    P.op("act", (lambda e, ps=ps, c=c, yt=yt: e.activation(out=yt[:, c, :], in_=ps[:, :TG], func=AF.Copy)),
                         reads=(t_ps,), writes=(t_yt,))
            xt, t_xt = big_rot.next()
            P.dma("sp", xt[:, :, :], xv[:, :, ts], writes=(t_xt,), key=f"big{big_rot.i}")
            rms_stats(P, yt, t_yt, TG, ones_f, t_c, sq_rot, ps_stat, t_pss, rstd, t_rstd, eps_sb)
            for kc in range(KC):
                tmp, t_tmp = tmp_rot.next()
                P.op("dve", (lambda e, kc=kc, tmp=tmp, yt=yt: e.scalar_tensor_tensor(
                    out=tmp[:, :], in0=yt[:, kc, :], scalar=nw_sb[:, kc:kc + 1], in1=rstd[:, :], op0=ALU.mult, op1=ALU.mult)),
                    reads=(t_yt, t_rstd, t_nw), writes=(t_tmp,))
                P.op("pool", (lambda e, kc=kc, tmp=tmp, xt=xt: e.tensor_tensor(out=xt[:, kc, :], in0=tmp[:, :], in1=xt[:, kc, :],
                                                                               op=ALU.add)),
                     reads=(t_tmp, t_xt), writes=(t_xt,))
            P.dma("pool", ov[:, :, ts], xt[:, :, :], reads=(t_xt,), writes=(t_out,), key="ox")
        P.finish((t_out,))
        P.emit()
    return nc


BIGR = 3000.0
NEGF = -1.0e30


def attn_tables(S):
    nb = S // 256
    j = np.arange(nb)[None, :]
    n = np.arange(nb)[:, None]
    past = (j < n)
    pastb = np.where(past, 0.0, NEGF).astype(np.float32).reshape(1, nb * nb)
    past01 = past.astype(np.float32).reshape(1, nb * nb)
    own01 = (j == n).astype(np.float32).reshape(1, nb * nb)
    rep = lambda a: np.ascontiguousarray(np.broadcast_to(a, (128, a.shape[1])))
    k = np.arange(128)[:, None]
    q = np.arange(256)[None, :]
    cmA = np.where(k <= q, 0.0, -BIGR)
    cmB = np.where(128 + k <= q, 0.0, -BIGR)
    cm = np.concatenate([cmA, cmB], axis=1).astype(NPBF)
    oh = np.zeros((32, nb, 128), np.float32)
    for jj in range(nb):
        oh[jj, jj, :] = 1.0
    half = 64
    inv = (10000.0 ** (-np.arange(half, dtype=np.float32) / half)).astype(np.float32)
    ang = np.arange(S, dtype=np.float32)[None, :] * inv[:, None]
    cos = np.cos(ang).astype(np.float32)
    sin = np.sin(ang).astype(np.float32)
    cosT = np.concatenate([cos, cos], axis=0)
    sinT = np.concatenate([-sin, sin], axis=0)
    return dict(pastb=rep(pastb), past01=rep(past01), own01=rep(own01), cm=np.ascontiguousarray(cm),
                onehot=np.ascontiguousarray(oh.reshape(32, nb * 128).astype(NPBF)),
                ident_f=np.eye(128, dtype=np.float32), ident_b=np.eye(128, dtype=np.float32).astype(NPBF),
                cosT=np.ascontiguousarray(cosT), sinT=np.ascontiguousarray(sinT))


def build_p2a(S):
    nb = S // 256
    NT = S // 128
    RC = min(S, 2048)
    nc = bass.Bass("TRN2", target_bir_lowering=False)
    di = lambda n, s, dt=F32: nc.dram_tensor(n, s, dt, kind="ExternalInput").ap()
    qk = di("qk", [4, 128, S], BF16)
    qks = di("qks", [4, 128, S], BF16)
    v = di("v", [2, S, 128], BF16)
    cosT = di("cosT", [128, S])
    sinT = di("sinT", [128, S])
    pastb = di("pastb", [128, nb * nb])
    past01 = di("past01", [128, nb * nb])
    own01 = di("own01", [128, nb * nb])
    cm = di("cm", [128, 512], BF16)
    onehot = di("onehot", [32, nb * 128], BF16)
    ident_f = di("ident_f", [128, 128])
    ident_b = di("ident_b", [128, 128], BF16)
    o_ya = nc.dram_tensor("o_ya", [256, S], BF16, kind="ExternalOutput").ap()
    scale = 128.0 ** -0.5
    with contextlib.ExitStack() as st:
        P = Prog(nc, st)
        t_c = Trk("consts")

        def cload(ap, shape, dt):
            t = P.sbuf(shape, dt)
            P.dma("sp", t[:, :], ap[:, :], writes=(t_c,), key="c0")
            return t
        pastb_sb = cload(pastb, [128, nb * nb], F32)
        past01_sb = cload(past01, [128, nb * nb], F32)
        own01_sb = cload(own01, [128, nb * nb], F32)
        cm_sb = cload(cm, [128, 512], BF16)
        oh_sb = cload(onehot, [32, nb * 128], BF16)
        idf_sb = cload(ident_f, [128, 128], F32)
        idb_sb = cload(ident_b, [128, 128], BF16)
        ones_b = P.sbuf([128, 128], BF16)
        P.op("pool", lambda e: e.memset(ones_b[:, :], 1.0), writes=(t_c,))

        QR = P.sbuf([128, S], BF16); t_QR = Trk()
        KR = P.sbuf([128, S], BF16); t_KR = Trk()
        V = P.sbuf([128, NT, 128], BF16); t_V = Trk()
        maskbT = P.sbuf([32, S], BF16); t_mb = Trk()
        outb = P.sbuf([128, S], BF16); t_ob = Trk()
        raw_rot = Rot([P.sbuf([128, 2, RC], BF16) for _ in range(2)])
        cs_sb = P.sbuf([128, 2, RC], F32); t_cs = Trk()
        r1_rot = Rot([P.sbuf([128, RC], F32) for _ in range(1)])
        r2_rot = Rot([P.sbuf([128, RC], F32) for _ in range(1)])
        kmf = P.sbuf([128, nb], F32); t_kmf = Trk()
        kmT = P.sbuf([128, nb], BF16); t_km = Trk()
        gm_rot = Rot([P.sbuf([128, nb], F32) for _ in range(2)])
        m8_rot = Rot([P.sbuf([128, 8], F32) for _ in range(2)])
        sel_rot = Rot([P.sbuf([128, 32], F32) for _ in range(2)])
        pT_rot = Rot([P.sbuf([128, 256], BF16) for _ in range(4)])
        rs_rot = Rot([P.sbuf([128, 256], F32) for _ in range(2)])
        ps_s_rot = Rot([P.psum([128, 512], F32) for _ in range(3)])
        ps_o_rot = Rot([P.psum([128, 512], F32) for _ in range(2)])
        ps_m_rot = Rot([P.psum([128, 512], F32) for _ in range(2)])
        ps_g = P.psum([128, 512], F32); t_psg = Trk()
        t_out = Trk("out")
        if nb < 32:
            for b in sel_rot.bufs:
                P.op("pool", (lambda e, b=b: e.memset(b[:, :], 0.0)), writes=(t_c,))

        for h in range(2):
            for which, dst, t_dst in ((0, QR, t_QR), (1, KR, t_KR)):
                src = which * 2 + h
                for c0 in range(0, S, RC):
                    raw, t_raw = raw_rot.next()
                    kk = f"raw{raw_rot.i}"
                    P.dma("sp", raw[:, 0, :], qk[src, :, c0:c0 + RC], writes=(t_raw,), key=kk)
                    P.dma("sp", raw[:, 1, :], qks[src, :, c0:c0 + RC], writes=(t_raw,), key=kk)
                    P.dma("sp", cs_sb[:, 0, :], cosT[:, c0:c0 + RC], writes=(t_cs,), key="cs")
                    P.dma("sp", cs_sb[:, 1, :], sinT[:, c0:c0 + RC], writes=(t_cs,), key="cs")
                    r1, t_r1 = r1_rot.next()
                    r2, t_r2 = r2_rot.next()
                    P.op("dve", (lambda e, raw=raw, r1=r1: e.tensor_tensor(out=r1[:, :], in0=raw[:, 0, :], in1=cs_sb[:, 0, :], op=ALU.mult)),
                         reads=(t_raw, t_cs), writes=(t_r1,))
                    P.op("pool", (lambda e, raw=raw, r2=r2: e.tensor_tensor(out=r2[:, :], in0=raw[:, 1, :], in1=cs_sb[:, 1, :], op=ALU.mult)),
                         reads=(t_raw, t_cs), writes=(t_r2,))
                    P.op("dve", (lambda e, r1=r1, r2=r2, dst=dst, c0=c0: e.tensor_tensor(out=dst[:, c0:c0 + RC], in0=r1[:, :], in1=r2[:, :],
                                                                                        op=ALU.add)),
                         reads=(t_r1, t_r2), writes=(t_dst,))
            P.dma("sp", V[:, :, :], v[h].rearrange("(n p) d -> p n d", p=128), writes=(t_V,), key="v")
            P.op("dve", (lambda e: e.tensor_reduce(out=kmf[:, :], in_=KR[:, :].rearrange("p (n k) -> p n k", k=256), axis=AX.X, op=ALU.add)),
                 reads=(t_KR,), writes=(t_kmf,))
            P.op("act", (lambda e: e.activation(out=kmT[:, :], in_=kmf[:, :], func=AF.Copy, scale=1.0 / 256.0)),
                 reads=(t_kmf,), writes=(t_km,))
            for qt in range(NT):
                n = qt // 2
                P.op("pe", (lambda e, qt=qt: e.matmul(ps_g[:, :nb], QR[:, qt * 128:(qt + 1) * 128], kmT[:, :], start=True, stop=True)),
                     reads=(t_QR, t_km), writes=(t_psg,))
                gm, t_gm = gm_rot.next()
                P.op("dve", (lambda e, gm=gm, n=n: e.tensor_tensor(out=gm[:, :], in0=ps_g[:, :nb], in1=pastb_sb[:, n * nb:(n + 1) * nb],
                                                                   op=ALU.add)), reads=(t_psg, t_c), writes=(t_gm,))
                m8, t_m8 = m8_rot.next()
                if nb >= 8:
                    P.op("dve", (lambda e, gm=gm, m8=m8: e.max(out=m8[:, :], in_=gm[:, :])), reads=(t_gm,), writes=(t_m8,))
                else:
                    raise NotImplementedError
                sel, t_sel = sel_rot.next()
                P.op("dve", (lambda e, gm=gm, m8=m8, sel=sel: e.tensor_scalar(out=sel[:, :nb], in0=gm[:, :], scalar1=m8[:, 2:3], scalar2=None,
                                                                              op0=ALU.is_ge)), reads=(t_gm, t_m8), writes=(t_sel,))
                P.op("dve", (lambda e, sel=sel, n=n: e.tensor_tensor(out=sel[:, :nb], in0=sel[:, :nb], in1=past01_sb[:, n * nb:(n + 1) * nb],
                                                                     op=ALU.mult)), reads=(t_sel, t_c), writes=(t_sel,))
                P.op("dve", (lambda e, sel=sel, n=n: e.tensor_tensor(out=sel[:, :nb], in0=sel[:, :nb], in1=own01_sb[:, n * nb:(n + 1) * nb],
                                                                     op=ALU.add)), reads=(t_sel, t_c), writes=(t_sel,))
                P.op("dve", (lambda e, sel=sel: e.tensor_scalar(out=sel[:, :nb], in0=sel[:, :nb], scalar1=BIGR, scalar2=-BIGR,
                                                                op0=ALU.mult, op1=ALU.add)), reads=(t_sel,), writes=(t_sel,))
                P.op("pe", (lambda e, sel=sel: e.transpose(ps_g[:32, 256:384], sel[:, :], idf_sb[:, :])),
                     reads=(t_sel, t_c, t_gm), writes=(t_psg,))
                P.op("act", (lambda e, qt=qt: e.activation(out=maskbT[:, qt * 128:(qt + 1) * 128], in_=ps_g[:32, 256:384], func=AF.Copy)),
                     reads=(t_psg,), writes=(t_mb,))
            for n in range(nb):
                qs = slice(n * 256, (n + 1) * 256)
                ps_o, t_po = ps_o_rot.next()
                ps_m, t_pm = ps_m_rot.next()
                nkt = 2 * n + 2
                LA = 2
                sc_tiles = {}

                def emit_score(kt, n=n, qs=qs):
                    j = kt // 2
                    ps_s, t_pss = ps_s_rot.next()
                    pairs = [(KR[:, kt * 128:(kt + 1) * 128], QR[:, qs]),
                             (oh_sb[:, j * 128:(j + 1) * 128], maskbT[:, qs])]
                    if j == n:
                        pairs.append((idb_sb[:, :], cm_sb[:, (kt % 2) * 256:(kt % 2 + 1) * 256]))
                    mm_group(P, ps_s[:, :256], t_pss, pairs, reads=(t_KR, t_QR, t_mb, t_c))
                    sc_tiles[kt] = (ps_s, t_pss)
                for kt in range(min(LA, nkt)):
                    emit_score(kt)
                for kt in range(nkt):
                    ps_s, t_pss = sc_tiles.pop(kt)
                    pT, t_pT = pT_rot.next()
                    P.op("act", (lambda e, ps_s=ps_s, pT=pT: e.activation(out=pT[:, :], in_=ps_s[:, :256], func=AF.Exp, scale=scale)),
                         reads=(t_pss,), writes=(t_pT,))
                    if kt + LA < nkt:
                        emit_score(kt + LA)

                    def mm2(e, kt=kt, pT=pT, ps_o=ps_o, ps_m=ps_m, nkt=nkt):
                        e.matmul(ps_o[:, :256], V[:, kt, :], pT[:, :], start=(kt == 0), stop=(kt == nkt - 1))
                        return e.matmul(ps_m[:, :256], ones_b[:, :], pT[:, :], start=(kt == 0), stop=(kt == nkt - 1))
                    P.op("pe", mm2, reads=(t_V, t_pT, t_c), writes=(t_po, t_pm))
                rs, t_rs = rs_rot.next()
                P.op("dve", (lambda e, rs=rs, ps_m=ps_m: e.reciprocal(out=rs[:, :], in_=ps_m[:, :256])), reads=(t_pm,), writes=(t_rs,))
                P.op("dve", (lambda e, rs=rs, ps_o=ps_o, qs=qs: e.tensor_tensor(out=outb[:, qs], in0=ps_o[:, :256], in1=rs[:, :], op=ALU.mult)),
                     reads=(t_po, t_rs), writes=(t_ob,))
            P.dma("pool", o_ya[h * 128:(h + 1) * 128, :], outb[:, :], reads=(t_ob,), writes=(t_out,), key="oya")
        P.finish((t_out,))
        P.emit()
    return nc


def ssd_tables():
    oh = np.zeros((8, 8, 128), np.float32)
    for h in range(8):
        oh[h, h, :] = 1.0
    s = np.arange(128)[:, None]
    l = np.arange(128)[None, :]
    tri = (l >= s).astype(np.float32)
    return dict(oh8=np.ascontiguousarray(oh.reshape(8, 1024)), tri=tri, ident_f=np.eye(128, dtype=np.float32),
                ident_b=np.eye(128, dtype=np.float32).astype(NPBF))


def build_p2b(S, SC=1024, dbg=99):
    nc = bass.Bass("TRN2", target_bir_lowering=False)
    di = lambda n, s, dt=F32: nc.dram_tensor(n, s, dt, kind="ExternalInput").ap()
    xbc = di("xbc", [768, S + 4], BF16)
    h_in = di("h_in", [128, 512])
    zT = di("zT", [512, S], BF16)
    dtT = di("dtT", [8, S])
    cw = di("cw", [128, 24])
    cb = di("cb", [128, 6])
    dtb = di("dtb", [128, 2])
    alog = di("alog", [128, 2])
    dsk = di("dsk", [128, 4])
    nw = di("nw", [128, 4])
    oh8 = di("oh8", [8, 1024])
    tri = di("tri", [128, 128])
    ident_f = di("ident_f", [128, 128])
    ident_b = di("ident_b", [128, 128], BF16)
    o_ys = nc.dram_tensor("o_ys", [512, S], BF16, kind="ExternalOutput").ap()
    o_h = nc.dram_tensor("o_h", [128, 512], F32, kind="ExternalOutput").ap()
    SC = min(SC, S)
    NSC = S // SC
    NCH = SC // 128
    with contextlib.ExitStack() as st:
        P = Prog(nc, st)
        t_c = Trk("consts")

        def cload(ap, shape, dt):
            t = P.sbuf(shape, dt)
            P.dma("sp", t[:, :], ap[:, :], writes=(t_c,), key="c0")
            return t
        cw_sb = cload(cw, [128, 24], F32)
        cb_sb = cload(cb, [128, 6], F32)
        dtb_sb = cload(dtb, [128, 2], F32)
        alog_sb = cload(alog, [128, 2], F32)
        dsk_sb = cload(dsk, [128, 4], F32)
        nw_sb = cload(nw, [128, 4], F32)
        oh8_sb = cload(oh8, [8, 1024], F32)
        tri_sb = cload(tri, [128, 128], F32)
        idf_sb = cload(ident_f, [128, 128], F32)
        idb_sb = cload(ident_b, [128, 128], BF16)
        ones_f = P.sbuf([128, 128], F32)
        P.op("pool", lambda e: e.memset(ones_f[:, :], 1.0), writes=(t_c,))
        eps_sb = P.sbuf([128, 1], F32)
        P.op("pool", lambda e: e.memset(eps_sb[:, :], EPS), writes=(t_c,))
        one_sb = P.sbuf([128, 1], F32)
        P.op("pool", lambda e: e.memset(one_sb[:, :], 1.0), writes=(t_c,))
        zero_sb = P.sbuf([128, 1], F32)
        P.op("pool", lambda e: e.memset(zero_sb[:, :], 0.0), writes=(t_c,))
        A_sb = P.sbuf([128, 2], F32)
        P.op("act", lambda e: e.activation(out=A_sb[:, :], in_=alog_sb[:, :], func=AF.Exp), reads=(t_c,), writes=(t_c,))
        P.op("dve", lambda e: e.tensor_scalar(out=A_sb[:, :], in0=A_sb[:, :], scalar1=-1.0, scalar2=None, op0=ALU.mult),
             reads=(t_c,), writes=(t_c,))

        raw = P.sbuf([128, 6, SC + 4], BF16); t_raw = Trk()
        zr = P.sbuf([128, 4, SC], BF16); t_zr = Trk()
        dtr = P.sbuf([8, SC], F32); t_dtr = Trk()
        xc = P.sbuf([128, 6, SC], BF16); t_xc = Trk()
        sz = P.sbuf([128, 4, SC], BF16); t_sz = Trk()
        dts = P.sbuf([8, SC], F32); t_dts = Trk()
        aT = P.sbuf([8, SC], F32); t_aT = Trk()
        ct_rot = Rot([P.sbuf([128, SC], F32) for _ in range(2)])
        acs_rot = Rot([P.sbuf([8, 128], F32) for _ in range(2)])
        datm_rot = Rot([P.sbuf([128, 16], F32) for _ in range(2)])
        E_rot = Rot([P.sbuf([128, 8, 128], F32) for _ in range(2)])
        Dp_rot = Rot([P.sbuf([128, 8, 128], F32) for _ in range(2)])
        cbm_rot = Rot([P.sbuf([128, 128], F32) for _ in range(2)])
        MT_rot = Rot([P.sbuf([128, 8, 128], BF16) for _ in range(2)])
        ChT_rot = Rot([P.sbuf([128, 8, 128], BF16) for _ in range(2)])
        X_rot = Rot([P.sbuf([128, 512], BF16) for _ in range(2)])
        Xd_rot = Rot([P.sbuf([128, 512], BF16) for _ in range(2)])
        Btm_rot = Rot([P.sbuf([128, 128], BF16) for _ in range(2)])
        yg_rot = Rot([P.sbuf([128, 4, 128], F32) for _ in range(2)])
        sq_rot = Rot([P.sbuf([128, 128], F32) for _ in range(3)])
        rstd_rot = Rot([P.sbuf([128, 128], F32) for _ in range(2)])
        H = P.sbuf([128, 512], F32); t_H = Trk()
        Hbf = P.sbuf([128, 512], BF16); t_Hbf = Trk()
        yo_rot = Rot([P.sbuf([128, 4, SC], BF16) for _ in range(2)])
        ps_bc = [P.psum([128, 512], F32) for _ in range(2)]; t_bc = Trk()
        ps_tr = P.psum([128, 1024], BF16); t_tr = Trk()
        ps_sm = P.psum([128, 512], F32); t_sm = Trk()
        ps_y_rot = Rot([P.psum([128, 512], F32) for _ in range(2)])
        ps_st = P.psum([128, 512], F32); t_st = Trk()
        ps_n = P.psum([128, 512], F32); t_n = Trk()
        t_out = Trk("out")
        xv = xbc.rearrange("(c p) t -> p c t", p=128)
        zv = zT.rearrange("(c p) t -> p c t", p=128)
        ov = o_ys.rearrange("(c p) t -> p c t", p=128)

        first = False
        P.dma("sp", H[:, :], h_in[:, :], writes=(t_H,), key="hin")
        P.op("act", (lambda e: e.activation(out=Hbf[:, :], in_=H[:, :], func=AF.Copy)), reads=(t_H,), writes=(t_Hbf,))
        for sc in range(NSC):
            t0 = sc * SC
            P.dma("sp", raw[:, :, :], xv[:, :, t0:t0 + SC + 4], writes=(t_raw,), key="raw")
            P.dma("sp", zr[:, :, :], zv[:, :, t0:t0 + SC], writes=(t_zr,), key="zr")
            P.dma("sp", dtr[:, :], dtT[:, t0:t0 + SC], writes=(t_dtr,), key="dtr")
            for ch in range(6):
                ct, t_ct = ct_rot.next()
                P.op("pool", (lambda e, ct=ct, ch=ch: e.tensor_scalar(out=ct[:, :], in0=raw[:, ch, 1:1 + SC], scalar1=cw_sb[:, ch * 4:ch * 4 + 1],
                                                                       scalar2=cb_sb[:, ch:ch + 1], op0=ALU.mult, op1=ALU.add)),
                     reads=(t_raw, t_c), writes=(t_ct,))
                for k in (1, 2, 3):
                    P.op("dve", (lambda e, ct=ct, ch=ch, k=k: e.scalar_tensor_tensor(
                        out=ct[:, :], in0=raw[:, ch, 1 + k:1 + k + SC], scalar=cw_sb[:, ch * 4 + k:ch * 4 + k + 1], in1=ct[:, :],
                        op0=ALU.mult, op1=ALU.add)), reads=(t_raw, t_c, t_ct), writes=(t_ct,))
                P.op("act", (lambda e, ct=ct, ch=ch: e.activation(out=xc[:, ch, :], in_=ct[:, :], func=AF.Silu)),
                     reads=(t_ct,), writes=(t_xc,))
            for pc in range(4):
                P.op("act", (lambda e, pc=pc: e.activation(out=sz[:, pc, :], in_=zr[:, pc, :], func=AF.Silu)),
                     reads=(t_zr,), writes=(t_sz,))
            P.op("act", (lambda e: e.activation(out=dts[:, :], in_=dtr[:, :], func=AF.Exp, bias=dtb_sb[:8, 0:1])),
                 reads=(t_dtr, t_c), writes=(t_dts,))
            P.op("act", (lambda e: e.activation(out=dts[:, :], in_=dts[:, :], func=AF.Ln, bias=one_sb[:8, 0:1])),
                 reads=(t_dts, t_c), writes=(t_dts,))
            P.op("dve", (lambda e: e.tensor_scalar(out=aT[:, :], in0=dts[:, :], scalar1=A_sb[:8, 0:1], scalar2=None, op0=ALU.mult)),
                 reads=(t_dts, t_c), writes=(t_aT,))
            yo, t_yo = yo_rot.next()
            for c in range(NCH):
                o = c * 128
                cs = slice(o, o + 128)
                if dbg <= 1:
                    continue
                acs, t_acs = acs_rot.next()
                P.op("dve", (lambda e, acs=acs, cs=cs: e.tensor_tensor_scan(out=acs[:, :], data0=ones_f[:8, :], data1=aT[:, cs],
                                                                           initial=0.0, op0=ALU.mult, op1=ALU.add)),
                     reads=(t_aT, t_c), writes=(t_acs,))
                if dbg <= 2:
                    continue
                def trs(e, acs=acs, cs=cs):
                    e.transpose(ps_sm[:, 0:8], dts[:, cs], idf_sb[:8, :8])
                    return e.transpose(ps_sm[:, 8:16], acs[:, :], idf_sb[:8, :8])
                P.op("pe", trs, reads=(t_dts, t_acs, t_c), writes=(t_sm,))
                datm, t_datm = datm_rot.next()
                P.op("act", (lambda e, datm=datm: e.activation(out=datm[:, :], in_=ps_sm[:, 0:16], func=AF.Copy)),
                     reads=(t_sm,), writes=(t_datm,))
                if dbg <= 3:
                    continue
                def bcs(e, acs=acs):
                    for hh in range(8):
                        ins = e.matmul(ps_bc[hh // 4][:, (hh % 4) * 128:(hh % 4 + 1) * 128], oh8_sb[:, hh * 128:(hh + 1) * 128],
                                       acs[:, :], start=True, stop=True)
                    return ins
                P.op("pe", bcs, reads=(t_acs, t_c), writes=(t_bc,))
                if dbg <= 4:
                    continue
                E, t_E = E_rot.next()
                for hh in range(8):
                    P.op("act", (lambda e, E=E, hh=hh: e.activation(out=E[:, hh, :],
                                                                    in_=ps_bc[hh // 4][:, (hh % 4) * 128:(hh % 4 + 1) * 128], func=AF.Exp)),
                         reads=(t_bc,), writes=(t_E,))
                Dp, t_Dp = Dp_rot.next()
                for hh in range(8):
                    P.op("dve", (lambda e, Dp=Dp, hh=hh, datm=datm: e.tensor_scalar(
                        out=Dp[:, hh, :], in0=ps_bc[hh // 4][:, (hh % 4) * 128:(hh % 4 + 1) * 128], scalar1=datm[:, 8 + hh:9 + hh],
                        scalar2=None, op0=ALU.subtract)), reads=(t_bc, t_datm, t_c), writes=(t_Dp,))
                    P.op("dve", (lambda e, Dp=Dp, hh=hh: e.tensor_scalar(out=Dp[:, hh, :], in0=Dp[:, hh, :], scalar1=0.0, scalar2=None,
                                                                        op0=ALU.min)), reads=(t_Dp,), writes=(t_Dp,))
                    P.op("act", (lambda e, Dp=Dp, hh=hh: e.activation(out=Dp[:, hh, :], in_=Dp[:, hh, :], func=AF.Exp)),
                         reads=(t_Dp,), writes=(t_Dp,))
                if dbg <= 5:
                    continue
                P.op("pe", (lambda e, cs=cs: e.matmul(ps_sm[:, 128:256], xc[:, 4, cs], xc[:, 5, cs], start=True, stop=True)),
                     reads=(t_xc, t_datm), writes=(t_sm,))
                cbm, t_cbm = cbm_rot.next()
                P.op("dve", (lambda e, cbm=cbm: e.tensor_tensor(out=cbm[:, :], in0=ps_sm[:, 128:256], in1=tri_sb[:, :], op=ALU.mult)),
                     reads=(t_sm, t_c), writes=(t_cbm,))
                if dbg <= 6:
                    continue
                MT, t_MT = MT_rot.next()
                ChT, t_ChT = ChT_rot.next()
                for hh in range(8):
                    P.op("pool", (lambda e, MT=MT, Dp=Dp, cbm=cbm, hh=hh: e.tensor_tensor(out=MT[:, hh, :], in0=Dp[:, hh, :], in1=cbm[:, :],
                                                                                         op=ALU.mult)),
                         reads=(t_Dp, t_cbm), writes=(t_MT,))
                    P.op("pool", (lambda e, ChT=ChT, E=E, hh=hh, cs=cs: e.tensor_tensor(out=ChT[:, hh, :], in0=E[:, hh, :], in1=xc[:, 5, cs],
                                                                                       op=ALU.mult)),
                         reads=(t_E, t_xc), writes=(t_ChT,))
                if dbg <= 7:
                    continue
                def trx(e, cs=cs):
                    for pc in range(4):
                        e.transpose(ps_tr[:, pc * 128:(pc + 1) * 128], xc[:, pc, cs], idb_sb[:, :])
                    return e.transpose(ps_tr[:, 512:640], xc[:, 4, cs], idb_sb[:, :])
                P.op("pe", trx, reads=(t_xc, t_c), writes=(t_tr,))
                X, t_X = X_rot.next()
                Xd, t_Xd = Xd_rot.next()
                Btm, t_Btm = Btm_rot.next()
                P.op("act", (lambda e, Btm=Btm: e.activation(out=Btm[:, :], in_=ps_tr[:, 512:640], func=AF.Copy)),
                     reads=(t_tr,), writes=(t_Btm,))
                for hh in range(8):
                    hs = slice(hh * 64, (hh + 1) * 64)
                    P.op("dve", (lambda e, X=X, hs=hs, hh=hh, datm=datm: e.tensor_scalar(
                        out=X[:, hs], in0=ps_tr[:, hs], scalar1=datm[:, hh:hh + 1], scalar2=None, op0=ALU.mult)),
                        reads=(t_tr, t_datm), writes=(t_X,))
                    P.op("dve", (lambda e, Xd=Xd, hs=hs, hh=hh, datm=datm, Dp=Dp: e.tensor_scalar(
                        out=Xd[:, hs], in0=ps_tr[:, hs], scalar1=datm[:, hh:hh + 1], scalar2=Dp[:, hh, 127:128],
                        op0=ALU.mult, op1=ALU.mult)), reads=(t_tr, t_datm, t_Dp), writes=(t_Xd,))
                if dbg <= 8:
                    continue
                ps_y, t_y = ps_y_rot.next()

                def ymm(e, X=X, MT=MT, ChT=ChT, ps_y=ps_y, first=first):
                    for hh in range(8):
                        hp, ee = hh // 2, hh % 2
                        out = ps_y[ee * 64:(ee + 1) * 64, hp * 128:(hp + 1) * 128]
                        ins = e.matmul(out, X[:, hh * 64:(hh + 1) * 64], MT[:, hh, :], start=True, stop=first)
                        if not first:
                            ins = e.matmul(out, Hbf[:, hh * 64:(hh + 1) * 64], ChT[:, hh, :], start=False, stop=True)
                    return ins
                P.op("pe", ymm, reads=(t_X, t_MT, t_ChT, t_Hbf), writes=(t_y,))
                if dbg <= 9:
                    continue
                P.op("pe", (lambda e, Btm=Btm, Xd=Xd: e.matmul(ps_st[:, :], Btm[:, :], Xd[:, :], start=True, stop=True)),
                     reads=(t_Btm, t_Xd), writes=(t_st,))
                for hh in range(8):
                    hs = slice(hh * 64, (hh + 1) * 64)
                    if first:
                        P.op("dve", (lambda e, hs=hs: e.tensor_copy(out=H[:, hs], in_=ps_st[:, hs])), reads=(t_st, t_y), writes=(t_H,))
                    else:
                        P.op("dve", (lambda e, hs=hs, hh=hh, E=E: e.scalar_tensor_tensor(
                            out=H[:, hs], in0=H[:, hs], scalar=E[:, hh, 127:128], in1=ps_st[:, hs], op0=ALU.mult, op1=ALU.add)),
                            reads=(t_st, t_E, t_H, t_y), writes=(t_H,))
                P.op("act", (lambda e: e.activation(out=Hbf[:, :], in_=H[:, :], func=AF.Copy)), reads=(t_H, t_y), writes=(t_Hbf,))
                if dbg <= 10:
                    continue
                yg, t_yg = yg_rot.next()
                for hp in range(4):
                    P.op("dve", (lambda e, yg=yg, hp=hp, cs=cs, ps_y=ps_y: e.scalar_tensor_tensor(
                        out=yg[:, hp, :], in0=xc[:, hp, cs], scalar=dsk_sb[:, hp:hp + 1], in1=ps_y[:, hp * 128:(hp + 1) * 128],
                        op0=ALU.mult, op1=ALU.add)), reads=(t_xc, t_y, t_c), writes=(t_yg,))
                    P.op("pool", (lambda e, yg=yg, hp=hp, cs=cs: e.tensor_tensor(out=yg[:, hp, :], in0=yg[:, hp, :], in1=sz[:, hp, cs], op=ALU.mult)),
                         reads=(t_yg, t_sz), writes=(t_yg,))
                for hp in range(4):
                    sq, t_sq = sq_rot.next()
                    P.op("pool", (lambda e, sq=sq, yg=yg, hp=hp: e.tensor_tensor(out=sq[:, :], in0=yg[:, hp, :], in1=yg[:, hp, :], op=ALU.mult)),
                         reads=(t_yg,), writes=(t_sq,))
                    P.op("pe", (lambda e, sq=sq, hp=hp: e.matmul(ps_n[:, :128], ones_f[:, :], sq[:, :], start=(hp == 0), stop=(hp == 3))),
                         reads=(t_sq, t_c), writes=(t_n,))
                rstd, t_rstd = rstd_rot.next()
                P.op("act", (lambda e, rstd=rstd: e.activation(out=rstd[:, :], in_=ps_n[:, :128], func=AF.Ln, bias=eps_sb[:, 0:1], scale=1.0 / 512.0)),
                     reads=(t_n, t_c), writes=(t_rstd,))
                P.op("act", (lambda e, rstd=rstd: e.activation(out=rstd[:, :], in_=rstd[:, :], func=AF.Exp, scale=-0.5)),
                     reads=(t_rstd,), writes=(t_rstd,))
                for hp in range(4):
                    P.op("dve", (lambda e, yg=yg, hp=hp, cs=cs, rstd=rstd, yo=yo: e.scalar_tensor_tensor(
                        out=yo[:, hp, cs], in0=yg[:, hp, :], scalar=nw_sb[:, hp:hp + 1], in1=rstd[:, :], op0=ALU.mult, op1=ALU.mult)),
                        reads=(t_yg, t_rstd, t_c), writes=(t_yo,))
                first = False
            P.dma("pool", ov[:, :, t0:t0 + SC], yo[:, :, :], reads=(t_yo,), writes=(t_out,), key=f"oys{yo_rot.i}")
        P.dma("pool", o_h[:, :], H[:, :], reads=(t_H,), writes=(t_out,), key="oh")
        P.finish((t_out,))
        P.emit()
    return nc


_PROGS = {}


def _prog(name, fn):
    if name not in _PROGS:
        _PROGS[name] = fn()
    return _PROGS[name]


def _lay(v):
    return np.ascontiguousarray(np.asarray(v, np.float32).reshape(-1, 128).T)


def _run(nc, in_maps):
    res = run_bass_kernel_spmd(nc, in_maps, core_ids=list(range(NCORES)))
    return res.results


def kernel(x, mem, norm_mix_pre, norm_mix_post, norm_ffn_pre, norm_ffn_post, norm_mem,
           w_in, conv_ssd_w, conv_ssd_b, dt_bias, a_log, d_skip, ssd_norm, w_mem_kv,
           w_br_attn, w_br_ssd, w_br_mem, w_out, w_up, conv_ffn_w, conv_ffn_b, w_down):
    f32 = lambda a: np.asarray(a, np.float32)
    x = f32(x)
    mem = f32(mem)
    B, S, _ = x.shape
    T = S * B // NCORES
    J = S // T
    depth = w_in.shape[0]
    p1 = _prog("p1", lambda: build_p1(T))
    p2a = _prog("p2a", lambda: build_p2a(S))
    p2b = _prog("p2b", lambda: build_p2b(2048, SC=512))
    p3a = _prog("p3a", lambda: build_p3a(T))
    p3b = _prog("p3b", lambda: build_p3b(T))
    atab = attn_tables(S)
    stab = ssd_tables()
    cores = [(b, j) for b in range(B) for j in range(J)]
    xT = [np.ascontiguousarray(x[b, j * T:(j + 1) * T, :].T) for (b, j) in cores]
    memT = [np.ascontiguousarray(mem[b].T) for b in range(B)]
    pad8 = lambda v: np.ascontiguousarray(np.broadcast_to(np.pad(f32(v), (0, 120))[:, None], (128, 2)))
    for l in range(depth):
        w_in_l = f32(w_in[l])
        nw = _lay(norm_mix_pre[l])
        r1 = _run(p1, [dict(xT=xT[c], nw=nw, w_in=w_in_l) for c in range(NCORES)])
        del w_in_l
        full = lambda key, b: np.concatenate([r1[b * J + j][key] for j in range(J)], axis=1)
        qk_f = [full("o_qk", b) for b in range(B)]
        z_f = [full("o_z", b) for b in range(B)]
        xbc_f = [full("o_xbc", b) for b in range(B)]
        dt_f = [full("o_dt", b) for b in range(B)]
        v_f = [np.concatenate([r1[b * J + j]["o_v"] for j in range(J)], axis=0) for b in range(B)]
        maps = []
        for (b, g) in cores:
            hs = (2 * g, 2 * g + 1)
            qk = np.stack([qk_f[b][h * 128:(h + 1) * 128] for h in hs] + [qk_f[b][1024 + h * 128:1024 + (h + 1) * 128] for h in hs])
            qks = np.ascontiguousarray(np.concatenate([qk[:, 64:], qk[:, :64]], axis=1))
            v = np.ascontiguousarray(np.stack([v_f[b][:, h * 128:(h + 1) * 128] for h in hs]))
            maps.append(dict(qk=np.ascontiguousarray(qk), qks=qks, v=v, **atab))
        r2a = _run(p2a, maps)
        cw_l = f32(conv_ssd_w[l])
        cb_l = f32(conv_ssd_b[l])
        SEG = 2048
        hst = [np.zeros((128, 512), np.float32) for _ in range(NCORES)]
        ys_parts = [[] for _ in range(NCORES)]
        for sg in range(S // SEG):
            maps = []
            for c, (b, g) in enumerate(cores):
                ch = np.concatenate([np.arange(g * 512, (g + 1) * 512), 2048 + np.arange(g * 128, (g + 1) * 128),
                                     2560 + np.arange(g * 128, (g + 1) * 128)])
                cw = np.ascontiguousarray(cw_l[:, ch].T.reshape(6, 128, 4).transpose(1, 0, 2).reshape(128, 24))
                seg = xbc_f[b][ch][:, sg * SEG:(sg + 1) * SEG]
                halo = xbc_f[b][ch][:, sg * SEG - 4:sg * SEG] if sg > 0 else np.zeros((768, 4), seg.dtype)
                maps.append(dict(xbc=np.ascontiguousarray(np.concatenate([halo, seg], axis=1)), h_in=hst[c],
                                 zT=np.ascontiguousarray(z_f[b][g * 512:(g + 1) * 512, sg * SEG:(sg + 1) * SEG]),
                                 dtT=np.ascontiguousarray(dt_f[b][g * 8:(g + 1) * 8, sg * SEG:(sg + 1) * SEG]), cw=cw, cb=_lay(cb_l[ch]),
                                 dtb=pad8(dt_bias[l][g * 8:(g + 1) * 8]), alog=pad8(a_log[l][g * 8:(g + 1) * 8]),
                                 dsk=_lay(np.repeat(f32(d_skip[l][g * 8:(g + 1) * 8]), 64)),
                                 nw=_lay(f32(ssd_norm[l])[g * 512:(g + 1) * 512]), **stab))
            rr = _run(p2b, maps)
            for c in range(NCORES):
                hst[c] = np.ascontiguousarray(rr[c]["o_h"])
                ys_parts[c].append(rr[c]["o_ys"])
        r2b = [dict(o_ys=np.concatenate(ys_parts[c], axis=1)) for c in range(NCORES)]
        ya_f = [np.concatenate([r2a[b * J + g]["o_ya"] for g in range(J)], axis=0) for b in range(B)]
        ys_f = [np.concatenate([r2b[b * J + g]["o_ys"] for g in range(J)], axis=0) for b in range(B)]
        del qk_f, z_f, xbc_f, dt_f, v_f
        wts = dict(w_kv=f32(w_mem_kv[l]), w_ba=f32(w_br_attn[l]), w_bs=f32(w_br_ssd[l]), w_bm=f32(w_br_mem[l]), w_o=f32(w_out[l]),
                   nw_mem=_lay(norm_mem[l]), nw_post=_lay(norm_mix_post[l]), nw_pre=_lay(norm_ffn_pre[l]))
        maps = []
        for c, (b, j) in enumerate(cores):
            ts = slice(j * T, (j + 1) * T)
            maps.append(dict(xT=xT[c], yaT=np.ascontiguousarray(ya_f[b][:, ts]), ysT=np.ascontiguousarray(ys_f[b][:, ts]),
                             qmT=r1[c]["o_qm"], gT=r1[c]["o_g"], memT=memT[b], **wts))
        r3a = _run(p3a, maps)
        del r1, r2a, r2b, ya_f, ys_f, wts, maps
        cwf = f32(conv_ffn_w[l])
        cw = np.ascontiguousarray(cwf.T.reshape(2 * NF, 128, 3).transpose(1, 0, 2).reshape(128, 2 * NF * 3))
        wts = dict(w_up=f32(w_up[l]), w_dn=f32(w_down[l]), cw=cw, cb=_lay(conv_ffn_b[l]), nw=_lay(norm_ffn_post[l]))
        maps = []
        for c, (b, j) in enumerate(cores):
            h2 = r3a[c]["o_h2"]
            halo = r3a[c - 1]["o_h2"][:, -2:] if j > 0 else np.zeros((D, 2), h2.dtype)
            maps.append(dict(h2T=np.ascontiguousarray(np.concatenate([halo, h2], axis=1)), xmT=r3a[c]["o_xm"], **wts))
        r3b = _run(p3b, maps)
        xT = [np.ascontiguousarray(r3b[c]["o_x"]) for c in range(NCORES)]
        del r3a, r3b, wts, maps
    out = np.empty((B, S, D), np.float32)
    for c, (b, j) in enumerate(cores):
        out[b, j * T:(j + 1) * T, :] = xT[c].T
    return out
```

```python
import contextlib
import numpy as np
import ml_dtypes
import concourse.bass as bass
import concourse.mybir as mybir
from concourse.bass_utils import run_bass_kernel_spmd

F32 = mybir.dt.float32
BF16 = mybir.dt.bfloat16
AF = mybir.ActivationFunctionType
ALU = mybir.AluOpType
AX = mybir.AxisListType
NPBF = ml_dtypes.bfloat16

D = 2048
KC = D // 128
NCORES = 8
EPS = 1e-6
ENGS = ("pe", "act", "dve", "pool", "sp")


class Trk:
    __slots__ = ("W", "R", "name")

    def __init__(self, name=""):
        self.W = {}
        self.R = {}
        self.name = name


class Prog:
    def __init__(self, nc, stack):
        self.nc = nc
        self.stack = stack
        self.ops = {e: [] for e in ENGS}
        self.cnt = {}
        self.seen = {e: {} for e in ENGS}
        self.esem = {}
        for e in ENGS:
            if e != "sp":
                s = stack.enter_context(nc.semaphore("c_" + e))
                self.esem[e] = s
                self.cnt[id(s)] = 0
        self.dsems = {}
        self.semobj = {}
        self.nuniq = 0

    def sbuf(self, shape, dt, name=None):
        self.nuniq += 1
        return self.stack.enter_context(self.nc.sbuf_tensor(name or f"sb{self.nuniq}", list(shape), dt))

    def psum(self, shape, dt, name=None):
        self.nuniq += 1
        return self.stack.enter_context(self.nc.psum_tensor(name or f"ps{self.nuniq}", list(shape), dt))

    def dma_sem(self, key):
        if key not in self.dsems:
            s = self.stack.enter_context(self.nc.semaphore("d_" + str(key)))
            self.dsems[key] = s
            self.cnt[id(s)] = 0
        return self.dsems[key]

    def _waits(self, eng, reads, writes):
        need = {}
        objs = {}
        for t in reads:
            for k, (s, v) in t.W.items():
                if need.get(k, 0) < v:
                    need[k] = v
                    objs[k] = s
        for t in writes:
            for dct in (t.W, t.R):
                for k, (s, v) in dct.items():
                    if need.get(k, 0) < v:
                        need[k] = v
                        objs[k] = s
        out = []
        seen = self.seen[eng]
        own = id(self.esem[eng]) if eng in self.esem else None
        for k, v in need.items():
            if eng == "pe" and k == own:
                continue
            if seen.get(k, 0) >= v:
                continue
            seen[k] = v
            out.append((objs[k], v))
        return out

    def _record(self, sem, val, reads, writes):
        k = id(sem)
        for t in reads:
            t.R[k] = (sem, val)
        for t in writes:
            t.W[k] = (sem, val)

    def op(self, eng, fn, reads=(), writes=()):
        waits = self._waits(eng, reads, writes)
        sem = self.esem[eng]
        self.cnt[id(sem)] += 1
        val = self.cnt[id(sem)]
        self._record(sem, val, reads, writes)
        self.ops[eng].append((waits, fn, sem, 1))

    def dma(self, q, out, in_, reads=(), writes=(), key="d", **kw):
        waits = self._waits(q, reads, writes)
        sem = self.dma_sem(key)
        self.cnt[id(sem)] += 16
        val = self.cnt[id(sem)]
        self._record(sem, val, reads, writes)
        self.ops[q].append((waits, (lambda e: e.dma_start(out=out, in_=in_, **kw)), sem, 16))

    def finish(self, trackers, eng="sp"):
        waits = self._waits(eng, trackers, ())
        self.ops[eng].append((waits, None, None, 0))

    def emit(self):
        nc = self.nc
        allsems = list(self.esem.values()) + list(self.dsems.values())
        with nc.Block() as b0:
            def clr(e):
                for s in allsems:
                    e.sem_clear(s)
            b0.sync(clr)
        with nc.Block() as block:
            def mk(ename):
                def body(e):
                    for waits, fn, sem, inc in self.ops[ename]:
                        for (s, v) in waits:
                            e.wait_ge(s, v)
                        if fn is not None:
                            ins = fn(e)
                            ins.then_inc(sem, inc)
                return body
            block.tensor(mk("pe"))
            block.scalar(mk("act"))
            block.vector(mk("dve"))
            block.gpsimd(mk("pool"))
            block.sync(mk("sp"))


class Rot:
    def __init__(self, bufs):
        self.bufs = bufs
        self.trk = [Trk() for _ in bufs]
        self.i = -1

    def next(self):
        self.i = (self.i + 1) % len(self.bufs)
        return self.bufs[self.i], self.trk[self.i]


class CastTrk:
    def __init__(self):
        self.blocks = []

    def add(self, c0, c1, trk):
        self.blocks.append((c0, c1, trk))

    def get(self, c0=None, c1=None):
        if c0 is None:
            return tuple(t for (_, _, t) in self.blocks)
        return tuple(t for (a, b, t) in self.blocks if a < c1 and c0 < b)


def cast_w_dram(P, src, dst, rows, cols, ct=None, key="wc", rstep=512):
    ct = ct if ct is not None else CastTrk()
    for c0 in range(0, cols, 2048):
        c1 = min(cols, c0 + 2048)
        trk = Trk()
        for r0 in range(0, rows, rstep):
            r1 = min(rows, r0 + rstep)
            P.dma("pool", dst[r0:r1, c0:c1], src[r0:r1, c0:c1], writes=(trk,), key=key)
        ct.add(c0, c1, trk)
    return ct


def rms_stats(P, xt, xt_trk, ncols, ones_f, ones_f_trk, sq_rot, ps_stat, ps_trk, rstd, rstd_trk, eps_sb, nchunks=KC, dim=D):
    for kc in range(nchunks):
        sq, sqt = sq_rot.next()
        P.op("act", (lambda e, sq=sq, kc=kc: e.activation(out=sq[:, :ncols], in_=xt[:, kc, :ncols], func=AF.Square)),
             reads=(xt_trk,), writes=(sqt,))
        P.op("pe", (lambda e, sq=sq, kc=kc: e.matmul(ps_stat[:, :ncols], ones_f[:, :], sq[:, :ncols],
                                                       start=(kc == 0), stop=(kc == nchunks - 1))),
             reads=(sqt, ones_f_trk), writes=(ps_trk,))
    P.op("act", (lambda e: e.activation(out=rstd[:, :ncols], in_=ps_stat[:, :ncols], func=AF.Sqrt,
                                        bias=eps_sb[:, 0:1], scale=1.0 / dim)),
         reads=(ps_trk, ones_f_trk), writes=(rstd_trk,))
    P.op("dve", (lambda e: e.reciprocal(out=rstd[:, :ncols], in_=rstd[:, :ncols])),
         reads=(rstd_trk,), writes=(rstd_trk,))


SEG_Q, SEG_K, SEG_V, SEG_Z, SEG_XBC, SEG_DT, SEG_QM, SEG_G = 0, 1024, 2048, 3072, 5120, 8192, 8224, 9248
N_IN = 15392


def build_p1(T, ncols_total=N_IN):
    nc = bass.Bass("TRN2", target_bir_lowering=False)
    xT = nc.dram_tensor("xT", [D, T], F32, kind="ExternalInput").ap()
    nw = nc.dram_tensor("nw", [128, KC], F32, kind="ExternalInput").ap()
    w_in = nc.dram_tensor("w_in", [D, N_IN], F32, kind="ExternalInput").ap()
    o_qk = nc.dram_tensor("o_qk", [2048, T], BF16, kind="ExternalOutput").ap()
    o_v = nc.dram_tensor("o_v", [T, 1024], BF16, kind="ExternalOutput").ap()
    o_z = nc.dram_tensor("o_z", [2048, T], BF16, kind="ExternalOutput").ap()
    o_xbc = nc.dram_tensor("o_xbc", [3072, T], BF16, kind="ExternalOutput").ap()
    o_dt = nc.dram_tensor("o_dt", [32, T], F32, kind="ExternalOutput").ap()
    o_qm = nc.dram_tensor("o_qm", [1024, T], BF16, kind="ExternalOutput").ap()
    o_g = nc.dram_tensor("o_g", [6144, T], BF16, kind="ExternalOutput").ap()
    w_bf = nc.dram_tensor("w_bf", [D, N_IN], BF16).ap()
    NTG = T // 512
    with contextlib.ExitStack() as st:
        P = Prog(nc, st)
        ct_w = cast_w_dram(P, w_in, w_bf, D, N_IN)

        ones_f = P.sbuf([128, 128], F32)
        t_ones = Trk()
        P.op("pool", lambda e: e.memset(ones_f[:, :], 1.0), writes=(t_ones,))
        eps_sb = P.sbuf([128, 1], F32)
        P.op("pool", lambda e: e.memset(eps_sb[:, :], EPS), writes=(t_ones,))
        nw_sb = P.sbuf([128, KC], F32)
        t_nw = Trk()
        P.dma("sp", nw_sb[:, :], nw[:, :], writes=(t_nw,), key="c0")
        hT = P.sbuf([128, KC, T], BF16)
        t_hT = Trk()
        xt_rot = Rot([P.sbuf([128, KC, 512], F32) for _ in range(1)])
        sq_rot = Rot([P.sbuf([128, 512], F32) for _ in range(3)])
        rstd = P.sbuf([128, 512], F32)
        t_rstd = Trk()
        ps_stat = P.psum([128, 512], F32)
        t_pss = Trk()
        xTv = xT.rearrange("(kc p) t -> p kc t", p=128)
        for tg in range(NTG):
            xt, t_xt = xt_rot.next()
            P.dma("sp", xt[:, :, :], xTv[:, :, tg * 512:(tg + 1) * 512], writes=(t_xt,), key=f"x{xt_rot.i}")
            rms_stats(P, xt, t_xt, 512, ones_f, t_ones, sq_rot, ps_stat, t_pss, rstd, t_rstd, eps_sb)
            for kc in range(KC):
                P.op("dve", (lambda e, xt=xt, kc=kc, tg=tg: e.scalar_tensor_tensor(
                    out=hT[:, kc, tg * 512:(tg + 1) * 512], in0=xt[:, kc, :], scalar=nw_sb[:, kc:kc + 1],
                    in1=rstd[:, :], op0=ALU.mult, op1=ALU.mult)),
                    reads=(t_xt, t_rstd, t_nw), writes=(t_hT,))

        wv = w_bf.rearrange("(kc p) n -> p kc n", p=128)
        w_rot = Rot([P.sbuf([128, KC, 512], BF16) for _ in range(2)])
        ps_rot = Rot([P.psum([128, 512], F32) for _ in range(4)])
        stg_rot = Rot([P.sbuf([128, 2048], BF16) for _ in range(3)])
        stgf_rot = Rot([P.sbuf([128, 2048], F32) for _ in range(1)])
        t_out = Trk("out")
        evac_i = [0]

        def evac(dst, src, rd, wr, func=None):
            if func is not None:
                P.op("act", lambda e: e.activation(out=dst, in_=src, func=func), reads=rd, writes=wr)
                return
            evac_i[0] += 1
            if evac_i[0] % 2 == 0:
                P.op("act", lambda e: e.activation(out=dst, in_=src, func=AF.Copy), reads=rd, writes=wr)
            else:
                P.op("dve", lambda e: e.tensor_copy(out=dst, in_=src), reads=rd, writes=wr)

        def load_w(c0, ncol):
            wt, t_w = w_rot.next()
            P.dma("sp", wt[:, :, :ncol], wv[:, :, c0:c0 + ncol], reads=ct_w.get(c0, c0 + ncol), writes=(t_w,), key=f"w{w_rot.i}")
            return wt, t_w

        def fm_group(c0, ncol, dst, drow0, func=None, f32out=False):
            wt, t_w = load_w(c0, ncol)
            for s0 in range(0, ncol, 128):
                sn = min(128, ncol - s0)
                stg, t_stg = (stgf_rot if f32out else stg_rot).next()
                for tg in range(NTG):
                    ps, t_ps = ps_rot.next()

                    def mm(e, wt=wt, s0=s0, sn=sn, tg=tg, ps=ps):
                        for kc in range(KC):
                            ins = e.matmul(ps[:sn, :], wt[:, kc, s0:s0 + sn], hT[:, kc, tg * 512:(tg + 1) * 512],
                                           start=(kc == 0), stop=(kc == KC - 1))
                        return ins
                    P.op("pe", mm, reads=(t_w, t_hT), writes=(t_ps,))
                    evac(stg[:sn, tg * 512:(tg + 1) * 512], ps[:sn, :], (t_ps,), (t_stg,), func)
                P.dma("pool", dst[drow0 + s0:drow0 + s0 + sn, :], stg[:sn, :T], reads=(t_stg,), writes=(t_out,),
                      key=f"o{(stgf_rot if f32out else stg_rot).i}{int(f32out)}")

        def tm_group(c0, ncol, dst, dcol0):
            wt, t_w = load_w(c0, ncol)
            for tt in range(T // 128):
                ps, t_ps = ps_rot.next()

                def mm(e, wt=wt, tt=tt, ps=ps):
                    for kc in range(KC):
                        ins = e.matmul(ps[:, :ncol], hT[:, kc, tt * 128:(tt + 1) * 128], wt[:, kc, :ncol],
                                       start=(kc == 0), stop=(kc == KC - 1))
                    return ins
                P.op("pe", mm, reads=(t_w, t_hT), writes=(t_ps,))
                stg, t_stg = stg_rot.next()
                evac(stg[:, :ncol], ps[:, :ncol], (t_ps,), (t_stg,))
                P.dma("pool", dst[tt * 128:(tt + 1) * 128, dcol0:dcol0 + ncol], stg[:, :ncol], reads=(t_stg,),
                      writes=(t_out,), key=f"o{stg_rot.i}0")

        for c0 in range(0, 2048, 512):
            fm_group(SEG_Q + c0, 512, o_qk, c0)
        for c0 in range(0, 1024, 512):
            tm_group(SEG_V + c0, 512, o_v, c0)
        for c0 in range(0, 2048, 512):
            fm_group(SEG_Z + c0, 512, o_z, c0)
        for c0 in range(0, 3072, 512):
            fm_group(SEG_XBC + c0, 512, o_xbc, c0)
        fm_group(SEG_DT, 32, o_dt, 0, f32out=True)
        for c0 in range(0, 1024, 512):
            fm_group(SEG_QM + c0, 512, o_qm, c0)
        for c0 in range(0, 6144, 512):
            fm_group(SEG_G + c0, 512, o_g, c0, func=AF.Sigmoid)
        P.finish((t_out,))
        P.emit()
    return nc


def load_vec16(P, dram_ap, key="c0"):
    n = dram_ap.shape[1]
    t = P.sbuf([128, n], F32)
    trk = Trk()
    P.dma("sp", t[:, :], dram_ap[:, :], writes=(trk,), key=key)
    return t, trk


def mm_group(P, ps, ps_trk, pairs, reads, M=128, N=None):
    def mm(e):
        n = len(pairs)
        for i, (l, r) in enumerate(pairs):
            ins = e.matmul(ps, l, r, start=(i == 0), stop=(i == n - 1))
        return ins
    P.op("pe", mm, reads=reads, writes=(ps_trk,))


def build_p3a(T, TG=256):
    nc = bass.Bass("TRN2", target_bir_lowering=False)
    di = lambda n, s, dt=F32: nc.dram_tensor(n, s, dt, kind="ExternalInput").ap()
    xT = di("xT", [D, T])
    yaT = di("yaT", [1024, T], BF16)
    ysT = di("ysT", [2048, T], BF16)
    qmT = di("qmT", [1024, T], BF16)
    gT = di("gT", [6144, T], BF16)
    memT = di("memT", [D, 256])
    nw_mem = di("nw_mem", [128, KC])
    nw_post = di("nw_post", [128, KC])
    nw_pre = di("nw_pre", [128, KC])
    w_kv = di("w_kv", [4, 128, KC * 512])
    w_br = di("w_br", [8, 128, 32 * 256])
    w_o = di("w_o", [8, 128, KC * 256])
    o_xm = nc.dram_tensor("o_xm", [D, T], F32, kind="ExternalOutput").ap()
    o_h2 = nc.dram_tensor("o_h2", [D, T], BF16, kind="ExternalOutput").ap()
    b_kv = nc.dram_tensor("b_kv", [4, 128, KC * 512], BF16).ap()
    b_br = nc.dram_tensor("b_br", [8, 128, 32 * 256], BF16).ap()
    b_o = nc.dram_tensor("b_o", [8, 128, KC * 256], BF16).ap()
    NTG = T // TG
    with contextlib.ExitStack() as st:
        P = Prog(nc, st)
        def flat_cast(src2d, dst2d, n, trk):
            for c0 in range(0, n, 2048):
                c1 = min(n, c0 + 2048)
                P.dma("pool", dst2d[:, c0:c1], src2d[:, c0:c1], writes=(trk,), key="wc")
        ct_kv = [Trk() for _ in range(4)]
        for cg in range(4):
            flat_cast(w_kv[cg], b_kv[cg], KC * 512, ct_kv[cg])
        ct_br = [Trk() for _ in range(8)]
        for cg in range(8):
            flat_cast(w_br[cg], b_br[cg], 32 * 256, ct_br[cg])
        ct_o = [Trk() for _ in range(8)]
        for cg in range(8):
            flat_cast(w_o[cg], b_o[cg], KC * 256, ct_o[cg])
        t_c = Trk("consts")
        ones_f = P.sbuf([128, 128], F32)
        P.op("pool", lambda e: e.memset(ones_f[:, :], 1.0), writes=(t_c,))
        ones_b = P.sbuf([128, 128], BF16)
        P.op("pool", lambda e: e.memset(ones_b[:, :], 1.0), writes=(t_c,))
        eps_sb = P.sbuf([128, 1], F32)
        P.op("pool", lambda e: e.memset(eps_sb[:, :], EPS), writes=(t_c,))
        nwm_sb, t_nwm = load_vec16(P, nw_mem)
        nwpo_sb, t_nwpo = load_vec16(P, nw_post)
        nwpr_sb, t_nwpr = load_vec16(P, nw_pre)
        sq_rot = Rot([P.sbuf([128, 256], F32) for _ in range(3)])
        rstd = P.sbuf([128, 256], F32)
        t_rstd = Trk()
        ps_stat = P.psum([128, 256], F32)
        t_pss = Trk()
        ps_rot = Rot([P.psum([128, 512], F32) for _ in range(5)])

        big_rot = Rot([P.sbuf([128, KC, 256], F32) for _ in range(2)])
        mt_sb, t_mt = big_rot.next()
        P.dma("sp", mt_sb[:, :, :], memT.rearrange("(kc p) m -> p kc m", p=128), writes=(t_mt,), key="big0")
        rms_stats(P, mt_sb, t_mt, 256, ones_f, t_c, sq_rot, ps_stat, t_pss, rstd, t_rstd, eps_sb)
        mnT = P.sbuf([128, KC, 256], BF16)
        t_mn = Trk()
        for kc in range(KC):
            P.op("dve", (lambda e, kc=kc: e.scalar_tensor_tensor(
                out=mnT[:, kc, :], in0=mt_sb[:, kc, :], scalar=nwm_sb[:, kc:kc + 1], in1=rstd[:, :],
                op0=ALU.mult, op1=ALU.mult)), reads=(t_mt, t_rstd, t_nwm), writes=(t_mn,))
        KmT = P.sbuf([128, 8, 256], BF16)
        Vm = P.sbuf([128, 2, 1024], BF16)
        t_kv = Trk()
        wkv_rot = Rot([P.sbuf([128, KC, 512], BF16) for _ in range(2)])
        for cg in range(4):
            wt, t_w = wkv_rot.next()
            P.dma("sp", wt[:, :, :], b_kv[cg].rearrange("p (k j) -> p k j", j=512), reads=(ct_kv[cg],), writes=(t_w,),
                  key=f"wkv{wkv_rot.i}")
            if cg < 2:
                for s in range(4):
                    ps, t_ps = ps_rot.next()
                    mm_group(P, ps[:, :256], t_ps, [(wt[:, kc, s * 128:(s + 1) * 128], mnT[:, kc, :]) for kc in range(KC)],
                             reads=(t_w, t_mn))
                    P.op("act", (lambda e, ps=ps, c=cg * 4 + s: e.activation(out=KmT[:, c, :], in_=ps[:, :256], func=AF.Copy)),
                         reads=(t_ps,), writes=(t_kv,))
            else:
                for mt in range(2):
                    ps, t_ps = ps_rot.next()
                    mm_group(P, ps[:, :], t_ps, [(mnT[:, kc, mt * 128:(mt + 1) * 128], wt[:, kc, :]) for kc in range(KC)],
                             reads=(t_w, t_mn))
                    P.op("act", (lambda e, ps=ps, mt=mt, c0=(cg - 2) * 512: e.activation(
                        out=Vm[:, mt, c0:c0 + 512], in_=ps[:, :], func=AF.Copy)), reads=(t_ps,), writes=(t_kv,))

        qm_sb = P.sbuf([128, 8, TG], BF16); t_qm = Trk()
        ya_sb = P.sbuf([128, 8, TG], BF16); t_ya = Trk()
        ys_sb = P.sbuf([128, 16, TG], BF16); t_ys = Trk()
        ym_sb = P.sbuf([128, 8, TG], BF16); t_ym = Trk()
        mg_sb = P.sbuf([128, KC, TG], BF16); t_mg = Trk()
        pT_rot = Rot([P.sbuf([128, TG], BF16) for _ in range(4)])
        rs_sb = P.sbuf([128, TG], F32); t_rs = Trk()
        g_rot = Rot([P.sbuf([128, 3, 2, TG], BF16) for _ in range(2)])
        wb_rot = Rot([P.sbuf([128, 32, 256], BF16) for _ in range(2)])
        wo_rot = Rot([P.sbuf([128, KC, 256], BF16) for _ in range(2)])
        macc_rot = Rot([P.sbuf([128, TG], F32) for _ in range(2)])
        mtmp_rot = Rot([P.sbuf([128, TG], F32) for _ in range(2)])
        tmp_rot = Rot([P.sbuf([128, TG], F32) for _ in range(2)])
        h2_rot = Rot([P.sbuf([128, KC, TG], BF16) for _ in range(1)])
        t_out = Trk("out")
        xv = xT.rearrange("(kc p) t -> p kc t", p=128)
        oxv = o_xm.rearrange("(kc p) t -> p kc t", p=128)
        ohv = o_h2.rearrange("(kc p) t -> p kc t", p=128)
        gv = gT.rearrange("(g c p) t -> p g c t", p=128, g=3)
        for tg in range(NTG):
            ts = slice(tg * TG, (tg + 1) * TG)
            P.dma("sp", qm_sb[:, :, :], qmT.rearrange("(c p) t -> p c t", p=128)[:, :, ts], writes=(t_qm,), key="qm")
            P.dma("sp", ya_sb[:, :, :], yaT.rearrange("(c p) t -> p c t", p=128)[:, :, ts], writes=(t_ya,), key="ya")
            P.dma("sp", ys_sb[:, :, :], ysT.rearrange("(c p) t -> p c t", p=128)[:, :, ts], writes=(t_ys,), key="ys")
            for hh in range(4):
                pts = []
                for mt in range(2):
                    ps, t_ps = ps_rot.next()
                    mm_group(P, ps[:, :TG], t_ps,
                             [(KmT[:, hh * 2 + dc, mt * 128:(mt + 1) * 128], qm_sb[:, hh * 2 + dc, :]) for dc in range(2)],
                             reads=(t_kv, t_qm))
                    pT, t_pT = pT_rot.next()
                    P.op("act", (lambda e, ps=ps, pT=pT: e.activation(out=pT[:, :], in_=ps[:, :TG], func=AF.Exp, scale=1.0 / 16.0)),
                         reads=(t_ps,), writes=(t_pT,))
                    pts.append((pT, t_pT))
                ps, t_ps = ps_rot.next()
                mm_group(P, ps[:, :TG], t_ps, [(ones_b[:, :], pT[:, :]) for (pT, _) in pts],
                         reads=(t_c,) + tuple(t for _, t in pts))
                P.op("dve", (lambda e, ps=ps: e.reciprocal(out=rs_sb[:, :], in_=ps[:, :TG])), reads=(t_ps,), writes=(t_rs,))
                for dc in range(2):
                    ps, t_ps = ps_rot.next()
                    mm_group(P, ps[:, :TG], t_ps,
                             [(Vm[:, mt, hh * 256 + dc * 128: hh * 256 + (dc + 1) * 128], pts[mt][0][:, :]) for mt in range(2)],
                             reads=(t_kv,) + tuple(t for _, t in pts))
                    P.op("dve", (lambda e, ps=ps, c=hh * 2 + dc: e.tensor_tensor(out=ym_sb[:, c, :], in0=ps[:, :TG], in1=rs_sb[:, :],
                                                                                op=ALU.mult)),
                         reads=(t_ps, t_rs), writes=(t_ym,))
            for cg in range(8):
                wb, t_w = wb_rot.next()
                cs = slice(cg * 256, (cg + 1) * 256)
                k = f"wb{wb_rot.i}"
                P.dma("sp", wb[:, :, :], b_br[cg].rearrange("p (k j) -> p k j", j=256), reads=(ct_br[cg],), writes=(t_w,), key=k)
                gt, t_g = g_rot.next()
                for g3 in range(3):
                    P.dma("sp", gt[:, g3, :, :], gv[:, g3, cg * 2:cg * 2 + 2, ts], writes=(t_g,), key=f"g{g_rot.i}")
                for ct in range(2):
                    c = cg * 2 + ct
                    macc, t_ma = macc_rot.next()
                    specs = ((0, 8, ya_sb, t_ya, 0), (8, 16, ys_sb, t_ys, 1), (24, 8, ym_sb, t_ym, 2))
                    for bi, (k0, nk, src, t_src, g3) in enumerate(specs):
                        ps, t_ps = ps_rot.next()
                        mm_group(P, ps[:, :TG], t_ps,
                                 [(wb[:, k0 + kc, ct * 128:(ct + 1) * 128], src[:, kc, :]) for kc in range(nk)],
                                 reads=(t_w, t_src))
                        if bi == 0:
                            P.op("dve", (lambda e, ps=ps, macc=macc, gt=gt, g3=g3, ct=ct: e.tensor_tensor(
                                out=macc[:, :], in0=ps[:, :TG], in1=gt[:, g3, ct, :], op=ALU.mult)),
                                reads=(t_ps, t_g), writes=(t_ma,))
                        else:
                            mtmp, t_mtmp = mtmp_rot.next()
                            P.op("dve", (lambda e, ps=ps, mtmp=mtmp, gt=gt, g3=g3, ct=ct: e.tensor_tensor(
                                out=mtmp[:, :], in0=ps[:, :TG], in1=gt[:, g3, ct, :], op=ALU.mult)),
                                reads=(t_ps, t_g), writes=(t_mtmp,))
                            last = (bi == 2)
                            dst = mg_sb[:, c, :] if last else macc[:, :]
                            P.op("pool", (lambda e, mtmp=mtmp, macc=macc, dst=dst: e.tensor_tensor(
                                out=dst, in0=mtmp[:, :], in1=macc[:, :], op=ALU.add)),
                                reads=(t_mtmp, t_ma), writes=((t_mg,) if last else (t_ma,)))
            yt, t_yt = big_rot.next()
            for cg in range(8):
                wo, t_w = wo_rot.next()
                P.dma("sp", wo[:, :, :], b_o[cg].rearrange("p (k j) -> p k j", j=256), reads=(ct_o[cg],), writes=(t_w,), key=f"wo{wo_rot.i}")
                for ct in range(2):
                    c = cg * 2 + ct
                    ps, t_ps = ps_rot.next()
                    mm_group(P, ps[:, :TG], t_ps, [(wo[:, kc, ct * 128:(ct + 1) * 128], mg_sb[:, kc, :]) for kc in range(KC)],
                             reads=(t_w, t_mg))
                    P.op("act", (lambda e, ps=ps, c=c, yt=yt: e.activation(out=yt[:, c, :], in_=ps[:, :TG], func=AF.Copy)),
                         reads=(t_ps,), writes=(t_yt,))
            xt, t_xt = big_rot.next()
            P.dma("sp", xt[:, :, :], xv[:, :, ts], writes=(t_xt,), key=f"big{big_rot.i}")
            rms_stats(P, yt, t_yt, TG, ones_f, t_c, sq_rot, ps_stat, t_pss, rstd, t_rstd, eps_sb)
            for kc in range(KC):
                tmp, t_tmp = tmp_rot.next()
                P.op("dve", (lambda e, kc=kc, tmp=tmp, yt=yt: e.scalar_tensor_tensor(
                    out=tmp[:, :], in0=yt[:, kc, :], scalar=nwpo_sb[:, kc:kc + 1], in1=rstd[:, :], op0=ALU.mult, op1=ALU.mult)),
                    reads=(t_yt, t_rstd, t_nwpo), writes=(t_tmp,))
                P.op("pool", (lambda e, kc=kc, tmp=tmp, xt=xt: e.tensor_tensor(out=xt[:, kc, :], in0=tmp[:, :], in1=xt[:, kc, :],
                                                                               op=ALU.add)),
                     reads=(t_tmp, t_xt), writes=(t_xt,))
            P.dma("pool", oxv[:, :, ts], xt[:, :, :], reads=(t_xt,), writes=(t_out,), key="oxm")
            rms_stats(P, xt, t_xt, TG, ones_f, t_c, sq_rot, ps_stat, t_pss, rstd, t_rstd, eps_sb)
            h2, t_h2 = h2_rot.next()
            for kc in range(KC):
                P.op("dve", (lambda e, kc=kc, xt=xt, h2=h2: e.scalar_tensor_tensor(
                    out=h2[:, kc, :], in0=xt[:, kc, :], scalar=nwpr_sb[:, kc:kc + 1], in1=rstd[:, :], op0=ALU.mult, op1=ALU.mult)),
                    reads=(t_xt, t_rstd, t_nwpr), writes=(t_h2,))
            P.dma("pool", ohv[:, :, ts], h2[:, :, :], reads=(t_h2,), writes=(t_out,), key="oh2")
        P.finish((t_out,))
        P.emit()
    return nc


DFF = 5632
NF = DFF // 128
GELU_C = 0.7978845608028654


def build_p3b(T, TG=256, gelu_native=True):
    nc = bass.Bass("TRN2", target_bir_lowering=False)
    di = lambda n, s, dt=F32: nc.dram_tensor(n, s, dt, kind="ExternalInput").ap()
    h2T = di("h2T", [D, T + 2], BF16)
    xmT = di("xmT", [D, T])
    w_up = di("w_up", [NF // 2, 128, 2 * KC * 256])
    w_dn = di("w_dn", [8, 128, NF * 256])
    cw = di("cw", [128, 2 * NF * 3])
    cb = di("cb", [128, 2 * NF])
    nw = di("nw", [128, KC])
    o_x = nc.dram_tensor("o_x", [D, T], F32, kind="ExternalOutput").ap()
    b_up = nc.dram_tensor("b_up", [NF // 2, 128, 2, KC, 256], BF16).ap()
    b_dn = nc.dram_tensor("b_dn", [8, 128, NF, 256], BF16).ap()
    NTG = T // TG
    NE = TG + 2
    with contextlib.ExitStack() as st:
        P = Prog(nc, st)
        def flat_cast(src2d, dst2d, n, trk):
            for c0 in range(0, n, 2048):
                c1 = min(n, c0 + 2048)
                P.dma("pool", dst2d[:, c0:c1], src2d[:, c0:c1], writes=(trk,), key="wc")
        ct_up = {}
        for fg in range(NF // 2):
            ct_up[fg] = Trk()
            flat_cast(w_up[fg], b_up[fg].rearrange("p h k j -> p (h k j)"), 2 * KC * 256, ct_up[fg])
        t_cdn = Trk()
        for cg in range(8):
            flat_cast(w_dn[cg], b_dn[cg].rearrange("p f j -> p (f j)"), NF * 256, t_cdn)
        t_c = Trk("consts")
        ones_f = P.sbuf([128, 128], F32)
        P.op("pool", lambda e: e.memset(ones_f[:, :], 1.0), writes=(t_c,))
        eps_sb = P.sbuf([128, 1], F32)
        P.op("pool", lambda e: e.memset(eps_sb[:, :], EPS), writes=(t_c,))
        cw_sb, t_cw = load_vec16(P, cw)
        cb_sb, t_cb = load_vec16(P, cb)
        nw_sb, t_nw = load_vec16(P, nw)
        sq_rot = Rot([P.sbuf([128, TG], F32) for _ in range(3)])
        rstd = P.sbuf([128, TG], F32); t_rstd = Trk()
        ps_stat = P.psum([128, TG], F32); t_pss = Trk()
        ps_rot = Rot([P.psum([128, 512], F32) for _ in range(6)])
        big_rot = Rot([P.sbuf([128, KC, TG], F32) for _ in range(2)])
        h2_rot = Rot([P.sbuf([128, KC, NE], BF16) for _ in range(2)])
        act_sb = P.sbuf([128, NF, TG], BF16); t_act = Trk()
        wu_rot = Rot([P.sbuf([128, 2, KC, 256], BF16) for _ in range(3)])
        wd_rot = Rot([P.sbuf([128, NF, 256], BF16) for _ in range(2)])
        u_rot = Rot([P.sbuf([128, NE], F32) for _ in range(4)])
        t_rot = Rot([P.sbuf([128, TG], F32) for _ in range(4)])
        ga_rot = Rot([P.sbuf([128, TG], F32) for _ in range(2)])
        tmp_rot = Rot([P.sbuf([128, TG], F32) for _ in range(2)])
        t_out = Trk("out")
        hv = h2T.rearrange("(kc p) t -> p kc t", p=128)
        xv = xmT.rearrange("(kc p) t -> p kc t", p=128)
        ov = o_x.rearrange("(kc p) t -> p kc t", p=128)

        def conv_chain(ps, t_ps, ch):
            u, t_u = u_rot.next()
            P.op("act", (lambda e: e.activation(out=u[:, :], in_=ps[:, :NE], func=AF.Copy)), reads=(t_ps,), writes=(t_u,))
            t, t_t = t_rot.next()
            P.op("pool", (lambda e: e.tensor_scalar(out=t[:, :], in0=u[:, 0:TG], scalar1=cw_sb[:, ch * 3:ch * 3 + 1],
                                                    scalar2=cb_sb[:, ch:ch + 1], op0=ALU.mult, op1=ALU.add)),
                 reads=(t_u, t_cw, t_cb), writes=(t_t,))
            for k in (1, 2):
                P.op("dve", (lambda e, k=k: e.scalar_tensor_tensor(out=t[:, :], in0=u[:, k:k + TG],
                                                                   scalar=cw_sb[:, ch * 3 + k:ch * 3 + k + 1], in1=t[:, :],
                                                                   op0=ALU.mult, op1=ALU.add)),
                     reads=(t_u, t_cw, t_t), writes=(t_t,))
            return t, t_t

        for tg in range(NTG):
            ts = slice(tg * TG, (tg + 1) * TG)
            h2, t_h2 = h2_rot.next()
            P.dma("sp", h2[:, :, :], hv[:, :, tg * TG:tg * TG + NE], writes=(t_h2,), key=f"h2{h2_rot.i}")
            for fg in range(NF // 2):
                wu, t_w = wu_rot.next()
                k = f"wu{wu_rot.i}"
                P.dma("sp", wu[:, :, :, :], b_up[fg], reads=(ct_up[fg],), writes=(t_w,), key=k)
                for ft in range(2):
                    f = fg * 2 + ft
                    res = []
                    for half in range(2):
                        ps, t_ps = ps_rot.next()
                        mm_group(P, ps[:, :NE], t_ps, [(wu[:, half, kc, ft * 128:(ft + 1) * 128], h2[:, kc, :]) for kc in range(KC)],
                                 reads=(t_w, t_h2))
                        res.append(conv_chain(ps, t_ps, half * NF + f))
                    (ta, t_ta), (tgg, t_tg) = res
                    ga, t_ga = ga_rot.next()
                    if gelu_native:
                        P.op("act", (lambda e, ta=ta, ga=ga: e.activation(out=ga[:, :], in_=ta[:, :], func=AF.Gelu_apprx_tanh)),
                             reads=(t_ta,), writes=(t_ga,))
                    else:
                        P.op("act", (lambda e, ta=ta, ga=ga: e.activation(out=ga[:, :], in_=ta[:, :], func=AF.Square)),
                             reads=(t_ta,), writes=(t_ga,))
                        P.op("pool", (lambda e, ga=ga: e.tensor_scalar(out=ga[:, :], in0=ga[:, :], scalar1=0.044715, scalar2=1.0,
                                                                      op0=ALU.mult, op1=ALU.add)), reads=(t_ga,), writes=(t_ga,))
                        P.op("pool", (lambda e, ta=ta, ga=ga: e.tensor_tensor(out=ga[:, :], in0=ga[:, :], in1=ta[:, :], op=ALU.mult)),
                             reads=(t_ga, t_ta), writes=(t_ga,))
                        P.op("act", (lambda e, ga=ga: e.activation(out=ga[:, :], in_=ga[:, :], func=AF.Sigmoid, scale=2.0 * GELU_C)),
                             reads=(t_ga,), writes=(t_ga,))
                        P.op("pool", (lambda e, ta=ta, ga=ga: e.tensor_tensor(out=ga[:, :], in0=ga[:, :], in1=ta[:, :], op=ALU.mult)),
                             reads=(t_ga, t_ta), writes=(t_ga,))
                    P.op("dve", (lambda e, ga=ga, tgg=tgg, f=f: e.tensor_tensor(out=act_sb[:, f, :], in0=ga[:, :], in1=tgg[:, :],
                                                                                op=ALU.mult)),
                         reads=(t_ga, t_tg), writes=(t_act,))
            yt, t_yt = big_rot.next()
            for cg in range(8):
                wd, t_w = wd_rot.next()
                P.dma("sp", wd[:, :, :], b_dn[cg], reads=(t_cdn,), writes=(t_w,), key=f"wd{wd_rot.i}")
                for ct in range(2):
                    c = cg * 2 + ct
                    ps, t_ps = ps_rot.next()
                    mm_group(P, ps[:, :TG], t_ps, [(wd[:, f, ct * 128:(ct + 1) * 128], act_sb[:, f, :]) for f in range(NF)],
                             reads=(t_w, t_act))
                    P.op("act", (lambda e, ps=ps, c=c, yt=yt: e.activation(out=yt[:, c, :], in_=ps[:, :TG], func=AF.Copy)),
                         reads=(t_ps,), writes=(t_yt,))
            xt, t_xt = big_rot.next()
            P.dma("sp", xt[:, :, :], xv[:, :, ts], writes=(t_xt,), key=f"big{big_rot.i}")
            rms_stats(P, yt, t_yt, TG, ones_f, t_c, sq_rot, ps_stat, t_pss, rstd, t_rstd, eps_sb)
            for kc in range(KC):
                tmp, t_tmp = tmp_rot.next()
                P.op("dve", (lambda e, kc=kc, tmp=tmp, yt=yt: e.scalar_tensor_tensor(
                    out=tmp[:, :], in0=yt[:, kc, :], scalar=nw_sb[:, kc:kc + 1], in1=rstd[:, :], op0=ALU.mult, op1=ALU.mult)),
                    reads=(t_yt, t_rstd, t_nw), writes=(t_tmp,))
                P.op("pool", (lambda e, kc=kc, tmp=tmp, xt=xt: e.tensor_tensor(out=xt[:, kc, :], in0=tmp[:, :], in1=xt[:, kc, :],
                                                                               op=ALU.add)),
                     reads=(t_tmp, t_xt), writes=(t_xt,))
            P.dma("pool", ov[:, :, ts], xt[:, :, :], reads=(t_xt,), writes=(t_out,), key="ox")
        P.finish((t_out,))
        P.emit()
    return nc


BIGR = 3000.0
NEGF = -1.0e30


def attn_tables(S):
    nb = S // 256
    j = np.arange(nb)[None, :]
    n = np.arange(nb)[:, None]
    past = (j < n)
    pastb = np.where(past, 0.0, NEGF).astype(np.float32).reshape(1, nb * nb)
    past01 = past.astype(np.float32).reshape(1, nb * nb)
    own01 = (j == n).astype(np.float32).reshape(1, nb * nb)
    rep = lambda a: np.ascontiguousarray(np.broadcast_to(a, (128, a.shape[1])))
    k = np.arange(128)[:, None]
    q = np.arange(256)[None, :]
    cmA = np.where(k <= q, 0.0, -BIGR)
    cmB = np.where(128 + k <= q, 0.0, -BIGR)
    cm = np.concatenate([cmA, cmB], axis=1).astype(NPBF)
    oh = np.zeros((32, nb, 128), np.float32)
    for jj in range(nb):
        oh[jj, jj, :] = 1.0
    half = 64
    inv = (10000.0 ** (-np.arange(half, dtype=np.float32) / half)).astype(np.float32)
    ang = np.arange(S, dtype=np.float32)[None, :] * inv[:, None]
    cos = np.cos(ang).astype(np.float32)
    sin = np.sin(ang).astype(np.float32)
    cosT = np.concatenate([cos, cos], axis=0)
    sinT = np.concatenate([-sin, sin], axis=0)
    return dict(pastb=rep(pastb), past01=rep(past01), own01=rep(own01), cm=np.ascontiguousarray(cm),
                onehot=np.ascontiguousarray(oh.reshape(32, nb * 128).astype(NPBF)),
                ident_f=np.eye(128, dtype=np.float32), ident_b=np.eye(128, dtype=np.float32).astype(NPBF),
                cosT=np.ascontiguousarray(cosT), sinT=np.ascontiguousarray(sinT))


def build_p2a(S):
    nb = S // 256
    NT = S // 128
    RC = min(S, 2048)
    nc = bass.Bass("TRN2", target_bir_lowering=False)
    di = lambda n, s, dt=F32: nc.dram_tensor(n, s, dt, kind="ExternalInput").ap()
    qk = di("qk", [4, 128, S], BF16)
    qks = di("qks", [4, 128, S], BF16)
    v = di("v", [2, S, 128], BF16)
    cosT = di("cosT", [128, S])
    sinT = di("sinT", [128, S])
    pastb = di("pastb", [128, nb * nb])
    past01 = di("past01", [128, nb * nb])
    own01 = di("own01", [128, nb * nb])
    cm = di("cm", [128, 512], BF16)
    onehot = di("onehot", [32, nb * 128], BF16)
    ident_f = di("ident_f", [128, 128])
    ident_b = di("ident_b", [128, 128], BF16)
    o_ya = nc.dram_tensor("o_ya", [256, S], BF16, kind="ExternalOutput").ap()
    scale = 128.0 ** -0.5
    with contextlib.ExitStack() as st:
        P = Prog(nc, st)
        t_c = Trk("consts")

        def cload(ap, shape, dt):
            t = P.sbuf(shape, dt)
            P.dma("sp", t[:, :], ap[:, :], writes=(t_c,), key="c0")
            return t
        pastb_sb = cload(pastb, [128, nb * nb], F32)
        past01_sb = cload(past01, [128, nb * nb], F32)
        own01_sb = cload(own01, [128, nb * nb], F32)
        cm_sb = cload(cm, [128, 512], BF16)
        oh_sb = cload(onehot, [32, nb * 128], BF16)
        idf_sb = cload(ident_f, [128, 128], F32)
        idb_sb = cload(ident_b, [128, 128], BF16)
        ones_b = P.sbuf([128, 128], BF16)
        P.op("pool", lambda e: e.memset(ones_b[:, :], 1.0), writes=(t_c,))

        QR = P.sbuf([128, S], BF16); t_QR = Trk()
        KR = P.sbuf([128, S], BF16); t_KR = Trk()
        V = P.sbuf([128, NT, 128], BF16); t_V = Trk()
        maskbT = P.sbuf([32, S], BF16); t_mb = Trk()
        outb = P.sbuf([128, S], BF16); t_ob = Trk()
        raw_rot = Rot([P.sbuf([128, 2, RC], BF16) for _ in range(2)])
        cs_sb = P.sbuf([128, 2, RC], F32); t_cs = Trk()
        r1_rot = Rot([P.sbuf([128, RC], F32) for _ in range(1)])
        r2_rot = Rot([P.sbuf([128, RC], F32) for _ in range(1)])
        kmf = P.sbuf([128, nb], F32); t_kmf = Trk()
        kmT = P.sbuf([128, nb], BF16); t_km = Trk()
        gm_rot = Rot([P.sbuf([128, nb], F32) for _ in range(2)])
        m8_rot = Rot([P.sbuf([128, 8], F32) for _ in range(2)])
        sel_rot = Rot([P.sbuf([128, 32], F32) for _ in range(2)])
        pT_rot = Rot([P.sbuf([128, 256], BF16) for _ in range(4)])
        rs_rot = Rot([P.sbuf([128, 256], F32) for _ in range(2)])
        ps_s_rot = Rot([P.psum([128, 512], F32) for _ in range(3)])
        ps_o_rot = Rot([P.psum([128, 512], F32) for _ in range(2)])
        ps_m_rot = Rot([P.psum([128, 512], F32) for _ in range(2)])
        ps_g = P.psum([128, 512], F32); t_psg = Trk()
        t_out = Trk("out")
        if nb < 32:
            for b in sel_rot.bufs:
                P.op("pool", (lambda e, b=b: e.memset(b[:, :], 0.0)), writes=(t_c,))

        for h in range(2):
            for which, dst, t_dst in ((0, QR, t_QR), (1, KR, t_KR)):
                src = which * 2 + h
                for c0 in range(0, S, RC):
                    raw, t_raw = raw_rot.next()
                    kk = f"raw{raw_rot.i}"
                    P.dma("sp", raw[:, 0, :], qk[src, :, c0:c0 + RC], writes=(t_raw,), key=kk)
                    P.dma("sp", raw[:, 1, :], qks[src, :, c0:c0 + RC], writes=(t_raw,), key=kk)
                    P.dma("sp", cs_sb[:, 0, :], cosT[:, c0:c0 + RC], writes=(t_cs,), key="cs")
                    P.dma("sp", cs_sb[:, 1, :], sinT[:, c0:c0 + RC], writes=(t_cs,), key="cs")
                    r1, t_r1 = r1_rot.next()
                    r2, t_r2 = r2_rot.next()
                    P.op("dve", (lambda e, raw=raw, r1=r1: e.tensor_tensor(out=r1[:, :], in0=raw[:, 0, :], in1=cs_sb[:, 0, :], op=ALU.mult)),
                         reads=(t_raw, t_cs), writes=(t_r1,))
                    P.op("pool", (lambda e, raw=raw, r2=r2: e.tensor_tensor(out=r2[:, :], in0=raw[:, 1, :], in1=cs_sb[:, 1, :], op=ALU.mult)),
                         reads=(t_raw, t_cs), writes=(t_r2,))
                    P.op("dve", (lambda e, r1=r1, r2=r2, dst=dst, c0=c0: e.tensor_tensor(out=dst[:, c0:c0 + RC], in0=r1[:, :], in1=r2[:, :],
                                                                                        op=ALU.add)),
                         reads=(t_r1, t_r2), writes=(t_dst,))
            P.dma("sp", V[:, :, :], v[h].rearrange("(n p) d -> p n d", p=128), writes=(t_V,), key="v")
            P.op("dve", (lambda e: e.tensor_reduce(out=kmf[:, :], in_=KR[:, :].rearrange("p (n k) -> p n k", k=256), axis=AX.X, op=ALU.add)),
                 reads=(t_KR,), writes=(t_kmf,))
            P.op("act", (lambda e: e.activation(out=kmT[:, :], in_=kmf[:, :], func=AF.Copy, scale=1.0 / 256.0)),
                 reads=(t_kmf,), writes=(t_km,))
            for qt in range(NT):
                n = qt // 2
                P.op("pe", (lambda e, qt=qt: e.matmul(ps_g[:, :nb], QR[:, qt * 128:(qt + 1) * 128], kmT[:, :], start=True, stop=True)),
                     reads=(t_QR, t_km), writes=(t_psg,))
                gm, t_gm = gm_rot.next()
                P.op("dve", (lambda e, gm=gm, n=n: e.tensor_tensor(out=gm[:, :], in0=ps_g[:, :nb], in1=pastb_sb[:, n * nb:(n + 1) * nb],
                                                                   op=ALU.add)), reads=(t_psg, t_c), writes=(t_gm,))
                m8, t_m8 = m8_rot.next()
                if nb >= 8:
                    P.op("dve", (lambda e, gm=gm, m8=m8: e.max(out=m8[:, :], in_=gm[:, :])), reads=(t_gm,), writes=(t_m8,))
                else:
                    raise NotImplementedError
                sel, t_sel = sel_rot.next()
                P.op("dve", (lambda e, gm=gm, m8=m8, sel=sel: e.tensor_scalar(out=sel[:, :nb], in0=gm[:, :], scalar1=m8[:, 2:3], scalar2=None,
                                                                              op0=ALU.is_ge)), reads=(t_gm, t_m8), writes=(t_sel,))
                P.op("dve", (lambda e, sel=sel, n=n: e.tensor_tensor(out=sel[:, :nb], in0=sel[:, :nb], in1=past01_sb[:, n * nb:(n + 1) * nb],
                                                                     op=ALU.mult)), reads=(t_sel, t_c), writes=(t_sel,))
                P.op("dve", (lambda e, sel=sel, n=n: e.tensor_tensor(out=sel[:, :nb], in0=sel[:, :nb], in1=own01_sb[:, n * nb:(n + 1) * nb],
                                                                     op=ALU.add)), reads=(t_sel, t_c), writes=(t_sel,))
                P.op("dve", (lambda e, sel=sel: e.tensor_scalar(out=sel[:, :nb], in0=sel[:, :nb], scalar1=BIGR, scalar2=-BIGR,
                                                                op0=ALU.mult, op1=ALU.add)), reads=(t_sel,), writes=(t_sel,))
                P.op("pe", (lambda e, sel=sel: e.transpose(ps_g[:32, 256:384], sel[:, :], idf_sb[:, :])),
                     reads=(t_sel, t_c, t_gm), writes=(t_psg,))
                P.op("act", (lambda e, qt=qt: e.activation(out=maskbT[:, qt * 128:(qt + 1) * 128], in_=ps_g[:32, 256:384], func=AF.Copy)),
                     reads=(t_psg,), writes=(t_mb,))
            for n in range(nb):
                qs = slice(n * 256, (n + 1) * 256)
                ps_o, t_po = ps_o_rot.next()
                ps_m, t_pm = ps_m_rot.next()
                nkt = 2 * n + 2
                LA = 2
                sc_tiles = {}

                def emit_score(kt, n=n, qs=qs):
                    j = kt // 2
                    ps_s, t_pss = ps_s_rot.next()
                    pairs = [(KR[:, kt * 128:(kt + 1) * 128], QR[:, qs]),
                             (oh_sb[:, j * 128:(j + 1) * 128], maskbT[:, qs])]
                    if j == n:
                        pairs.append((idb_sb[:, :], cm_sb[:, (kt % 2) * 256:(kt % 2 + 1) * 256]))
                    mm_group(P, ps_s[:, :256], t_pss, pairs, reads=(t_KR, t_QR, t_mb, t_c))
                    sc_tiles[kt] = (ps_s, t_pss)
                for kt in range(min(LA, nkt)):
                    emit_score(kt)
                for kt in range(nkt):
                    ps_s, t_pss = sc_tiles.pop(kt)
                    pT, t_pT = pT_rot.next()
                    P.op("act", (lambda e, ps_s=ps_s, pT=pT: e.activation(out=pT[:, :], in_=ps_s[:, :256], func=AF.Exp, scale=scale)),
                         reads=(t_pss,), writes=(t_pT,))
                    if kt + LA < nkt:
                        emit_score(kt + LA)

                    def mm2(e, kt=kt, pT=pT, ps_o=ps_o, ps_m=ps_m, nkt=nkt):
                        e.matmul(ps_o[:, :256], V[:, kt, :], pT[:, :], start=(kt == 0), stop=(kt == nkt - 1))
                        return e.matmul(ps_m[:, :256], ones_b[:, :], pT[:, :], start=(kt == 0), stop=(kt == nkt - 1))
                    P.op("pe", mm2, reads=(t_V, t_pT, t_c), writes=(t_po, t_pm))
                rs, t_rs = rs_rot.next()
                P.op("dve", (lambda e, rs=rs, ps_m=ps_m: e.reciprocal(out=rs[:, :], in_=ps_m[:, :256])), reads=(t_pm,), writes=(t_rs,))
                P.op("dve", (lambda e, rs=rs, ps_o=ps_o, qs=qs: e.tensor_tensor(out=outb[:, qs], in0=ps_o[:, :256], in1=rs[:, :], op=ALU.mult)),
                     reads=(t_po, t_rs), writes=(t_ob,))
            P.dma("pool", o_ya[h * 128:(h + 1) * 128, :], outb[:, :], reads=(t_ob,), writes=(t_out,), key="oya")
        P.finish((t_out,))
        P.emit()
    return nc


def ssd_tables():
    oh = np.zeros((128, 8, 128), np.float32)
    for h in range(8):
        oh[h, h, :] = 1.0
    s = np.arange(128)[:, None]
    l = np.arange(128)[None, :]
    tri = (l >= s).astype(np.float32)
    return dict(oh8=np.ascontiguousarray(oh.reshape(128, 1024)), tri=tri, ident_f=np.eye(128, dtype=np.float32),
                ident_b=np.eye(128, dtype=np.float32).astype(NPBF))


def build_p2b(S, SC=512, pipe=1):
    nc = bass.Bass("TRN2", target_bir_lowering=False)
    di = lambda n, s, dt=F32: nc.dram_tensor(n, s, dt, kind="ExternalInput").ap()
    xbc = di("xbc", [768, S + 4], BF16)
    h_in = di("h_in", [128, 512])
    zT = di("zT", [512, S], BF16)
    dtT = di("dtT", [8, S])
    cw = di("cw", [128, 24])
    cb = di("cb", [128, 6])
    dtb = di("dtb", [128, 2])
    alog = di("alog", [128, 2])
    dsk = di("dsk", [128, 4])
    nw = di("nw", [128, 4])
    oh8 = di("oh8", [128, 1024])
    tri = di("tri", [128, 128])
    ident_f = di("ident_f", [128, 128])
    ident_b = di("ident_b", [128, 128], BF16)
    o_ys = nc.dram_tensor("o_ys", [512, S], BF16, kind="ExternalOutput").ap()
    o_h = nc.dram_tensor("o_h", [128, 512], F32, kind="ExternalOutput").ap()
    SC = min(SC, S)
    NSC = S // SC
    NCH = SC // 128
    with contextlib.ExitStack() as st:
        P = Prog(nc, st)
        t_c = Trk("consts")

        def cload(ap, shape, dt):
            t = P.sbuf(shape, dt)
            P.dma("sp", t[:, :], ap[:, :], writes=(t_c,), key="c0")
            return t
        cw_sb = cload(cw, [128, 24], F32)
        cb_sb = cload(cb, [128, 6], F32)
        dtb_sb = cload(dtb, [128, 2], F32)
        alog_sb = cload(alog, [128, 2], F32)
        dsk_sb = cload(dsk, [128, 4], F32)
        nw_sb = cload(nw, [128, 4], F32)
        oh8_sb = cload(oh8, [128, 1024], F32)
        tri_sb = cload(tri, [128, 128], F32)
        idf_sb = cload(ident_f, [128, 128], F32)
        idb_sb = cload(ident_b, [128, 128], BF16)
        ones_f = P.sbuf([128, 128], F32)
        P.op("pool", lambda e: e.memset(ones_f[:, :], 1.0), writes=(t_c,))
        eps_sb = P.sbuf([128, 1], F32)
        P.op("pool", lambda e: e.memset(eps_sb[:, :], EPS), writes=(t_c,))
        one_sb = P.sbuf([128, 1], F32)
        P.op("pool", lambda e: e.memset(one_sb[:, :], 1.0), writes=(t_c,))
        zero_sb = P.sbuf([128, 1], F32)
        P.op("pool", lambda e: e.memset(zero_sb[:, :], 0.0), writes=(t_c,))
        A_sb = P.sbuf([128, 2], F32)
        P.op("act", lambda e: e.activation(out=A_sb[:, :], in_=alog_sb[:, :], func=AF.Exp), reads=(t_c,), writes=(t_c,))
        P.op("dve", lambda e: e.tensor_scalar(out=A_sb[:, :], in0=A_sb[:, :], scalar1=-1.0, scalar2=None, op0=ALU.mult),
             reads=(t_c,), writes=(t_c,))

        prep_rot = Rot([dict(raw=P.sbuf([128, 6, SC + 4], BF16), zr=P.sbuf([128, 4, SC], BF16), dtr=P.sbuf([8, SC], F32),
                             xc=P.sbuf([128, 6, SC], BF16), sz=P.sbuf([128, 4, SC], BF16), dts=P.sbuf([128, SC], F32),
                             aT=P.sbuf([8, SC], F32), t_raw=Trk(), t_zr=Trk(), t_dtr=Trk(), t_xc=Trk(), t_sz=Trk(),
                             t_dts=Trk(), t_aT=Trk()) for _ in range(2)])
        ct_rot = Rot([P.sbuf([128, SC], F32) for _ in range(2)])
        acs_rot = Rot([P.sbuf([128, 128], F32) for _ in range(2)])
        datm_rot = Rot([P.sbuf([128, 16], F32) for _ in range(2)])
        E_rot = Rot([P.sbuf([128, 8, 128], F32) for _ in range(2)])
        Dp_rot = Rot([P.sbuf([128, 8, 128], F32) for _ in range(2)])
        cbm_rot = Rot([P.sbuf([128, 128], F32) for _ in range(2)])
        MT_rot = Rot([P.sbuf([128, 8, 128], BF16) for _ in range(2)])
        ChT_rot = Rot([P.sbuf([128, 8, 128], BF16) for _ in range(2)])
        X_rot = Rot([P.sbuf([128, 8, 128], BF16) for _ in range(2)])
        Xd_rot = Rot([P.sbuf([128, 512], BF16) for _ in range(2)])
        Btm_rot = Rot([P.sbuf([128, 128], BF16) for _ in range(2)])
        yg_rot = Rot([P.sbuf([128, 4, 128], F32) for _ in range(2)])
        sq_rot = Rot([P.sbuf([128, 128], F32) for _ in range(3)])
        rstd_rot = Rot([P.sbuf([128, 128], F32) for _ in range(2)])
        H = P.sbuf([128, 512], F32); t_H = Trk()
        Hbf = P.sbuf([128, 8, 128], BF16); t_Hbf = Trk()
        yo_rot = Rot([P.sbuf([128, 4, SC], BF16) for _ in range(2)])
        ps_bc = [P.psum([128, 512], F32) for _ in range(2)]; t_bc = Trk()
        ps_tr = P.psum([128, 1024], BF16); t_tr = Trk()
        ps_tq = P.psum([128, 512], F32); t_tq = Trk()
        ps_y_rot = Rot([P.psum([128, 512], F32) for _ in range(2)])
        ps_st = P.psum([128, 512], F32); t_st = Trk()
        ps_n = P.psum([128, 512], F32); t_n = Trk()
        t_out = Trk("out")
        xv = xbc.rearrange("(c p) t -> p c t", p=128)
        zv = zT.rearrange("(c p) t -> p c t", p=128)
        ov = o_ys.rearrange("(c p) t -> p c t", p=128)

        P.dma("sp", H[:, :], h_in[:, :], writes=(t_H,), key="hin")
        for pbz in prep_rot.bufs:
            P.op("pool", (lambda e, pbz=pbz: e.memset(pbz["dts"][:, :], 0.0)), writes=(pbz["t_dts"],))
        for bi, bz in enumerate(acs_rot.bufs):
            P.op("pool", (lambda e, bz=bz: e.memset(bz[:, :], 0.0)), writes=(acs_rot.trk[bi],))
        for bi, bz in enumerate(X_rot.bufs):
            P.op("pool", (lambda e, bz=bz: e.memset(bz[:, :, :], 0.0)), writes=(X_rot.trk[bi],))
        P.op("pool", (lambda e: e.memset(Hbf[:, :, :], 0.0)), writes=(t_Hbf,))

        def h_to_bf():
            for hh in range(8):
                ee = hh % 2
                P.op("act", (lambda e, hh=hh, ee=ee: e.activation(out=Hbf[:, hh, ee * 64:(ee + 1) * 64], in_=H[:, hh * 64:(hh + 1) * 64],
                                                                  func=AF.Copy)), reads=(t_H,), writes=(t_Hbf,))
        h_to_bf()

        def prep(sc):
            pb, _ = prep_rot.next()
            k = str(prep_rot.i)
            raw, zr, dtr, xc, sz, dts, aT = pb["raw"], pb["zr"], pb["dtr"], pb["xc"], pb["sz"], pb["dts"], pb["aT"]
            t0 = sc * SC
            P.dma("sp", raw[:, :, :], xv[:, :, t0:t0 + SC + 4], writes=(pb["t_raw"],), key="raw" + k)
            P.dma("sp", zr[:, :, :], zv[:, :, t0:t0 + SC], writes=(pb["t_zr"],), key="zr" + k)
            P.dma("sp", dtr[:, :], dtT[:, t0:t0 + SC], writes=(pb["t_dtr"],), key="dtr" + k)
            for ch in range(6):
                ct, t_ct = ct_rot.next()
                P.op("pool", (lambda e, ct=ct, ch=ch: e.tensor_scalar(out=ct[:, :], in0=raw[:, ch, 1:1 + SC], scalar1=cw_sb[:, ch * 4:ch * 4 + 1],
                                                                       scalar2=cb_sb[:, ch:ch + 1], op0=ALU.mult, op1=ALU.add)),
                     reads=(pb["t_raw"], t_c), writes=(t_ct,))
                for kk in (1, 2, 3):
                    P.op("dve", (lambda e, ct=ct, ch=ch, kk=kk: e.scalar_tensor_tensor(
                        out=ct[:, :], in0=raw[:, ch, 1 + kk:1 + kk + SC], scalar=cw_sb[:, ch * 4 + kk:ch * 4 + kk + 1], in1=ct[:, :],
                        op0=ALU.mult, op1=ALU.add)), reads=(pb["t_raw"], t_c, t_ct), writes=(t_ct,))
                P.op("act", (lambda e, ct=ct, ch=ch: e.activation(out=xc[:, ch, :], in_=ct[:, :], func=AF.Silu)),
                     reads=(t_ct,), writes=(pb["t_xc"],))
            for pc in range(4):
                P.op("act", (lambda e, pc=pc: e.activation(out=sz[:, pc, :], in_=zr[:, pc, :], func=AF.Silu)),
                     reads=(pb["t_zr"],), writes=(pb["t_sz"],))
            P.op("act", (lambda e: e.activation(out=dts[:8, :], in_=dtr[:, :], func=AF.Exp, bias=dtb_sb[:8, 0:1])),
                 reads=(pb["t_dtr"], t_c), writes=(pb["t_dts"],))
            P.op("act", (lambda e: e.activation(out=dts[:8, :], in_=dts[:8, :], func=AF.Ln, bias=one_sb[:8, 0:1])),
                 reads=(pb["t_dts"], t_c), writes=(pb["t_dts"],))
            P.op("dve", (lambda e: e.tensor_scalar(out=aT[:, :], in0=dts[:8, :], scalar1=A_sb[:8, 0:1], scalar2=None, op0=ALU.mult)),
                 reads=(pb["t_dts"], t_c), writes=(pb["t_aT"],))
            return pb

        def stage_a(pb, c):
            xc, dts, aT = pb["xc"], pb["dts"], pb["aT"]
            t_xc, t_dts, t_aT = pb["t_xc"], pb["t_dts"], pb["t_aT"]
            cs = slice(c * 128, c * 128 + 128)
            acs, t_acs = acs_rot.next()
            P.op("dve", (lambda e: e.tensor_tensor_scan(out=acs[:8, :], data0=ones_f[:8, :], data1=aT[:, cs],
                                                        initial=0.0, op0=ALU.mult, op1=ALU.add)),
                 reads=(t_aT, t_c), writes=(t_acs,))

            def trs(e):
                e.transpose(ps_tq[:, 0:128], dts[:, cs], idf_sb[:, :])
                return e.transpose(ps_tq[:, 128:256], acs[:, :], idf_sb[:, :])
            P.op("pe", trs, reads=(t_dts, t_acs, t_c), writes=(t_tq,))
            datm, t_datm = datm_rot.next()
            P.op("act", (lambda e: e.activation(out=datm[:, 0:8], in_=ps_tq[:, 0:8], func=AF.Copy)), reads=(t_tq,), writes=(t_datm,))
            P.op("act", (lambda e: e.activation(out=datm[:, 8:16], in_=ps_tq[:, 128:136], func=AF.Copy)), reads=(t_tq,), writes=(t_datm,))

            def bcs(e):
                for hh in range(8):
                    ins = e.matmul(ps_bc[hh // 4][:, (hh % 4) * 128:(hh % 4 + 1) * 128], oh8_sb[:, hh * 128:(hh + 1) * 128],
                                   acs[:, :], start=True, stop=True)
                return ins
            P.op("pe", bcs, reads=(t_acs, t_c), writes=(t_bc,))
            E, t_E = E_rot.next()
            Dp, t_Dp = Dp_rot.next()
            for hh in range(8):
                src = ps_bc[hh // 4][:, (hh % 4) * 128:(hh % 4 + 1) * 128]
                P.op("act", (lambda e, hh=hh, src=src: e.activation(out=E[:, hh, :], in_=src, func=AF.Exp)), reads=(t_bc,), writes=(t_E,))
                P.op("dve", (lambda e, hh=hh, src=src: e.tensor_scalar(out=Dp[:, hh, :], in0=src, scalar1=datm[:, 8 + hh:9 + hh], scalar2=None,
                                                                      op0=ALU.subtract)), reads=(t_bc, t_datm), writes=(t_Dp,))
                P.op("dve", (lambda e, hh=hh: e.tensor_scalar(out=Dp[:, hh, :], in0=Dp[:, hh, :], scalar1=0.0, scalar2=None, op0=ALU.min)),
                     reads=(t_Dp,), writes=(t_Dp,))
                P.op("act", (lambda e, hh=hh: e.activation(out=Dp[:, hh, :], in_=Dp[:, hh, :], func=AF.Exp)), reads=(t_Dp,), writes=(t_Dp,))
            P.op("pe", (lambda e: e.matmul(ps_tq[:, 256:384], xc[:, 4, cs], xc[:, 5, cs], start=True, stop=True)),
                 reads=(t_xc, t_datm), writes=(t_tq,))
            cbm, t_cbm = cbm_rot.next()
            P.op("dve", (lambda e: e.tensor_tensor(out=cbm[:, :], in0=ps_tq[:, 256:384], in1=tri_sb[:, :], op=ALU.mult)),
                 reads=(t_tq, t_c), writes=(t_cbm,))
            MT, t_MT = MT_rot.next()
            ChT, t_ChT = ChT_rot.next()
            for hh in range(8):
                P.op("pool", (lambda e, hh=hh: e.tensor_tensor(out=MT[:, hh, :], in0=Dp[:, hh, :], in1=cbm[:, :], op=ALU.mult)),
                     reads=(t_Dp, t_cbm), writes=(t_MT,))
                P.op("pool", (lambda e, hh=hh: e.tensor_tensor(out=ChT[:, hh, :], in0=E[:, hh, :], in1=xc[:, 5, cs], op=ALU.mult)),
                     reads=(t_E, t_xc), writes=(t_ChT,))

            def trx(e):
                for pc in range(4):
                    e.transpose(ps_tr[:, pc * 128:(pc + 1) * 128], xc[:, pc, cs], idb_sb[:, :])
                return e.transpose(ps_tr[:, 512:640], xc[:, 4, cs], idb_sb[:, :])
            P.op("pe", trx, reads=(t_xc, t_c), writes=(t_tr,))
            X, t_X = X_rot.next()
            Xd, t_Xd = Xd_rot.next()
            Btm, t_Btm = Btm_rot.next()
            P.op("act", (lambda e: e.activation(out=Btm[:, :], in_=ps_tr[:, 512:640], func=AF.Copy)), reads=(t_tr,), writes=(t_Btm,))
            for hh in range(8):
                hs = slice(hh * 64, (hh + 1) * 64)
                P.op("dve", (lambda e, hs=hs, hh=hh: e.tensor_scalar(out=X[:, hh, (hh % 2) * 64:(hh % 2 + 1) * 64], in0=ps_tr[:, hs],
                                                                    scalar1=datm[:, hh:hh + 1], scalar2=None,
                                                                    op0=ALU.mult)), reads=(t_tr, t_datm), writes=(t_X,))
                P.op("dve", (lambda e, hs=hs, hh=hh: e.tensor_scalar(out=Xd[:, hs], in0=ps_tr[:, hs], scalar1=datm[:, hh:hh + 1],
                                                                    scalar2=Dp[:, hh, 127:128], op0=ALU.mult, op1=ALU.mult)),
                     reads=(t_tr, t_datm, t_Dp), writes=(t_Xd,))
            return dict(E=E, t_E=t_E, MT=MT, t_MT=t_MT, ChT=ChT, t_ChT=t_ChT, X=X, t_X=t_X, Xd=Xd, t_Xd=t_Xd, Btm=Btm, t_Btm=t_Btm)

        def stage_b(pb, c, a, yo, t_yo):
            xc, sz = pb["xc"], pb["sz"]
            t_xc, t_sz = pb["t_xc"], pb["t_sz"]
            cs = slice(c * 128, c * 128 + 128)
            E, MT, ChT, X, Xd, Btm = a["E"], a["MT"], a["ChT"], a["X"], a["Xd"], a["Btm"]
            ps_y, t_y = ps_y_rot.next()

            def ymm(e):
                for hp in range(4):
                    out = ps_y[:, hp * 128:(hp + 1) * 128]
                    e.matmul(out, X[:, 2 * hp, :], MT[:, 2 * hp, :], start=True, stop=False)
                    e.matmul(out, X[:, 2 * hp + 1, :], MT[:, 2 * hp + 1, :], start=False, stop=False)
                    e.matmul(out, Hbf[:, 2 * hp, :], ChT[:, 2 * hp, :], start=False, stop=False)
                    ins = e.matmul(out, Hbf[:, 2 * hp + 1, :], ChT[:, 2 * hp + 1, :], start=False, stop=True)
                return ins
            P.op("pe", ymm, reads=(a["t_X"], a["t_MT"], a["t_ChT"], t_Hbf), writes=(t_y,))
            P.op("pe", (lambda e: e.matmul(ps_st[:, :], Btm[:, :], Xd[:, :], start=True, stop=True)),
                 reads=(a["t_Btm"], a["t_Xd"]), writes=(t_st,))
            for hh in range(8):
                hs = slice(hh * 64, (hh + 1) * 64)
                P.op("dve", (lambda e, hs=hs, hh=hh: e.scalar_tensor_tensor(
                    out=H[:, hs], in0=H[:, hs], scalar=E[:, hh, 127:128], in1=ps_st[:, hs], op0=ALU.mult, op1=ALU.add)),
                    reads=(t_st, a["t_E"], t_H, t_y), writes=(t_H,))
            h_to_bf()
            yg, t_yg = yg_rot.next()
            for hp in range(4):
                P.op("dve", (lambda e, hp=hp: e.scalar_tensor_tensor(
                    out=yg[:, hp, :], in0=xc[:, hp, cs], scalar=dsk_sb[:, hp:hp + 1], in1=ps_y[:, hp * 128:(hp + 1) * 128],
                    op0=ALU.mult, op1=ALU.add)), reads=(t_xc, t_y, t_c), writes=(t_yg,))
                P.op("pool", (lambda e, hp=hp: e.tensor_tensor(out=yg[:, hp, :], in0=yg[:, hp, :], in1=sz[:, hp, cs], op=ALU.mult)),
                     reads=(t_yg, t_sz), writes=(t_yg,))
            for hp in range(4):
                sq, t_sq = sq_rot.next()
                P.op("pool", (lambda e, sq=sq, hp=hp: e.tensor_tensor(out=sq[:, :], in0=yg[:, hp, :], in1=yg[:, hp, :], op=ALU.mult)),
                     reads=(t_yg,), writes=(t_sq,))
                P.op("pe", (lambda e, sq=sq, hp=hp: e.matmul(ps_n[:, :128], ones_f[:, :], sq[:, :], start=(hp == 0), stop=(hp == 3))),
                     reads=(t_sq, t_c), writes=(t_n,))
            rstd, t_rstd = rstd_rot.next()
            P.op("act", (lambda e: e.activation(out=rstd[:, :], in_=ps_n[:, :128], func=AF.Ln, bias=eps_sb[:, 0:1], scale=1.0 / 512.0)),
                 reads=(t_n, t_c), writes=(t_rstd,))
            P.op("act", (lambda e: e.activation(out=rstd[:, :], in_=rstd[:, :], func=AF.Exp, scale=-0.5)), reads=(t_rstd,), writes=(t_rstd,))
            for hp in range(4):
                P.op("dve", (lambda e, hp=hp: e.scalar_tensor_tensor(
                    out=yo[:, hp, cs], in0=yg[:, hp, :], scalar=nw_sb[:, hp:hp + 1], in1=rstd[:, :], op0=ALU.mult, op1=ALU.mult)),
                    reads=(t_yg, t_rstd, t_c), writes=(t_yo,))

        chunks = [(sc, c) for sc in range(NSC) for c in range(NCH)]
        pbs = {}
        yos = {}
        if pipe == 0:
            for i, (sc, c) in enumerate(chunks):
                if c == 0:
                    pbs[sc] = prep(sc)
                    yos[sc] = yo_rot.next()
                a_cur = stage_a(pbs[sc], c)
                yo, t_yo = yos[sc]
                stage_b(pbs[sc], c, a_cur, yo, t_yo)
                if c == NCH - 1:
                    P.dma("pool", ov[:, :, sc * SC:(sc + 1) * SC], yo[:, :, :], reads=(t_yo,), writes=(t_out,), key=f"oys{sc % 2}")
        else:
            a_cur = None
            for i, (sc, c) in enumerate(chunks):
                if c == 0:
                    if sc not in pbs:
                        pbs[sc] = prep(sc)
                    yos[sc] = yo_rot.next()
                    if pipe == 2 and sc + 1 < NSC:
                        pbs[sc + 1] = prep(sc + 1)
                if a_cur is None:
                    a_cur = stage_a(pbs[sc], c)
                a_next = None
                if i + 1 < len(chunks):
                    nsc, ncc = chunks[i + 1]
                    if nsc == sc or pipe == 2:
                        a_next = stage_a(pbs[nsc], ncc)
                yo, t_yo = yos[sc]
                stage_b(pbs[sc], c, a_cur, yo, t_yo)
                a_cur = a_next
                if c == NCH - 1:
                    P.dma("pool", ov[:, :, sc * SC:(sc + 1) * SC], yo[:, :, :], reads=(t_yo,), writes=(t_out,), key=f"oys{sc % 2}")
        P.dma("pool", o_h[:, :], H[:, :], reads=(t_H,), writes=(t_out,), key="oh")
        P.finish((t_out,))
        P.emit()
    return nc


_PROGS = {}


def _prog(name, fn):
    if name not in _PROGS:
        _PROGS[name] = fn()
    return _PROGS[name]


def _lay(v):
    return np.ascontiguousarray(np.asarray(v, np.float32).reshape(-1, 128).T)


def _tile_cols(w, ncol):
    K, N = w.shape
    t = np.asarray(w, np.float32).reshape(K // 128, 128, N // ncol, ncol).transpose(2, 1, 0, 3)
    return t


def _run(nc, in_maps):
    res = run_bass_kernel_spmd(nc, in_maps, core_ids=list(range(NCORES)))
    return res.results


def kernel(x, mem, norm_mix_pre, norm_mix_post, norm_ffn_pre, norm_ffn_post, norm_mem,
           w_in, conv_ssd_w, conv_ssd_b, dt_bias, a_log, d_skip, ssd_norm, w_mem_kv,
           w_br_attn, w_br_ssd, w_br_mem, w_out, w_up, conv_ffn_w, conv_ffn_b, w_down):
    f32 = lambda a: np.asarray(a, np.float32)
    x = f32(x)
    mem = f32(mem)
    B, S, _ = x.shape
    T = S * B // NCORES
    J = S // T
    depth = w_in.shape[0]
    p1 = _prog("p1", lambda: build_p1(T))
    p2a = _prog("p2a", lambda: build_p2a(S))
    p2b = _prog("p2b", lambda: build_p2b(2048, SC=512))
    p3a = _prog("p3a", lambda: build_p3a(T))
    p3b = _prog("p3b", lambda: build_p3b(T))
    atab = attn_tables(S)
    stab = ssd_tables()
    cores = [(b, j) for b in range(B) for j in range(J)]
    xT = [np.ascontiguousarray(x[b, j * T:(j + 1) * T, :].T) for (b, j) in cores]
    memT = [np.ascontiguousarray(mem[b].T) for b in range(B)]
    pad8 = lambda v: np.ascontiguousarray(np.broadcast_to(np.pad(f32(v), (0, 120))[:, None], (128, 2)))
    for l in range(depth):
        w_in_l = f32(w_in[l])
        nw = _lay(norm_mix_pre[l])
        r1 = _run(p1, [dict(xT=xT[c], nw=nw, w_in=w_in_l) for c in range(NCORES)])
        del w_in_l
        full = lambda key, b: np.concatenate([r1[b * J + j][key] for j in range(J)], axis=1)
        qk_f = [full("o_qk", b) for b in range(B)]
        z_f = [full("o_z", b) for b in range(B)]
        xbc_f = [full("o_xbc", b) for b in range(B)]
        dt_f = [full("o_dt", b) for b in range(B)]
        v_f = [np.concatenate([r1[b * J + j]["o_v"] for j in range(J)], axis=0) for b in range(B)]
        maps = []
        for (b, g) in cores:
            hs = (2 * g, 2 * g + 1)
            qk = np.stack([qk_f[b][h * 128:(h + 1) * 128] for h in hs] + [qk_f[b][1024 + h * 128:1024 + (h + 1) * 128] for h in hs])
            qks = np.ascontiguousarray(np.concatenate([qk[:, 64:], qk[:, :64]], axis=1))
            v = np.ascontiguousarray(np.stack([v_f[b][:, h * 128:(h + 1) * 128] for h in hs]))
            maps.append(dict(qk=np.ascontiguousarray(qk), qks=qks, v=v, **atab))
        r2a = _run(p2a, maps)
        cw_l = f32(conv_ssd_w[l])
        cb_l = f32(conv_ssd_b[l])
        SEG = 2048
        hst = [np.zeros((128, 512), np.float32) for _ in range(NCORES)]
        ys_parts = [[] for _ in range(NCORES)]
        for sg in range(S // SEG):
            maps = []
            for c, (b, g) in enumerate(cores):
                ch = np.concatenate([np.arange(g * 512, (g + 1) * 512), 2048 + np.arange(g * 128, (g + 1) * 128),
                                     2560 + np.arange(g * 128, (g + 1) * 128)])
                cw = np.ascontiguousarray(cw_l[:, ch].T.reshape(6, 128, 4).transpose(1, 0, 2).reshape(128, 24))
                seg = xbc_f[b][ch][:, sg * SEG:(sg + 1) * SEG]
                halo = xbc_f[b][ch][:, sg * SEG - 4:sg * SEG] if sg > 0 else np.zeros((768, 4), seg.dtype)
                maps.append(dict(xbc=np.ascontiguousarray(np.concatenate([halo, seg], axis=1)), h_in=hst[c],
                                 zT=np.ascontiguousarray(z_f[b][g * 512:(g + 1) * 512, sg * SEG:(sg + 1) * SEG]),
                                 dtT=np.ascontiguousarray(dt_f[b][g * 8:(g + 1) * 8, sg * SEG:(sg + 1) * SEG]), cw=cw, cb=_lay(cb_l[ch]),
                                 dtb=pad8(dt_bias[l][g * 8:(g + 1) * 8]), alog=pad8(a_log[l][g * 8:(g + 1) * 8]),
                                 dsk=_lay(np.repeat(f32(d_skip[l][g * 8:(g + 1) * 8]), 64)),
                                 nw=_lay(f32(ssd_norm[l])[g * 512:(g + 1) * 512]), **stab))
            rr = _run(p2b, maps)
            for c in range(NCORES):
                hst[c] = np.ascontiguousarray(rr[c]["o_h"])
                ys_parts[c].append(rr[c]["o_ys"])
        r2b = [dict(o_ys=np.concatenate(ys_parts[c], axis=1)) for c in range(NCORES)]
        ya_f = [np.concatenate([r2a[b * J + g]["o_ya"] for g in range(J)], axis=0) for b in range(B)]
        ys_f = [np.concatenate([r2b[b * J + g]["o_ys"] for g in range(J)], axis=0) for b in range(B)]
        del qk_f, z_f, xbc_f, dt_f, v_f
        w_br_t = np.concatenate([_tile_cols(w_br_attn[l], 256), _tile_cols(w_br_ssd[l], 256), _tile_cols(w_br_mem[l], 256)], axis=2)
        wts = dict(w_kv=np.ascontiguousarray(_tile_cols(w_mem_kv[l], 512)).reshape(4, 128, KC * 512),
                   w_br=np.ascontiguousarray(w_br_t).reshape(8, 128, 32 * 256),
                   w_o=np.ascontiguousarray(_tile_cols(w_out[l], 256)).reshape(8, 128, KC * 256),
                   nw_mem=_lay(norm_mem[l]), nw_post=_lay(norm_mix_post[l]), nw_pre=_lay(norm_ffn_pre[l]))
        maps = []
        for c, (b, j) in enumerate(cores):
            ts = slice(j * T, (j + 1) * T)
            maps.append(dict(xT=xT[c], yaT=np.ascontiguousarray(ya_f[b][:, ts]), ysT=np.ascontiguousarray(ys_f[b][:, ts]),
                             qmT=r1[c]["o_qm"], gT=r1[c]["o_g"], memT=memT[b], **wts))
        r3a = _run(p3a, maps)
        del r1, r2a, r2b, ya_f, ys_f, wts, maps
        cwf = f32(conv_ffn_w[l])
        cw = np.ascontiguousarray(cwf.T.reshape(2 * NF, 128, 3).transpose(1, 0, 2).reshape(128, 2 * NF * 3))
        wu = f32(w_up[l])
        w_up_t = np.stack([_tile_cols(wu[:, :DFF], 256), _tile_cols(wu[:, DFF:], 256)], axis=2)
        wts = dict(w_up=np.ascontiguousarray(w_up_t).reshape(NF // 2, 128, 2 * KC * 256),
                   w_dn=np.ascontiguousarray(_tile_cols(w_down[l], 256)).reshape(8, 128, NF * 256),
                   cw=cw, cb=_lay(conv_ffn_b[l]), nw=_lay(norm_ffn_post[l]))
        del wu, w_up_t
        maps = []
        for c, (b, j) in enumerate(cores):
            h2 = r3a[c]["o_h2"]
            halo = r3a[c - 1]["o_h2"][:, -2:] if j > 0 else np.zeros((D, 2), h2.dtype)
            maps.append(dict(h2T=np.ascontiguousarray(np.concatenate([halo, h2], axis=1)), xmT=r3a[c]["o_xm"], **wts))
        r3b = _run(p3b, maps)
        xT = [np.ascontiguousarray(r3b[c]["o_x"]) for c in range(NCORES)]
        del r3a, r3b, wts, maps
    out = np.empty((B, S, D), np.float32)
    for c, (b, j) in enumerate(cores):
        out[b, j * T:(j + 1) * T, :] = xT[c].T
    return out
```

```python
import contextlib
import numpy as np
import ml_dtypes
import concourse.bass as bass
import concourse.mybir as mybir
from concourse.bass_utils import run_bass_kernel_spmd

F32 = mybir.dt.float32
BF16 = mybir.dt.bfloat16
AF = mybir.ActivationFunctionType
ALU = mybir.AluOpType
AX = mybir.AxisListType
NPBF = ml_dtypes.bfloat16

D = 2048
KC = D // 128
NCORES = 8
EPS = 1e-6
ENGS = ("pe", "act", "dve", "pool", "sp")


class Trk:
    __slots__ = ("W", "R", "name")

    def __init__(self, name=""):
        self.W = {}
        self.R = {}
        self.name = name


class Prog:
    def __init__(self, nc, stack):
        self.nc = nc
        self.stack = stack
        self.ops = {e: [] for e in ENGS}
        self.cnt = {}
        self.seen = {e: {} for e in ENGS}
        self.esem = {}
        for e in ENGS:
            if e != "sp":
                s = stack.enter_context(nc.semaphore("c_" + e))
                self.esem[e] = s
                self.cnt[id(s)] = 0
        self.dsems = {}
        self.semobj = {}
        self.nuniq = 0

    def sbuf(self, shape, dt, name=None):
        self.nuniq += 1
        return self.stack.enter_context(self.nc.sbuf_tensor(name or f"sb{self.nuniq}", list(shape), dt))

    def psum(self, shape, dt, name=None):
        self.nuniq += 1
        return self.stack.enter_context(self.nc.psum_tensor(name or f"ps{self.nuniq}", list(shape), dt))

    def dma_sem(self, key):
        if key not in self.dsems:
            s = self.stack.enter_context(self.nc.semaphore("d_" + str(key)))
            self.dsems[key] = s
            self.cnt[id(s)] = 0
        return self.dsems[key]

    def _waits(self, eng, reads, writes):
        need = {}
        objs = {}
        for t in reads:
            for k, (s, v) in t.W.items():
                if need.get(k, 0) < v:
                    need[k] = v
                    objs[k] = s
        for t in writes:
            for dct in (t.W, t.R):
                for k, (s, v) in dct.items():
                    if need.get(k, 0) < v:
                        need[k] = v
                        objs[k] = s
        out = []
        seen = self.seen[eng]
        own = id(self.esem[eng]) if eng in self.esem else None
        for k, v in need.items():
            if eng == "pe" and k == own:
                continue
            if seen.get(k, 0) >= v:
                continue
            seen[k] = v
            out.append((objs[k], v))
        return out

    def _record(self, sem, val, reads, writes):
        k = id(sem)
        for t in reads:
            t.R[k] = (sem, val)
        for t in writes:
            t.W[k] = (sem, val)

    def op(self, eng, fn, reads=(), writes=()):
        waits = self._waits(eng, reads, writes)
        sem = self.esem[eng]
        self.cnt[id(sem)] += 1
        val = self.cnt[id(sem)]
        self._record(sem, val, reads, writes)
        self.ops[eng].append((waits, fn, sem, 1))

    def dma(self, q, out, in_, reads=(), writes=(), key="d", **kw):
        waits = self._waits(q, reads, writes)
        sem = self.dma_sem(key)
        self.cnt[id(sem)] += 16
        val = self.cnt[id(sem)]
        self._record(sem, val, reads, writes)
        self.ops[q].append((waits, (lambda e: e.dma_start(out=out, in_=in_, **kw)), sem, 16))

    def finish(self, trackers, eng="sp"):
        waits = self._waits(eng, trackers, ())
        self.ops[eng].append((waits, None, None, 0))

    def emit(self):
        nc = self.nc
        allsems = list(self.esem.values()) + list(self.dsems.values())
        with nc.Block() as b0:
            def clr(e):
                for s in allsems:
                    e.sem_clear(s)
            b0.sync(clr)
        with nc.Block() as block:
            def mk(ename):
                def body(e):
                    for waits, fn, sem, inc in self.ops[ename]:
                        for (s, v) in waits:
                            e.wait_ge(s, v)
                        if fn is not None:
                            ins = fn(e)
                            ins.then_inc(sem, inc)
                return body
            block.tensor(mk("pe"))
            block.scalar(mk("act"))
            block.vector(mk("dve"))
            block.gpsimd(mk("pool"))
            block.sync(mk("sp"))


class Rot:
    def __init__(self, bufs):
        self.bufs = bufs
        self.trk = [Trk() for _ in bufs]
        self.i = -1

    def next(self):
        self.i = (self.i + 1) % len(self.bufs)
        return self.bufs[self.i], self.trk[self.i]


class CastTrk:
    def __init__(self):
        self.blocks = []

    def add(self, c0, c1, trk):
        self.blocks.append((c0, c1, trk))

    def get(self, c0=None, c1=None):
        if c0 is None:
            return tuple(t for (_, _, t) in self.blocks)
        return tuple(t for (a, b, t) in self.blocks if a < c1 and c0 < b)


def cast_w_dram(P, src, dst, rows, cols, ct=None, key="wc", rstep=512):
    ct = ct if ct is not None else CastTrk()
    for c0 in range(0, cols, 2048):
        c1 = min(cols, c0 + 2048)
        trk = Trk()
        for r0 in range(0, rows, rstep):
            r1 = min(rows, r0 + rstep)
            P.dma("pool", dst[r0:r1, c0:c1], src[r0:r1, c0:c1], writes=(trk,), key=key)
        ct.add(c0, c1, trk)
    return ct


def rms_stats(P, xt, xt_trk, ncols, ones_f, ones_f_trk, sq_rot, ps_stat, ps_trk, rstd, rstd_trk, eps_sb, nchunks=KC, dim=D):
    for kc in range(nchunks):
        sq, sqt = sq_rot.next()
        P.op("act", (lambda e, sq=sq, kc=kc: e.activation(out=sq[:, :ncols], in_=xt[:, kc, :ncols], func=AF.Square)),
             reads=(xt_trk,), writes=(sqt,))
        P.op("pe", (lambda e, sq=sq, kc=kc: e.matmul(ps_stat[:, :ncols], ones_f[:, :], sq[:, :ncols],
                                                       start=(kc == 0), stop=(kc == nchunks - 1))),
             reads=(sqt, ones_f_trk), writes=(ps_trk,))
    P.op("act", (lambda e: e.activation(out=rstd[:, :ncols], in_=ps_stat[:, :ncols], func=AF.Sqrt,
                                        bias=eps_sb[:, 0:1], scale=1.0 / dim)),
         reads=(ps_trk, ones_f_trk), writes=(rstd_trk,))
    P.op("dve", (lambda e: e.reciprocal(out=rstd[:, :ncols], in_=rstd[:, :ncols])),
         reads=(rstd_trk,), writes=(rstd_trk,))


SEG_Q, SEG_K, SEG_V, SEG_Z, SEG_XBC, SEG_DT, SEG_QM, SEG_G = 0, 1024, 2048, 3072, 5120, 8192, 8224, 9248
N_IN = 15392


def build_p1(T, ncols_total=N_IN):
    nc = bass.Bass("TRN2", target_bir_lowering=False)
    xT = nc.dram_tensor("xT", [D, T], F32, kind="ExternalInput").ap()
    nw = nc.dram_tensor("nw", [128, KC], F32, kind="ExternalInput").ap()
    w_in = nc.dram_tensor("w_in", [D, N_IN], F32, kind="ExternalInput").ap()
    o_qk = nc.dram_tensor("o_qk", [2048, T], BF16, kind="ExternalOutput").ap()
    o_v = nc.dram_tensor("o_v", [T, 1024], BF16, kind="ExternalOutput").ap()
    o_z = nc.dram_tensor("o_z", [2048, T], BF16, kind="ExternalOutput").ap()
    o_xbc = nc.dram_tensor("o_xbc", [3072, T], BF16, kind="ExternalOutput").ap()
    o_dt = nc.dram_tensor("o_dt", [32, T], F32, kind="ExternalOutput").ap()
    o_qm = nc.dram_tensor("o_qm", [1024, T], BF16, kind="ExternalOutput").ap()
    o_g = nc.dram_tensor("o_g", [6144, T], BF16, kind="ExternalOutput").ap()
    w_bf = nc.dram_tensor("w_bf", [D, N_IN], BF16).ap()
    NTG = T // 512
    with contextlib.ExitStack() as st:
        P = Prog(nc, st)
        ct_w = cast_w_dram(P, w_in, w_bf, D, N_IN)

        ones_f = P.sbuf([128, 128], F32)
        t_ones = Trk()
        P.op("pool", lambda e: e.memset(ones_f[:, :], 1.0), writes=(t_ones,))
        eps_sb = P.sbuf([128, 1], F32)
        P.op("pool", lambda e: e.memset(eps_sb[:, :], EPS), writes=(t_ones,))
        nw_sb = P.sbuf([128, KC], F32)
        t_nw = Trk()
        P.dma("sp", nw_sb[:, :], nw[:, :], writes=(t_nw,), key="c0")
        hT = P.sbuf([128, KC, T], BF16)
        t_hT = Trk()
        xt_rot = Rot([P.sbuf([128, KC, 512], F32) for _ in range(1)])
        sq_rot = Rot([P.sbuf([128, 512], F32) for _ in range(3)])
        rstd = P.sbuf([128, 512], F32)
        t_rstd = Trk()
        ps_stat = P.psum([128, 512], F32)
        t_pss = Trk()
        xTv = xT.rearrange("(kc p) t -> p kc t", p=128)
        for tg in range(NTG):
            xt, t_xt = xt_rot.next()
            P.dma("sp", xt[:, :, :], xTv[:, :, tg * 512:(tg + 1) * 512], writes=(t_xt,), key=f"x{xt_rot.i}")
            rms_stats(P, xt, t_xt, 512, ones_f, t_ones, sq_rot, ps_stat, t_pss, rstd, t_rstd, eps_sb)
            for kc in range(KC):
                P.op("dve", (lambda e, xt=xt, kc=kc, tg=tg: e.scalar_tensor_tensor(
                    out=hT[:, kc, tg * 512:(tg + 1) * 512], in0=xt[:, kc, :], scalar=nw_sb[:, kc:kc + 1],
                    in1=rstd[:, :], op0=ALU.mult, op1=ALU.mult)),
                    reads=(t_xt, t_rstd, t_nw), writes=(t_hT,))

        wv = w_bf.rearrange("(kc p) n -> p kc n", p=128)
        w_rot = Rot([P.sbuf([128, KC, 512], BF16) for _ in range(2)])
        ps_rot = Rot([P.psum([128, 512], F32) for _ in range(4)])
        stg_rot = Rot([P.sbuf([128, 2048], BF16) for _ in range(3)])
        stgf_rot = Rot([P.sbuf([128, 2048], F32) for _ in range(1)])
        t_out = Trk("out")
        evac_i = [0]

        def evac(dst, src, rd, wr, func=None):
            if func is not None:
                P.op("act", lambda e: e.activation(out=dst, in_=src, func=func), reads=rd, writes=wr)
                return
            evac_i[0] += 1
            if evac_i[0] % 2 == 0:
                P.op("act", lambda e: e.activation(out=dst, in_=src, func=AF.Copy), reads=rd, writes=wr)
            else:
                P.op("dve", lambda e: e.tensor_copy(out=dst, in_=src), reads=rd, writes=wr)

        def load_w(c0, ncol):
            wt, t_w = w_rot.next()
            P.dma("sp", wt[:, :, :ncol], wv[:, :, c0:c0 + ncol], reads=ct_w.get(c0, c0 + ncol), writes=(t_w,), key=f"w{w_rot.i}")
            return wt, t_w

        def fm_group(c0, ncol, dst, drow0, func=None, f32out=False):
            wt, t_w = load_w(c0, ncol)
            for s0 in range(0, ncol, 128):
                sn = min(128, ncol - s0)
                stg, t_stg = (stgf_rot if f32out else stg_rot).next()
                for tg in range(NTG):
                    ps, t_ps = ps_rot.next()

                    def mm(e, wt=wt, s0=s0, sn=sn, tg=tg, ps=ps):
                        for kc in range(KC):
                            ins = e.matmul(ps[:sn, :], wt[:, kc, s0:s0 + sn], hT[:, kc, tg * 512:(tg + 1) * 512],
                                           start=(kc == 0), stop=(kc == KC - 1))
                        return ins
                    P.op("pe", mm, reads=(t_w, t_hT), writes=(t_ps,))
                    evac(stg[:sn, tg * 512:(tg + 1) * 512], ps[:sn, :], (t_ps,), (t_stg,), func)
                P.dma("pool", dst[drow0 + s0:drow0 + s0 + sn, :], stg[:sn, :T], reads=(t_stg,), writes=(t_out,),
                      key=f"o{(stgf_rot if f32out else stg_rot).i}{int(f32out)}")

        def tm_group(c0, ncol, dst, dcol0):
            wt, t_w = load_w(c0, ncol)
            for tt in range(T // 128):
                ps, t_ps = ps_rot.next()

                def mm(e, wt=wt, tt=tt, ps=ps):
                    for kc in range(KC):
                        ins = e.matmul(ps[:, :ncol], hT[:, kc, tt * 128:(tt + 1) * 128], wt[:, kc, :ncol],
                                       start=(kc == 0), stop=(kc == KC - 1))
                    return ins
                P.op("pe", mm, reads=(t_w, t_hT), writes=(t_ps,))
                stg, t_stg = stg_rot.next()
                evac(stg[:, :ncol], ps[:, :ncol], (t_ps,), (t_stg,))
                P.dma("pool", dst[tt * 128:(tt + 1) * 128, dcol0:dcol0 + ncol], stg[:, :ncol], reads=(t_stg,),
                      writes=(t_out,), key=f"o{stg_rot.i}0")

        for c0 in range(0, 2048, 512):
            fm_group(SEG_Q + c0, 512, o_qk, c0)
        for c0 in range(0, 1024, 512):
            tm_group(SEG_V + c0, 512, o_v, c0)
        for c0 in range(0, 2048, 512):
            fm_group(SEG_Z + c0, 512, o_z, c0)
        for c0 in range(0, 3072, 512):
            fm_group(SEG_XBC + c0, 512, o_xbc, c0)
        fm_group(SEG_DT, 32, o_dt, 0, f32out=True)
        for c0 in range(0, 1024, 512):
            fm_group(SEG_QM + c0, 512, o_qm, c0)
        for c0 in range(0, 6144, 512):
            fm_group(SEG_G + c0, 512, o_g, c0, func=AF.Sigmoid)
        P.finish((t_out,))
        P.emit()
    return nc


def load_vec16(P, dram_ap, key="c0"):
    n = dram_ap.shape[1]
    t = P.sbuf([128, n], F32)
    trk = Trk()
    P.dma("sp", t[:, :], dram_ap[:, :], writes=(trk,), key=key)
    return t, trk


def mm_group(P, ps, ps_trk, pairs, reads, M=128, N=None):
    def mm(e):
        n = len(pairs)
        for i, (l, r) in enumerate(pairs):
            ins = e.matmul(ps, l, r, start=(i == 0), stop=(i == n - 1))
        return ins
    P.op("pe", mm, reads=reads, writes=(ps_trk,))


def build_p3a(T, TG=256):
    nc = bass.Bass("TRN2", target_bir_lowering=False)
    di = lambda n, s, dt=F32: nc.dram_tensor(n, s, dt, kind="ExternalInput").ap()
    xT = di("xT", [D, T])
    yaT = di("yaT", [1024, T], BF16)
    ysT = di("ysT", [2048, T], BF16)
    qmT = di("qmT", [1024, T], BF16)
    gT = di("gT", [6144, T], BF16)
    memT = di("memT", [D, 256])
    nw_mem = di("nw_mem", [128, KC])
    nw_post = di("nw_post", [128, KC])
    nw_pre = di("nw_pre", [128, KC])
    w_kv = di("w_kv", [4, 128, KC * 512])
    w_br = di("w_br", [8, 128, 32 * 256])
    w_o = di("w_o", [8, 128, KC * 256])
    o_xm = nc.dram_tensor("o_xm", [D, T], F32, kind="ExternalOutput").ap()
    o_h2 = nc.dram_tensor("o_h2", [D, T], BF16, kind="ExternalOutput").ap()
    b_kv = nc.dram_tensor("b_kv", [4, 128, KC * 512], BF16).ap()
    b_br = nc.dram_tensor("b_br", [8, 128, 32 * 256], BF16).ap()
    b_o = nc.dram_tensor("b_o", [8, 128, KC * 256], BF16).ap()
    NTG = T // TG
    with contextlib.ExitStack() as st:
        P = Prog(nc, st)
        def flat_cast(src2d, dst2d, n, trk):
            for c0 in range(0, n, 2048):
                c1 = min(n, c0 + 2048)
                P.dma("pool", dst2d[:, c0:c1], src2d[:, c0:c1], writes=(trk,), key="wc")
        ct_kv = [Trk() for _ in range(4)]
        for cg in range(4):
            flat_cast(w_kv[cg], b_kv[cg], KC * 512, ct_kv[cg])
        ct_br = [Trk() for _ in range(8)]
        for cg in range(8):
            flat_cast(w_br[cg], b_br[cg], 32 * 256, ct_br[cg])
        ct_o = [Trk() for _ in range(8)]
        for cg in range(8):
            flat_cast(w_o[cg], b_o[cg], KC * 256, ct_o[cg])
        t_c = Trk("consts")
        ones_f = P.sbuf([128, 128], F32)
        P.op("pool", lambda e: e.memset(ones_f[:, :], 1.0), writes=(t_c,))
        ones_b = P.sbuf([128, 128], BF16)
        P.op("pool", lambda e: e.memset(ones_b[:, :], 1.0), writes=(t_c,))
        eps_sb = P.sbuf([128, 1], F32)
        P.op("pool", lambda e: e.memset(eps_sb[:, :], EPS), writes=(t_c,))
        nwm_sb, t_nwm = load_vec16(P, nw_mem)
        nwpo_sb, t_nwpo = load_vec16(P, nw_post)
        nwpr_sb, t_nwpr = load_vec16(P, nw_pre)
        sq_rot = Rot([P.sbuf([128, 256], F32) for _ in range(3)])
        rstd = P.sbuf([128, 256], F32)
        t_rstd = Trk()
        ps_stat = P.psum([128, 256], F32)
        t_pss = Trk()
        ps_rot = Rot([P.psum([128, 512], F32) for _ in range(5)])

        big_rot = Rot([P.sbuf([128, KC, 256], F32) for _ in range(2)])
        mt_sb, t_mt = big_rot.next()
        P.dma("sp", mt_sb[:, :, :], memT.rearrange("(kc p) m -> p kc m", p=128), writes=(t_mt,), key="big0")
        rms_stats(P, mt_sb, t_mt, 256, ones_f, t_c, sq_rot, ps_stat, t_pss, rstd, t_rstd, eps_sb)
        mnT = P.sbuf([128, KC, 256], BF16)
        t_mn = Trk()
        for kc in range(KC):
            P.op("dve", (lambda e, kc=kc: e.scalar_tensor_tensor(
                out=mnT[:, kc, :], in0=mt_sb[:, kc, :], scalar=nwm_sb[:, kc:kc + 1], in1=rstd[:, :],
                op0=ALU.mult, op1=ALU.mult)), reads=(t_mt, t_rstd, t_nwm), writes=(t_mn,))
        KmT = P.sbuf([128, 8, 256], BF16)
        Vm = P.sbuf([128, 2, 1024], BF16)
        t_kv = Trk()
        wkv_rot = Rot([P.sbuf([128, KC, 512], BF16) for _ in range(2)])
        for cg in range(4):
            wt, t_w = wkv_rot.next()
            P.dma("sp", wt[:, :, :], b_kv[cg].rearrange("p (k j) -> p k j", j=512), reads=(ct_kv[cg],), writes=(t_w,),
                  key=f"wkv{wkv_rot.i}")
            if cg < 2:
                for s in range(4):
                    ps, t_ps = ps_rot.next()
                    mm_group(P, ps[:, :256], t_ps, [(wt[:, kc, s * 128:(s + 1) * 128], mnT[:, kc, :]) for kc in range(KC)],
                             reads=(t_w, t_mn))
                    P.op("act", (lambda e, ps=ps, c=cg * 4 + s: e.activation(out=KmT[:, c, :], in_=ps[:, :256], func=AF.Copy)),
                         reads=(t_ps,), writes=(t_kv,))
            else:
                for mt in range(2):
                    ps, t_ps = ps_rot.next()
                    mm_group(P, ps[:, :], t_ps, [(mnT[:, kc, mt * 128:(mt + 1) * 128], wt[:, kc, :]) for kc in range(KC)],
                             reads=(t_w, t_mn))
                    P.op("act", (lambda e, ps=ps, mt=mt, c0=(cg - 2) * 512: e.activation(
                        out=Vm[:, mt, c0:c0 + 512], in_=ps[:, :], func=AF.Copy)), reads=(t_ps,), writes=(t_kv,))

        qm_sb = P.sbuf([128, 8, TG], BF16); t_qm = Trk()
        ya_sb = P.sbuf([128, 8, TG], BF16); t_ya = Trk()
        ys_sb = P.sbuf([128, 16, TG], BF16); t_ys = Trk()
        ym_sb = P.sbuf([128, 8, TG], BF16); t_ym = Trk()
        mg_sb = P.sbuf([128, KC, TG], BF16); t_mg = Trk()
        pT_rot = Rot([P.sbuf([128, TG], BF16) for _ in range(4)])
        rs_sb = P.sbuf([128, TG], F32); t_rs = Trk()
        g_rot = Rot([P.sbuf([128, 3, 2, TG], BF16) for _ in range(2)])
        wb_rot = Rot([P.sbuf([128, 32, 256], BF16) for _ in range(2)])
        wo_rot = Rot([P.sbuf([128, KC, 256], BF16) for _ in range(2)])
        macc_rot = Rot([P.sbuf([128, TG], F32) for _ in range(2)])
        mtmp_rot = Rot([P.sbuf([128, TG], F32) for _ in range(2)])
        tmp_rot = Rot([P.sbuf([128, TG], F32) for _ in range(2)])
        h2_rot = Rot([P.sbuf([128, KC, TG], BF16) for _ in range(1)])
        t_out = Trk("out")
        xv = xT.rearrange("(kc p) t -> p kc t", p=128)
        oxv = o_xm.rearrange("(kc p) t -> p kc t", p=128)
        ohv = o_h2.rearrange("(kc p) t -> p kc t", p=128)
        gv = gT.rearrange("(g c p) t -> p g c t", p=128, g=3)
        for tg in range(NTG):
            ts = slice(tg * TG, (tg + 1) * TG)
            P.dma("sp", qm_sb[:, :, :], qmT.rearrange("(c p) t -> p c t", p=128)[:, :, ts], writes=(t_qm,), key="qm")
            P.dma("sp", ya_sb[:, :, :], yaT.rearrange("(c p) t -> p c t", p=128)[:, :, ts], writes=(t_ya,), key="ya")
            P.dma("sp", ys_sb[:, :, :], ysT.rearrange("(c p) t -> p c t", p=128)[:, :, ts], writes=(t_ys,), key="ys")
            for hh in range(4):
                pts = []
                for mt in range(2):
                    ps, t_ps = ps_rot.next()
                    mm_group(P, ps[:, :TG], t_ps,
                             [(KmT[:, hh * 2 + dc, mt * 128:(mt + 1) * 128], qm_sb[:, hh * 2 + dc, :]) for dc in range(2)],
                             reads=(t_kv, t_qm))
                    pT, t_pT = pT_rot.next()
                    P.op("act", (lambda e, ps=ps, pT=pT: e.activation(out=pT[:, :], in_=ps[:, :TG], func=AF.Exp, scale=1.0 / 16.0)),
                         reads=(t_ps,), writes=(t_pT,))
                    pts.append((pT, t_pT))
                ps, t_ps = ps_rot.next()
                mm_group(P, ps[:, :TG], t_ps, [(ones_b[:, :], pT[:, :]) for (pT, _) in pts],
                         reads=(t_c,) + tuple(t for _, t in pts))
                P.op("dve", (lambda e, ps=ps: e.reciprocal(out=rs_sb[:, :], in_=ps[:, :TG])), reads=(t_ps,), writes=(t_rs,))
                for dc in range(2):
                    ps, t_ps = ps_rot.next()
                    mm_group(P, ps[:, :TG], t_ps,
                             [(Vm[:, mt, hh * 256 + dc * 128: hh * 256 + (dc + 1) * 128], pts[mt][0][:, :]) for mt in range(2)],
                             reads=(t_kv,) + tuple(t for _, t in pts))
                    P.op("dve", (lambda e, ps=ps, c=hh * 2 + dc: e.tensor_tensor(out=ym_sb[:, c, :], in0=ps[:, :TG], in1=rs_sb[:, :],
                                                                                op=ALU.mult)),
                         reads=(t_ps, t_rs), writes=(t_ym,))
            for cg in range(8):
                wb, t_w = wb_rot.next()
                cs = slice(cg * 256, (cg + 1) * 256)
                k = f"wb{wb_rot.i}"
                P.dma("sp", wb[:, :, :], b_br[cg].rearrange("p (k j) -> p k j", j=256), reads=(ct_br[cg],), writes=(t_w,), key=k)
                gt, t_g = g_rot.next()
                for g3 in range(3):
                    P.dma("sp", gt[:, g3, :, :], gv[:, g3, cg * 2:cg * 2 + 2, ts], writes=(t_g,), key=f"g{g_rot.i}")
                for ct in range(2):
                    c = cg * 2 + ct
                    macc, t_ma = macc_rot.next()
                    specs = ((0, 8, ya_sb, t_ya, 0), (8, 16, ys_sb, t_ys, 1), (24, 8, ym_sb, t_ym, 2))
                    for bi, (k0, nk, src, t_src, g3) in enumerate(specs):
                        ps, t_ps = ps_rot.next()
                        mm_group(P, ps[:, :TG], t_ps,
                                 [(wb[:, k0 + kc, ct * 128:(ct + 1) * 128], src[:, kc, :]) for kc in range(nk)],
                                 reads=(t_w, t_src))
                        if bi == 0:
                            P.op("dve", (lambda e, ps=ps, macc=macc, gt=gt, g3=g3, ct=ct: e.tensor_tensor(
                                out=macc[:, :], in0=ps[:, :TG], in1=gt[:, g3, ct, :], op=ALU.mult)),
                                reads=(t_ps, t_g), writes=(t_ma,))
                        else:
                            mtmp, t_mtmp = mtmp_rot.next()
                            P.op("dve", (lambda e, ps=ps, mtmp=mtmp, gt=gt, g3=g3, ct=ct: e.tensor_tensor(
                                out=mtmp[:, :], in0=ps[:, :TG], in1=gt[:, g3, ct, :], op=ALU.mult)),
                                reads=(t_ps, t_g), writes=(t_mtmp,))
                            last = (bi == 2)
                            dst = mg_sb[:, c, :] if last else macc[:, :]
                            P.op("pool", (lambda e, mtmp=mtmp, macc=macc, dst=dst: e.tensor_tensor(
                                out=dst, in0=mtmp[:, :], in1=macc[:, :], op=ALU.add)),
                                reads=(t_mtmp, t_ma), writes=((t_mg,) if last else (t_ma,)))
            yt, t_yt = big_rot.next()
            for cg in range(8):
                wo, t_w = wo_rot.next()
                P.dma("sp", wo[:, :, :], b_o[cg].rearrange("p (k j) -> p k j", j=256), reads=(ct_o[cg],), writes=(t_w,), key=f"wo{wo_rot.i}")
                for ct in range(2):
                    c = cg * 2 + ct
                    ps, t_ps = ps_rot.next()
                    mm_group(P, ps[:, :TG], t_ps, [(wo[:, kc, ct * 128:(ct + 1) * 128], mg_sb[:, kc, :]) for kc in range(KC)],
                             reads=(t_w, t_mg))
                    P.op("act", (lambda e, ps=ps, c=c, yt=yt: e.activation(out=yt[:, c, :], in_=ps[:, :TG], func=AF.Copy)),
                         reads=(t_ps,), writes=(t_yt,))
            xt, t_xt = big_rot.next()
            P.dma("sp", xt[:, :, :], xv[:, :, ts], writes=(t_xt,), key=f"big{big_rot.i}")
            rms_stats(P, yt, t_yt, TG, ones_f, t_c, sq_rot, ps_stat, t_pss, rstd, t_rstd, eps_sb)
            for kc in range(KC):
                tmp, t_tmp = tmp_rot.next()
                P.op("dve", (lambda e, kc=kc, tmp=tmp, yt=yt: e.scalar_tensor_tensor(
                    out=tmp[:, :], in0=yt[:, kc, :], scalar=nwpo_sb[:, kc:kc + 1], in1=rstd[:, :], op0=ALU.mult, op1=ALU.mult)),
                    reads=(t_yt, t_rstd, t_nwpo), writes=(t_tmp,))
                P.op("pool", (lambda e, kc=kc, tmp=tmp, xt=xt: e.tensor_tensor(out=xt[:, kc, :], in0=tmp[:, :], in1=xt[:, kc, :],
                                                                               op=ALU.add)),
                     reads=(t_tmp, t_xt), writes=(t_xt,))
            P.dma("pool", oxv[:, :, ts], xt[:, :, :], reads=(t_xt,), writes=(t_out,), key="oxm")
            rms_stats(P, xt, t_xt, TG, ones_f, t_c, sq_rot, ps_stat, t_pss, rstd, t_rstd, eps_sb)
            h2, t_h2 = h2_rot.next()
            for kc in range(KC):
                P.op("dve", (lambda e, kc=kc, xt=xt, h2=h2: e.scalar_tensor_tensor(
                    out=h2[:, kc, :], in0=xt[:, kc, :], scalar=nwpr_sb[:, kc:kc + 1], in1=rstd[:, :], op0=ALU.mult, op1=ALU.mult)),
                    reads=(t_xt, t_rstd, t_nwpr), writes=(t_h2,))
            P.dma("pool", ohv[:, :, ts], h2[:, :, :], reads=(t_h2,), writes=(t_out,), key="oh2")
        P.finish((t_out,))
        P.emit()
    return nc


DFF = 5632
NF = DFF // 128
GELU_C = 0.7978845608028654


def build_p3b(T, TG=256, gelu_native=True):
    nc = bass.Bass("TRN2", target_bir_lowering=False)
    di = lambda n, s, dt=F32: nc.dram_tensor(n, s, dt, kind="ExternalInput").ap()
    h2T = di("h2T", [D, T + 2], BF16)
    xmT = di("xmT", [D, T])
    w_up = di("w_up", [NF // 2, 128, 2 * KC * 256])
    w_dn = di("w_dn", [8, 128, NF * 256])
    cw = di("cw", [128, 2 * NF * 3])
    cb = di("cb", [128, 2 * NF])
    nw = di("nw", [128, KC])
    o_x = nc.dram_tensor("o_x", [D, T], F32, kind="ExternalOutput").ap()
    b_up = nc.dram_tensor("b_up", [NF // 2, 128, 2, KC, 256], BF16).ap()
    b_dn = nc.dram_tensor("b_dn", [8, 128, NF, 256], BF16).ap()
    NTG = T // TG
    NE = TG + 2
    with contextlib.ExitStack() as st:
        P = Prog(nc, st)
        def flat_cast(src2d, dst2d, n, trk):
            for c0 in range(0, n, 2048):
                c1 = min(n, c0 + 2048)
                P.dma("pool", dst2d[:, c0:c1], src2d[:, c0:c1], writes=(trk,), key="wc")
        ct_up = {}
        for fg in range(NF // 2):
            ct_up[fg] = Trk()
            flat_cast(w_up[fg], b_up[fg].rearrange("p h k j -> p (h k j)"), 2 * KC * 256, ct_up[fg])
        t_cdn = Trk()
        for cg in range(8):
            flat_cast(w_dn[cg], b_dn[cg].rearrange("p f j -> p (f j)"), NF * 256, t_cdn)
        t_c = Trk("consts")
        ones_f = P.sbuf([128, 128], F32)
        P.op("pool", lambda e: e.memset(ones_f[:, :], 1.0), writes=(t_c,))
        eps_sb = P.sbuf([128, 1], F32)
        P.op("pool", lambda e: e.memset(eps_sb[:, :], EPS), writes=(t_c,))
        cw_sb, t_cw = load_vec16(P, cw)
        cb_sb, t_cb = load_vec16(P, cb)
        nw_sb, t_nw = load_vec16(P, nw)
        sq_rot = Rot([P.sbuf([128, TG], F32) for _ in range(3)])
        rstd = P.sbuf([128, TG], F32); t_rstd = Trk()
        ps_stat = P.psum([128, TG], F32); t_pss = Trk()
        ps_rot = Rot([P.psum([128, 512], F32) for _ in range(6)])
        big_rot = Rot([P.sbuf([128, KC, TG], F32) for _ in range(2)])
        h2_rot = Rot([P.sbuf([128, KC, NE], BF16) for _ in range(2)])
        act_sb = P.sbuf([128, NF, TG], BF16); t_act = Trk()
        wu_rot = Rot([P.sbuf([128, 2, KC, 256], BF16) for _ in range(3)])
        wd_rot = Rot([P.sbuf([128, NF, 256], BF16) for _ in range(2)])
        u_rot = Rot([P.sbuf([128, NE], F32) for _ in range(4)])
        t_rot = Rot([P.sbuf([128, TG], F32) for _ in range(4)])
        ga_rot = Rot([P.sbuf([128, TG], F32) for _ in range(2)])
        tmp_rot = Rot([P.sbuf([128, TG], F32) for _ in range(2)])
        t_out = Trk("out")
        hv = h2T.rearrange("(kc p) t -> p kc t", p=128)
        xv = xmT.rearrange("(kc p) t -> p kc t", p=128)
        ov = o_x.rearrange("(kc p) t -> p kc t", p=128)

        def conv_chain(ps, t_ps, ch):
            u, t_u = u_rot.next()
            P.op("act", (lambda e: e.activation(out=u[:, :], in_=ps[:, :NE], func=AF.Copy)), reads=(t_ps,), writes=(t_u,))
            t, t_t = t_rot.next()
            P.op("pool", (lambda e: e.tensor_scalar(out=t[:, :], in0=u[:, 0:TG], scalar1=cw_sb[:, ch * 3:ch * 3 + 1],
                                                    scalar2=cb_sb[:, ch:ch + 1], op0=ALU.mult, op1=ALU.add)),
                 reads=(t_u, t_cw, t_cb), writes=(t_t,))
            for k in (1, 2):
                P.op("dve", (lambda e, k=k: e.scalar_tensor_tensor(out=t[:, :], in0=u[:, k:k + TG],
                                                                   scalar=cw_sb[:, ch * 3 + k:ch * 3 + k + 1], in1=t[:, :],
                                                                   op0=ALU.mult, op1=ALU.add)),
                     reads=(t_u, t_cw, t_t), writes=(t_t,))
            return t, t_t

        for tg in range(NTG):
            ts = slice(tg * TG, (tg + 1) * TG)
            h2, t_h2 = h2_rot.next()
            P.dma("sp", h2[:, :, :], hv[:, :, tg * TG:tg * TG + NE], writes=(t_h2,), key=f"h2{h2_rot.i}")
            for fg in range(NF // 2):
                wu, t_w = wu_rot.next()
                k = f"wu{wu_rot.i}"
                P.dma("sp", wu[:, :, :, :], b_up[fg], reads=(ct_up[fg],), writes=(t_w,), key=k)
                for ft in range(2):
                    f = fg * 2 + ft
                    res = []
                    for half in range(2):
                        ps, t_ps = ps_rot.next()
                        mm_group(P, ps[:, :NE], t_ps, [(wu[:, half, kc, ft * 128:(ft + 1) * 128], h2[:, kc, :]) for kc in range(KC)],
                                 reads=(t_w, t_h2))
                        res.append(conv_chain(ps, t_ps, half * NF + f))
                    (ta, t_ta), (tgg, t_tg) = res
                    ga, t_ga = ga_rot.next()
                    if gelu_native:
                        P.op("act", (lambda e, ta=ta, ga=ga: e.activation(out=ga[:, :], in_=ta[:, :], func=AF.Gelu_apprx_tanh)),
                             reads=(t_ta,), writes=(t_ga,))
                    else:
                        P.op("act", (lambda e, ta=ta, ga=ga: e.activation(out=ga[:, :], in_=ta[:, :], func=AF.Square)),
                             reads=(t_ta,), writes=(t_ga,))
                        P.op("pool", (lambda e, ga=ga: e.tensor_scalar(out=ga[:, :], in0=ga[:, :], scalar1=0.044715, scalar2=1.0,
                                                                      op0=ALU.mult, op1=ALU.add)), reads=(t_ga,), writes=(t_ga,))
                        P.op("pool", (lambda e, ta=ta, ga=ga: e.tensor_tensor(out=ga[:, :], in0=ga[:, :], in1=ta[:, :], op=ALU.mult)),
                             reads=(t_ga, t_ta), writes=(t_ga,))
                        P.op("act", (lambda e, ga=ga: e.activation(out=ga[:, :], in_=ga[:, :], func=AF.Sigmoid, scale=2.0 * GELU_C)),
                             reads=(t_ga,), writes=(t_ga,))
                        P.op("pool", (lambda e, ta=ta, ga=ga: e.tensor_tensor(out=ga[:, :], in0=ga[:, :], in1=ta[:, :], op=ALU.mult)),
                             reads=(t_ga, t_ta), writes=(t_ga,))
                    P.op("dve", (lambda e, ga=ga, tgg=tgg, f=f: e.tensor_tensor(out=act_sb[:, f, :], in0=ga[:, :], in1=tgg[:, :],
                                                                                op=ALU.mult)),
                         reads=(t_ga, t_tg), writes=(t_act,))
            yt, t_yt = big_rot.next()
            for cg in range(8):
                wd, t_w = wd_rot.next()
                P.dma("sp", wd[:, :, :], b_dn[cg], reads=(t_cdn,), writes=(t_w,), key=f"wd{wd_rot.i}")
                for ct in range(2):
                    c = cg * 2 + ct
                    ps, t_ps = ps_rot.next()
                    mm_group(P, ps[:, :TG], t_ps, [(wd[:, f, ct * 128:(ct + 1) * 128], act_sb[:, f, :]) for f in range(NF)],
                             reads=(t_w, t_act))
                    P.op("act", (lambda e, ps=ps, c=c, yt=yt: e.activation(out=yt[:, c, :], in_=ps[:, :TG], func=AF.Copy)),
                         reads=(t_ps,), writes=(t_yt,))
            xt, t_xt = big_rot.next()
            P.dma("sp", xt[:, :, :], xv[:, :, ts], writes=(t_xt,), key=f"big{big_rot.i}")
            rms_stats(P, yt, t_yt, TG, ones_f, t_c, sq_rot, ps_stat, t_pss, rstd, t_rstd, eps_sb)
            for kc in range(KC):
                tmp, t_tmp = tmp_rot.next()
                P.op("dve", (lambda e, kc=kc, tmp=tmp, yt=yt: e.scalar_tensor_tensor(
                    out=tmp[:, :], in0=yt[:, kc, :], scalar=nw_sb[:, kc:kc + 1], in1=rstd[:, :], op0=ALU.mult, op1=ALU.mult)),
                    reads=(t_yt, t_rstd, t_nw), writes=(t_tmp,))
                P.op("pool", (lambda e, kc=kc, tmp=tmp, xt=xt: e.tensor_tensor(out=xt[:, kc, :], in0=tmp[:, :], in1=xt[:, kc, :],
                                                                               op=ALU.add)),
                     reads=(t_tmp, t_xt), writes=(t_xt,))
            P.dma("pool", ov[:, :, ts], xt[:, :, :], reads=(t_xt,), writes=(t_out,), key="ox")
        P.finish((t_out,))
        P.emit()
    return nc


BIGR = 3000.0
NEGF = -1.0e30


def attn_tables(S):
    nb = S // 256
    j = np.arange(nb)[None, :]
    n = np.arange(nb)[:, None]
    past = (j < n)
    pastb = np.where(past, 0.0, NEGF).astype(np.float32).reshape(1, nb * nb)
    past01 = past.astype(np.float32).reshape(1, nb * nb)
    own01 = (j == n).astype(np.float32).reshape(1, nb * nb)
    rep = lambda a: np.ascontiguousarray(np.broadcast_to(a, (128, a.shape[1])))
    k = np.arange(128)[:, None]
    q = np.arange(256)[None, :]
    cmA = np.where(k <= q, 0.0, -BIGR)
    cmB = np.where(128 + k <= q, 0.0, -BIGR)
    cm = np.concatenate([cmA, cmB], axis=1).astype(NPBF)
    oh = np.zeros((32, nb, 128), np.float32)
    for jj in range(nb):
        oh[jj, jj, :] = 1.0
    half = 64
    inv = (10000.0 ** (-np.arange(half, dtype=np.float32) / half)).astype(np.float32)
    ang = np.arange(S, dtype=np.float32)[None, :] * inv[:, None]
    cos = np.cos(ang).astype(np.float32)
    sin = np.sin(ang).astype(np.float32)
    cosT = np.concatenate([cos, cos], axis=0)
    sinT = np.concatenate([-sin, sin], axis=0)
    return dict(pastb=rep(pastb), past01=rep(past01), own01=rep(own01), cm=np.ascontiguousarray(cm),
                onehot=np.ascontiguousarray(oh.reshape(32, nb * 128).astype(NPBF)),
                ident_f=np.eye(128, dtype=np.float32), ident_b=np.eye(128, dtype=np.float32).astype(NPBF),
                cosT=np.ascontiguousarray(cosT), sinT=np.ascontiguousarray(sinT))


def build_p2a(S):
    nb = S // 256
    NT = S // 128
    RC = min(S, 2048)
    nc = bass.Bass("TRN2", target_bir_lowering=False)
    di = lambda n, s, dt=F32: nc.dram_tensor(n, s, dt, kind="ExternalInput").ap()
    qk = di("qk", [4, 128, S], BF16)
    qks = di("qks", [4, 128, S], BF16)
    v = di("v", [2, S, 128], BF16)
    cosT = di("cosT", [128, S])
    sinT = di("sinT", [128, S])
    pastb = di("pastb", [128, nb * nb])
    past01 = di("past01", [128, nb * nb])
    own01 = di("own01", [128, nb * nb])
    cm = di("cm", [128, 512], BF16)
    onehot = di("onehot", [32, nb * 128], BF16)
    ident_f = di("ident_f", [128, 128])
    ident_b = di("ident_b", [128, 128], BF16)
    o_ya = nc.dram_tensor("o_ya", [256, S], BF16, kind="ExternalOutput").ap()
    scale = 128.0 ** -0.5
    with contextlib.ExitStack() as st:
        P = Prog(nc, st)
        t_c = Trk("consts")

        def cload(ap, shape, dt):
            t = P.sbuf(shape, dt)
            P.dma("sp", t[:, :], ap[:, :], writes=(t_c,), key="c0")
            return t
        pastb_sb = cload(pastb, [128, nb * nb], F32)
        past01_sb = cload(past01, [128, nb * nb], F32)
        own01_sb = cload(own01, [128, nb * nb], F32)
        cm_sb = cload(cm, [128, 512], BF16)
        oh_sb = cload(onehot, [32, nb * 128], BF16)
        idf_sb = cload(ident_f, [128, 128], F32)
        idb_sb = cload(ident_b, [128, 128], BF16)
        ones_b = P.sbuf([128, 128], BF16)
        P.op("pool", lambda e: e.memset(ones_b[:, :], 1.0), writes=(t_c,))

        QR = P.sbuf([128, S], BF16); t_QR = Trk()
        KR = P.sbuf([128, S], BF16); t_KR = Trk()
        V = P.sbuf([128, NT, 128], BF16); t_V = Trk()
        maskbT = P.sbuf([32, S], BF16); t_mb = Trk()
        outb = P.sbuf([128, S], BF16); t_ob = Trk()
        raw_rot = Rot([P.sbuf([128, 2, RC], BF16) for _ in range(2)])
        cs_sb = P.sbuf([128, 2, RC], F32); t_cs = Trk()
        r1_rot = Rot([P.sbuf([128, RC], F32) for _ in range(1)])
        r2_rot = Rot([P.sbuf([128, RC], F32) for _ in range(1)])
        kmf = P.sbuf([128, nb], F32); t_kmf = Trk()
        kmT = P.sbuf([128, nb], BF16); t_km = Trk()
        gm_rot = Rot([P.sbuf([128, nb], F32) for _ in range(2)])
        m8_rot = Rot([P.sbuf([128, 8], F32) for _ in range(2)])
        sel_rot = Rot([P.sbuf([128, 32], F32) for _ in range(2)])
        pT_rot = Rot([P.sbuf([128, 512], BF16) for _ in range(4)])
        rs_rot = Rot([P.sbuf([128, 512], F32) for _ in range(2)])
        ps_s_rot = Rot([P.psum([128, 512], F32) for _ in range(3)])
        ps_o_rot = Rot([P.psum([128, 512], F32) for _ in range(2)])
        ps_m_rot = Rot([P.psum([128, 512], F32) for _ in range(2)])
        ps_g = P.psum([128, 512], F32); t_psg = Trk()
        t_out = Trk("out")
        if nb < 32:
            for b in sel_rot.bufs:
                P.op("pool", (lambda e, b=b: e.memset(b[:, :], 0.0)), writes=(t_c,))

        for h in range(2):
            for which, dst, t_dst in ((0, QR, t_QR), (1, KR, t_KR)):
                src = which * 2 + h
                for c0 in range(0, S, RC):
                    raw, t_raw = raw_rot.next()
                    kk = f"raw{raw_rot.i}"
                    P.dma("sp", raw[:, 0, :], qk[src, :, c0:c0 + RC], writes=(t_raw,), key=kk)
                    P.dma("sp", raw[:, 1, :], qks[src, :, c0:c0 + RC], writes=(t_raw,), key=kk)
                    P.dma("sp", cs_sb[:, 0, :], cosT[:, c0:c0 + RC], writes=(t_cs,), key="cs")
                    P.dma("sp", cs_sb[:, 1, :], sinT[:, c0:c0 + RC], writes=(t_cs,), key="cs")
                    r1, t_r1 = r1_rot.next()
                    r2, t_r2 = r2_rot.next()
                    P.op("dve", (lambda e, raw=raw, r1=r1: e.tensor_tensor(out=r1[:, :], in0=raw[:, 0, :], in1=cs_sb[:, 0, :], op=ALU.mult)),
                         reads=(t_raw, t_cs), writes=(t_r1,))
                    P.op("pool", (lambda e, raw=raw, r2=r2: e.tensor_tensor(out=r2[:, :], in0=raw[:, 1, :], in1=cs_sb[:, 1, :], op=ALU.mult)),
                         reads=(t_raw, t_cs), writes=(t_r2,))
                    P.op("dve", (lambda e, r1=r1, r2=r2, dst=dst, c0=c0: e.tensor_tensor(out=dst[:, c0:c0 + RC], in0=r1[:, :], in1=r2[:, :],
                                                                                        op=ALU.add)),
                         reads=(t_r1, t_r2), writes=(t_dst,))
            P.dma("sp", V[:, :, :], v[h].rearrange("(n p) d -> p n d", p=128), writes=(t_V,), key="v")
            P.op("dve", (lambda e: e.tensor_reduce(out=kmf[:, :], in_=KR[:, :].rearrange("p (n k) -> p n k", k=256), axis=AX.X, op=ALU.add)),
                 reads=(t_KR,), writes=(t_kmf,))
            P.op("act", (lambda e: e.activation(out=kmT[:, :], in_=kmf[:, :], func=AF.Copy, scale=1.0 / 256.0)),
                 reads=(t_kmf,), writes=(t_km,))
            for qt in range(NT):
                n = qt // 2
                P.op("pe", (lambda e, qt=qt: e.matmul(ps_g[:, :nb], QR[:, qt * 128:(qt + 1) * 128], kmT[:, :], start=True, stop=True)),
                     reads=(t_QR, t_km), writes=(t_psg,))
                gm, t_gm = gm_rot.next()
                P.op("dve", (lambda e, gm=gm, n=n: e.tensor_tensor(out=gm[:, :], in0=ps_g[:, :nb], in1=pastb_sb[:, n * nb:(n + 1) * nb],
                                                                   op=ALU.add)), reads=(t_psg, t_c), writes=(t_gm,))
                m8, t_m8 = m8_rot.next()
                if nb >= 8:
                    P.op("dve", (lambda e, gm=gm, m8=m8: e.max(out=m8[:, :], in_=gm[:, :])), reads=(t_gm,), writes=(t_m8,))
                else:
                    raise NotImplementedError
                sel, t_sel = sel_rot.next()
                P.op("dve", (lambda e, gm=gm, m8=m8, sel=sel: e.tensor_scalar(out=sel[:, :nb], in0=gm[:, :], scalar1=m8[:, 2:3], scalar2=None,
                                                                              op0=ALU.is_ge)), reads=(t_gm, t_m8), writes=(t_sel,))
                P.op("dve", (lambda e, sel=sel, n=n: e.tensor_tensor(out=sel[:, :nb], in0=sel[:, :nb], in1=past01_sb[:, n * nb:(n + 1) * nb],
                                                                     op=ALU.mult)), reads=(t_sel, t_c), writes=(t_sel,))
                P.op("dve", (lambda e, sel=sel, n=n: e.tensor_tensor(out=sel[:, :nb], in0=sel[:, :nb], in1=own01_sb[:, n * nb:(n + 1) * nb],
                                                                     op=ALU.add)), reads=(t_sel, t_c), writes=(t_sel,))
                P.op("dve", (lambda e, sel=sel: e.tensor_scalar(out=sel[:, :nb], in0=sel[:, :nb], scalar1=BIGR, scalar2=-BIGR,
                                                                op0=ALU.mult, op1=ALU.add)), reads=(t_sel,), writes=(t_sel,))
                P.op("pe", (lambda e, sel=sel: e.transpose(ps_g[:32, 256:384], sel[:, :], idf_sb[:, :])),
                     reads=(t_sel, t_c, t_gm), writes=(t_psg,))
                P.op("act", (lambda e, qt=qt: e.activation(out=maskbT[:, qt * 128:(qt + 1) * 128], in_=ps_g[:32, 256:384], func=AF.Copy)),
                     reads=(t_psg,), writes=(t_mb,))
            for m in range(nb // 2):
                n0 = 2 * m
                q0 = n0 * 256
                ps_o, t_po = ps_o_rot.next()
                ps_m, t_pm = ps_m_rot.next()
                nkt = 2 * n0 + 4
                LA = 2
                sc_tiles = {}

                def emit_score(kt, n0=n0, q0=q0):
                    j = kt // 2
                    c0 = 256 if j == n0 + 1 else 0
                    ps_s, t_pss = ps_s_rot.next()

                    def mm(e, kt=kt, j=j, c0=c0, ps_s=ps_s):
                        e.matmul(ps_s[:, c0:512], KR[:, kt * 128:(kt + 1) * 128], QR[:, q0 + c0:q0 + 512], start=True, stop=False)
                        last = (j < n0)
                        ins = e.matmul(ps_s[:, c0:512], oh_sb[:, j * 128:(j + 1) * 128], maskbT[:, q0 + c0:q0 + 512], start=False, stop=last)
                        if not last:
                            cc = 0 if j == n0 else 256
                            ins = e.matmul(ps_s[:, cc:cc + 256], idb_sb[:, :], cm_sb[:, (kt % 2) * 256:(kt % 2 + 1) * 256], start=False, stop=True)
                        return ins
                    P.op("pe", mm, reads=(t_KR, t_QR, t_mb, t_c), writes=(t_pss,))
                    sc_tiles[kt] = (ps_s, t_pss, c0)
                for kt in range(min(LA, nkt)):
                    emit_score(kt)
                for kt in range(nkt):
                    ps_s, t_pss, c0 = sc_tiles.pop(kt)
                    pT, t_pT = pT_rot.next()
                    P.op("act", (lambda e, ps_s=ps_s, pT=pT, c0=c0: e.activation(out=pT[:, c0:512], in_=ps_s[:, c0:512], func=AF.Exp, scale=scale)),
                         reads=(t_pss,), writes=(t_pT,))
                    if kt + LA < nkt:
                        emit_score(kt + LA)

                    def mm2(e, kt=kt, pT=pT, ps_o=ps_o, ps_m=ps_m, nkt=nkt, c0=c0):
                        e.matmul(ps_o[:, c0:512], V[:, kt, :], pT[:, c0:512], start=(kt == 0), stop=(kt == nkt - 1))
                        return e.matmul(ps_m[:, c0:512], ones_b[:, :], pT[:, c0:512], start=(kt == 0), stop=(kt == nkt - 1))
                    P.op("pe", mm2, reads=(t_V, t_pT, t_c), writes=(t_po, t_pm))
                rs, t_rs = rs_rot.next()
                P.op("dve", (lambda e, rs=rs, ps_m=ps_m: e.reciprocal(out=rs[:, :], in_=ps_m[:, :])), reads=(t_pm,), writes=(t_rs,))
                P.op("dve", (lambda e, rs=rs, ps_o=ps_o, q0=q0: e.tensor_tensor(out=outb[:, q0:q0 + 512], in0=ps_o[:, :], in1=rs[:, :], op=ALU.mult)),
                     reads=(t_po, t_rs), writes=(t_ob,))
            P.dma("pool", o_ya[h * 128:(h + 1) * 128, :], outb[:, :], reads=(t_ob,), writes=(t_out,), key="oya")
        P.finish((t_out,))
        P.emit()
    return nc


def ssd_tables():
    oh = np.zeros((128, 8, 128), np.float32)
    for h in range(8):
        oh[h, h, :] = 1.0
    s = np.arange(128)[:, None]
    l = np.arange(128)[None, :]
    tri = (l >= s).astype(np.float32)
    return dict(oh8=np.ascontiguousarray(oh.reshape(128, 1024)), tri=tri, ident_f=np.eye(128, dtype=np.float32),
                ident_b=np.eye(128, dtype=np.float32).astype(NPBF))


def build_p2b(S, SC=512, pipe=1):
    nc = bass.Bass("TRN2", target_bir_lowering=False)
    di = lambda n, s, dt=F32: nc.dram_tensor(n, s, dt, kind="ExternalInput").ap()
    xbc = di("xbc", [768, S + 4], BF16)
    h_in = di("h_in", [128, 512])
    zT = di("zT", [512, S], BF16)
    dtT = di("dtT", [8, S])
    cw = di("cw", [128, 24])
    cb = di("cb", [128, 6])
    dtb = di("dtb", [128, 2])
    alog = di("alog", [128, 2])
    dsk = di("dsk", [128, 4])
    nw = di("nw", [128, 4])
    oh8 = di("oh8", [128, 1024])
    tri = di("tri", [128, 128])
    ident_f = di("ident_f", [128, 128])
    ident_b = di("ident_b", [128, 128], BF16)
    o_ys = nc.dram_tensor("o_ys", [512, S], BF16, kind="ExternalOutput").ap()
    o_h = nc.dram_tensor("o_h", [128, 512], F32, kind="ExternalOutput").ap()
    SC = min(SC, S)
    NSC = S // SC
    NCH = SC // 128
    with contextlib.ExitStack() as st:
        P = Prog(nc, st)
        t_c = Trk("consts")

        def cload(ap, shape, dt):
            t = P.sbuf(shape, dt)
            P.dma("sp", t[:, :], ap[:, :], writes=(t_c,), key="c0")
            return t
        cw_sb = cload(cw, [128, 24], F32)
        cb_sb = cload(cb, [128, 6], F32)
        dtb_sb = cload(dtb, [128, 2], F32)
        alog_sb = cload(alog, [128, 2], F32)
        dsk_sb = cload(dsk, [128, 4], F32)
        nw_sb = cload(nw, [128, 4], F32)
        oh8_sb = cload(oh8, [128, 1024], F32)
        tri_sb = cload(tri, [128, 128], F32)
        idf_sb = cload(ident_f, [128, 128], F32)
        idb_sb = cload(ident_b, [128, 128], BF16)
        ones_f = P.sbuf([128, 128], F32)
        P.op("pool", lambda e: e.memset(ones_f[:, :], 1.0), writes=(t_c,))
        eps_sb = P.sbuf([128, 1], F32)
        P.op("pool", lambda e: e.memset(eps_sb[:, :], EPS), writes=(t_c,))
        one_sb = P.sbuf([128, 1], F32)
        P.op("pool", lambda e: e.memset(one_sb[:, :], 1.0), writes=(t_c,))
        zero_sb = P.sbuf([128, 1], F32)
        P.op("pool", lambda e: e.memset(zero_sb[:, :], 0.0), writes=(t_c,))
        A_sb = P.sbuf([128, 2], F32)
        P.op("act", lambda e: e.activation(out=A_sb[:, :], in_=alog_sb[:, :], func=AF.Exp), reads=(t_c,), writes=(t_c,))
        P.op("dve", lambda e: e.tensor_scalar(out=A_sb[:, :], in0=A_sb[:, :], scalar1=-1.0, scalar2=None, op0=ALU.mult),
             reads=(t_c,), writes=(t_c,))

        prep_rot = Rot([dict(raw=P.sbuf([128, 6, SC + 4], BF16), zr=P.sbuf([128, 4, SC], BF16), dtr=P.sbuf([8, SC], F32),
                             xc=P.sbuf([128, 6, SC], BF16), sz=P.sbuf([128, 4, SC], BF16), dts=P.sbuf([128, SC], F32),
                             aT=P.sbuf([8, SC], F32), t_raw=Trk(), t_zr=Trk(), t_dtr=Trk(), t_xc=Trk(), t_sz=Trk(),
                             t_dts=Trk(), t_aT=Trk()) for _ in range(2)])
        ct_rot = Rot([P.sbuf([128, SC], F32) for _ in range(2)])
        acs_rot = Rot([P.sbuf([128, 128], F32) for _ in range(2)])
        datm_rot = Rot([P.sbuf([128, 16], F32) for _ in range(2)])
        E_rot = Rot([P.sbuf([128, 8, 128], F32) for _ in range(2)])
        Dp_rot = Rot([P.sbuf([128, 8, 128], F32) for _ in range(2)])
        cbm_rot = Rot([P.sbuf([128, 128], F32) for _ in range(2)])
        MT_rot = Rot([P.sbuf([128, 8, 128], BF16) for _ in range(2)])
        ChT_rot = Rot([P.sbuf([128, 8, 128], BF16) for _ in range(2)])
        X_rot = Rot([P.sbuf([128, 8, 128], BF16) for _ in range(2)])
        Xd_rot = Rot([P.sbuf([128, 512], BF16) for _ in range(2)])
        Btm_rot = Rot([P.sbuf([128, 128], BF16) for _ in range(2)])
        yg_rot = Rot([P.sbuf([128, 4, 128], F32) for _ in range(2)])
        sq_rot = Rot([P.sbuf([128, 128], F32) for _ in range(3)])
        rstd_rot = Rot([P.sbuf([128, 128], F32) for _ in range(2)])
        H = P.sbuf([128, 512], F32); t_H = Trk()
        Hbf = P.sbuf([128, 8, 128], BF16); t_Hbf = Trk()
        yo_rot = Rot([P.sbuf([128, 4, SC], BF16) for _ in range(2)])
        ps_bc = [P.psum([128, 512], F32) for _ in range(2)]; t_bc = Trk()
        ps_tr = P.psum([128, 1024], BF16); t_tr = Trk()
        ps_tq = P.psum([128, 512], F32); t_tq = Trk()
        ps_y_rot = Rot([P.psum([128, 512], F32) for _ in range(2)])
        ps_st = P.psum([128, 512], F32); t_st = Trk()
        ps_n = P.psum([128, 512], F32); t_n = Trk()
        t_out = Trk("out")
        xv = xbc.rearrange("(c p) t -> p c t", p=128)
        zv = zT.rearrange("(c p) t -> p c t", p=128)
        ov = o_ys.rearrange("(c p) t -> p c t", p=128)

        P.dma("sp", H[:, :], h_in[:, :], writes=(t_H,), key="hin")
        for pbz in prep_rot.bufs:
            P.op("pool", (lambda e, pbz=pbz: e.memset(pbz["dts"][:, :], 0.0)), writes=(pbz["t_dts"],))
        for bi, bz in enumerate(acs_rot.bufs):
            P.op("pool", (lambda e, bz=bz: e.memset(bz[:, :], 0.0)), writes=(acs_rot.trk[bi],))
        for bi, bz in enumerate(X_rot.bufs):
            P.op("pool", (lambda e, bz=bz: e.memset(bz[:, :, :], 0.0)), writes=(X_rot.trk[bi],))
        P.op("pool", (lambda e: e.memset(Hbf[:, :, :], 0.0)), writes=(t_Hbf,))

        def h_to_bf():
            for hh in range(8):
                ee = hh % 2
                P.op("act", (lambda e, hh=hh, ee=ee: e.activation(out=Hbf[:, hh, ee * 64:(ee + 1) * 64], in_=H[:, hh * 64:(hh + 1) * 64],
                                                                  func=AF.Copy)), reads=(t_H,), writes=(t_Hbf,))
        h_to_bf()

        def prep(sc):
            pb, _ = prep_rot.next()
            k = str(prep_rot.i)
            raw, zr, dtr, xc, sz, dts, aT = pb["raw"], pb["zr"], pb["dtr"], pb["xc"], pb["sz"], pb["dts"], pb["aT"]
            t0 = sc * SC
            P.dma("sp", raw[:, :, :], xv[:, :, t0:t0 + SC + 4], writes=(pb["t_raw"],), key="raw" + k)
            P.dma("sp", zr[:, :, :], zv[:, :, t0:t0 + SC], writes=(pb["t_zr"],), key="zr" + k)
            P.dma("sp", dtr[:, :], dtT[:, t0:t0 + SC], writes=(pb["t_dtr"],), key="dtr" + k)
            for ch in range(6):
                ct, t_ct = ct_rot.next()
                P.op("pool", (lambda e, ct=ct, ch=ch: e.tensor_scalar(out=ct[:, :], in0=raw[:, ch, 1:1 + SC], scalar1=cw_sb[:, ch * 4:ch * 4 + 1],
                                                                       scalar2=cb_sb[:, ch:ch + 1], op0=ALU.mult, op1=ALU.add)),
                     reads=(pb["t_raw"], t_c), writes=(t_ct,))
                for kk in (1, 2, 3):
                    P.op("dve", (lambda e, ct=ct, ch=ch, kk=kk: e.scalar_tensor_tensor(
                        out=ct[:, :], in0=raw[:, ch, 1 + kk:1 + kk + SC], scalar=cw_sb[:, ch * 4 + kk:ch * 4 + kk + 1], in1=ct[:, :],
                        op0=ALU.mult, op1=ALU.add)), reads=(pb["t_raw"], t_c, t_ct), writes=(t_ct,))
                P.op("act", (lambda e, ct=ct, ch=ch: e.activation(out=xc[:, ch, :], in_=ct[:, :], func=AF.Silu)),
                     reads=(t_ct,), writes=(pb["t_xc"],))
            for pc in range(4):
                P.op("act", (lambda e, pc=pc: e.activation(out=sz[:, pc, :], in_=zr[:, pc, :], func=AF.Silu)),
                     reads=(pb["t_zr"],), writes=(pb["t_sz"],))
            P.op("act", (lambda e: e.activation(out=dts[:8, :], in_=dtr[:, :], func=AF.Exp, bias=dtb_sb[:8, 0:1])),
                 reads=(pb["t_dtr"], t_c), writes=(pb["t_dts"],))
            P.op("act", (lambda e: e.activation(out=dts[:8, :], in_=dts[:8, :], func=AF.Ln, bias=one_sb[:8, 0:1])),
                 reads=(pb["t_dts"], t_c), writes=(pb["t_dts"],))
            P.op("dve", (lambda e: e.tensor_scalar(out=aT[:, :], in0=dts[:8, :], scalar1=A_sb[:8, 0:1], scalar2=None, op0=ALU.mult)),
                 reads=(pb["t_dts"], t_c), writes=(pb["t_aT"],))
            return pb

        def stage_a(pb, c):
            xc, dts, aT = pb["xc"], pb["dts"], pb["aT"]
            t_xc, t_dts, t_aT = pb["t_xc"], pb["t_dts"], pb["t_aT"]
            cs = slice(c * 128, c * 128 + 128)
            acs, t_acs = acs_rot.next()
            P.op("dve", (lambda e: e.tensor_tensor_scan(out=acs[:8, :], data0=ones_f[:8, :], data1=aT[:, cs],
                                                        initial=0.0, op0=ALU.mult, op1=ALU.add)),
                 reads=(t_aT, t_c), writes=(t_acs,))

            def trs(e):
                e.transpose(ps_tq[:, 0:128], dts[:, cs], idf_sb[:, :])
                return e.transpose(ps_tq[:, 128:256], acs[:, :], idf_sb[:, :])
            P.op("pe", trs, reads=(t_dts, t_acs, t_c), writes=(t_tq,))
            datm, t_datm = datm_rot.next()
            P.op("act", (lambda e: e.activation(out=datm[:, 0:8], in_=ps_tq[:, 0:8], func=AF.Copy)), reads=(t_tq,), writes=(t_datm,))
            P.op("act", (lambda e: e.activation(out=datm[:, 8:16], in_=ps_tq[:, 128:136], func=AF.Copy)), reads=(t_tq,), writes=(t_datm,))

            def bcs(e):
                for hh in range(8):
                    ins = e.matmul(ps_bc[hh // 4][:, (hh % 4) * 128:(hh % 4 + 1) * 128], oh8_sb[:, hh * 128:(hh + 1) * 128],
                                   acs[:, :], start=True, stop=True)
                return ins
            P.op("pe", bcs, reads=(t_acs, t_c), writes=(t_bc,))
            E, t_E = E_rot.next()
            Dp, t_Dp = Dp_rot.next()
            for hh in range(8):
                src = ps_bc[hh // 4][:, (hh % 4) * 128:(hh % 4 + 1) * 128]
                P.op("act", (lambda e, hh=hh, src=src: e.activation(out=E[:, hh, :], in_=src, func=AF.Exp)), reads=(t_bc,), writes=(t_E,))
                P.op("dve", (lambda e, hh=hh, src=src: e.tensor_scalar(out=Dp[:, hh, :], in0=src, scalar1=datm[:, 8 + hh:9 + hh], scalar2=None,
                                                                      op0=ALU.subtract)), reads=(t_bc, t_datm), writes=(t_Dp,))
                P.op("dve", (lambda e, hh=hh: e.tensor_scalar(out=Dp[:, hh, :], in0=Dp[:, hh, :], scalar1=0.0, scalar2=None, op0=ALU.min)),
                     reads=(t_Dp,), writes=(t_Dp,))
                P.op("act", (lambda e, hh=hh: e.activation(out=Dp[:, hh, :], in_=Dp[:, hh, :], func=AF.Exp)), reads=(t_Dp,), writes=(t_Dp,))
            P.op("pe", (lambda e: e.matmul(ps_tq[:, 256:384], xc[:, 4, cs], xc[:, 5, cs], start=True, stop=True)),
                 reads=(t_xc, t_datm), writes=(t_tq,))
            cbm, t_cbm = cbm_rot.next()
            P.op("dve", (lambda e: e.tensor_tensor(out=cbm[:, :], in0=ps_tq[:, 256:384], in1=tri_sb[:, :], op=ALU.mult)),
                 reads=(t_tq, t_c), writes=(t_cbm,))
            MT, t_MT = MT_rot.next()
            ChT, t_ChT = ChT_rot.next()
            for hh in range(8):
                P.op("pool", (lambda e, hh=hh: e.tensor_tensor(out=MT[:, hh, :], in0=Dp[:, hh, :], in1=cbm[:, :], op=ALU.mult)),
                     reads=(t_Dp, t_cbm), writes=(t_MT,))
                P.op("pool", (lambda e, hh=hh: e.tensor_tensor(out=ChT[:, hh, :], in0=E[:, hh, :], in1=xc[:, 5, cs], op=ALU.mult)),
                     reads=(t_E, t_xc), writes=(t_ChT,))

            def trx(e):
                for pc in range(4):
                    e.transpose(ps_tr[:, pc * 128:(pc + 1) * 128], xc[:, pc, cs], idb_sb[:, :])
                return e.transpose(ps_tr[:, 512:640], xc[:, 4, cs], idb_sb[:, :])
            P.op("pe", trx, reads=(t_xc, t_c), writes=(t_tr,))
            X, t_X = X_rot.next()
            Xd, t_Xd = Xd_rot.next()
            Btm, t_Btm = Btm_rot.next()
            P.op("act", (lambda e: e.activation(out=Btm[:, :], in_=ps_tr[:, 512:640], func=AF.Copy)), reads=(t_tr,), writes=(t_Btm,))
            for hh in range(8):
                hs = slice(hh * 64, (hh + 1) * 64)
                P.op("dve", (lambda e, hs=hs, hh=hh: e.tensor_scalar(out=X[:, hh, (hh % 2) * 64:(hh % 2 + 1) * 64], in0=ps_tr[:, hs],
                                                                    scalar1=datm[:, hh:hh + 1], scalar2=None,
                                                                    op0=ALU.mult)), reads=(t_tr, t_datm), writes=(t_X,))
                P.op("dve", (lambda e, hs=hs, hh=hh: e.tensor_scalar(out=Xd[:, hs], in0=ps_tr[:, hs], scalar1=datm[:, hh:hh + 1],
                                                                    scalar2=Dp[:, hh, 127:128], op0=ALU.mult, op1=ALU.mult)),
                     reads=(t_tr, t_datm, t_Dp), writes=(t_Xd,))
            return dict(E=E, t_E=t_E, MT=MT, t_MT=t_MT, ChT=ChT, t_ChT=t_ChT, X=X, t_X=t_X, Xd=Xd, t_Xd=t_Xd, Btm=Btm, t_Btm=t_Btm)

        def stage_b(pb, c, a, yo, t_yo):
            xc, sz = pb["xc"], pb["sz"]
            t_xc, t_sz = pb["t_xc"], pb["t_sz"]
            cs = slice(c * 128, c * 128 + 128)
            E, MT, ChT, X, Xd, Btm = a["E"], a["MT"], a["ChT"], a["X"], a["Xd"], a["Btm"]
            ps_y, t_y = ps_y_rot.next()

            def ymm(e):
                for hp in range(4):
                    out = ps_y[:, hp * 128:(hp + 1) * 128]
                    e.matmul(out, X[:, 2 * hp, :], MT[:, 2 * hp, :], start=True, stop=False)
                    e.matmul(out, X[:, 2 * hp + 1, :], MT[:, 2 * hp + 1, :], start=False, stop=False)
                    e.matmul(out, Hbf[:, 2 * hp, :], ChT[:, 2 * hp, :], start=False, stop=False)
                    ins = e.matmul(out, Hbf[:, 2 * hp + 1, :], ChT[:, 2 * hp + 1, :], start=False, stop=True)
                return ins
            P.op("pe", ymm, reads=(a["t_X"], a["t_MT"], a["t_ChT"], t_Hbf), writes=(t_y,))
            P.op("pe", (lambda e: e.matmul(ps_st[:, :], Btm[:, :], Xd[:, :], start=True, stop=True)),
                 reads=(a["t_Btm"], a["t_Xd"]), writes=(t_st,))
            for hh in range(8):
                hs = slice(hh * 64, (hh + 1) * 64)
                P.op("dve", (lambda e, hs=hs, hh=hh: e.scalar_tensor_tensor(
                    out=H[:, hs], in0=H[:, hs], scalar=E[:, hh, 127:128], in1=ps_st[:, hs], op0=ALU.mult, op1=ALU.add)),
                    reads=(t_st, a["t_E"], t_H, t_y), writes=(t_H,))
            h_to_bf()
            yg, t_yg = yg_rot.next()
            for hp in range(4):
                P.op("dve", (lambda e, hp=hp: e.scalar_tensor_tensor(
                    out=yg[:, hp, :], in0=xc[:, hp, cs], scalar=dsk_sb[:, hp:hp + 1], in1=ps_y[:, hp * 128:(hp + 1) * 128],
                    op0=ALU.mult, op1=ALU.add)), reads=(t_xc, t_y, t_c), writes=(t_yg,))
                P.op("pool", (lambda e, hp=hp: e.tensor_tensor(out=yg[:, hp, :], in0=yg[:, hp, :], in1=sz[:, hp, cs], op=ALU.mult)),
                     reads=(t_yg, t_sz), writes=(t_yg,))
            for hp in range(4):
                sq, t_sq = sq_rot.next()
                P.op("pool", (lambda e, sq=sq, hp=hp: e.tensor_tensor(out=sq[:, :], in0=yg[:, hp, :], in1=yg[:, hp, :], op=ALU.mult)),
                     reads=(t_yg,), writes=(t_sq,))
                P.op("pe", (lambda e, sq=sq, hp=hp: e.matmul(ps_n[:, :128], ones_f[:, :], sq[:, :], start=(hp == 0), stop=(hp == 3))),
                     reads=(t_sq, t_c), writes=(t_n,))
            rstd, t_rstd = rstd_rot.next()
            P.op("act", (lambda e: e.activation(out=rstd[:, :], in_=ps_n[:, :128], func=AF.Ln, bias=eps_sb[:, 0:1], scale=1.0 / 512.0)),
                 reads=(t_n, t_c), writes=(t_rstd,))
            P.op("act", (lambda e: e.activation(out=rstd[:, :], in_=rstd[:, :], func=AF.Exp, scale=-0.5)), reads=(t_rstd,), writes=(t_rstd,))
            for hp in range(4):
                P.op("dve", (lambda e, hp=hp: e.scalar_tensor_tensor(
                    out=yo[:, hp, cs], in0=yg[:, hp, :], scalar=nw_sb[:, hp:hp + 1], in1=rstd[:, :], op0=ALU.mult, op1=ALU.mult)),
                    reads=(t_yg, t_rstd, t_c), writes=(t_yo,))

        chunks = [(sc, c) for sc in range(NSC) for c in range(NCH)]
        pbs = {}
        yos = {}
        if pipe == 0:
            for i, (sc, c) in enumerate(chunks):
                if c == 0:
                    pbs[sc] = prep(sc)
                    yos[sc] = yo_rot.next()
                a_cur = stage_a(pbs[sc], c)
                yo, t_yo = yos[sc]
                stage_b(pbs[sc], c, a_cur, yo, t_yo)
                if c == NCH - 1:
                    P.dma("pool", ov[:, :, sc * SC:(sc + 1) * SC], yo[:, :, :], reads=(t_yo,), writes=(t_out,), key=f"oys{sc % 2}")
        else:
            a_cur = None
            for i, (sc, c) in enumerate(chunks):
                if c == 0:
                    if sc not in pbs:
                        pbs[sc] = prep(sc)
                    yos[sc] = yo_rot.next()
                    if pipe == 2 and sc + 1 < NSC:
                        pbs[sc + 1] = prep(sc + 1)
                if a_cur is None:
                    a_cur = stage_a(pbs[sc], c)
                a_next = None
                if i + 1 < len(chunks):
                    nsc, ncc = chunks[i + 1]
                    if nsc == sc or pipe == 2:
                        a_next = stage_a(pbs[nsc], ncc)
                yo, t_yo = yos[sc]
                stage_b(pbs[sc], c, a_cur, yo, t_yo)
                a_cur = a_next
                if c == NCH - 1:
                    P.dma("pool", ov[:, :, sc * SC:(sc + 1) * SC], yo[:, :, :], reads=(t_yo,), writes=(t_out,), key=f"oys{sc % 2}")
        P.dma("pool", o_h[:, :], H[:, :], reads=(t_H,), writes=(t_out,), key="oh")
        P.finish((t_out,))
        P.emit()
    return nc


_PROGS = {}


def _prog(name, fn):
    if name not in _PROGS:
        _PROGS[name] = fn()
    return _PROGS[name]


def _lay(v):
    return np.ascontiguousarray(np.asarray(v, np.float32).reshape(-1, 128).T)


def _tile_cols(w, ncol):
    K, N = w.shape
    t = np.asarray(w, np.float32).reshape(K // 128, 128, N // ncol, ncol).transpose(2, 1, 0, 3)
    return t


def _run(nc, in_maps):
    res = run_bass_kernel_spmd(nc, in_maps, core_ids=list(range(NCORES)))
    return res.results


def kernel(x, mem, norm_mix_pre, norm_mix_post, norm_ffn_pre, norm_ffn_post, norm_mem,
           w_in, conv_ssd_w, conv_ssd_b, dt_bias, a_log, d_skip, ssd_norm, w_mem_kv,
           w_br_attn, w_br_ssd, w_br_mem, w_out, w_up, conv_ffn_w, conv_ffn_b, w_down):
    f32 = lambda a: np.asarray(a, np.float32)
    x = f32(x)
    mem = f32(mem)
    B, S, _ = x.shape
    T = S * B // NCORES
    J = S // T
    depth = w_in.shape[0]
    p1 = _prog("p1", lambda: build_p1(T))
    p2a = _prog("p2a", lambda: build_p2a(S))
    p2b = _prog("p2b", lambda: build_p2b(2048, SC=512))
    p3a = _prog("p3a", lambda: build_p3a(T))
    p3b = _prog("p3b", lambda: build_p3b(T))
    atab = attn_tables(S)
    stab = ssd_tables()
    cores = [(b, j) for b in range(B) for j in range(J)]
    xT = [np.ascontiguousarray(x[b, j * T:(j + 1) * T, :].T) for (b, j) in cores]
    memT = [np.ascontiguousarray(mem[b].T) for b in range(B)]
    pad8 = lambda v: np.ascontiguousarray(np.broadcast_to(np.pad(f32(v), (0, 120))[:, None], (128, 2)))
    for l in range(depth):
        w_in_l = f32(w_in[l])
        nw = _lay(norm_mix_pre[l])
        r1 = _run(p1, [dict(xT=xT[c], nw=nw, w_in=w_in_l) for c in range(NCORES)])
        del w_in_l
        full = lambda key, b: np.concatenate([r1[b * J + j][key] for j in range(J)], axis=1)
        qk_f = [full("o_qk", b) for b in range(B)]
        z_f = [full("o_z", b) for b in range(B)]
        xbc_f = [full("o_xbc", b) for b in range(B)]
        dt_f = [full("o_dt", b) for b in range(B)]
        v_f = [np.concatenate([r1[b * J + j]["o_v"] for j in range(J)], axis=0) for b in range(B)]
        maps = []
        for (b, g) in cores:
            hs = (2 * g, 2 * g + 1)
            qk = np.stack([qk_f[b][h * 128:(h + 1) * 128] for h in hs] + [qk_f[b][1024 + h * 128:1024 + (h + 1) * 128] for h in hs])
            qks = np.ascontiguousarray(np.concatenate([qk[:, 64:], qk[:, :64]], axis=1))
            v = np.ascontiguousarray(np.stack([v_f[b][:, h * 128:(h + 1) * 128] for h in hs]))
            maps.append(dict(qk=np.ascontiguousarray(qk), qks=qks, v=v, **atab))
        r2a = _run(p2a, maps)
        cw_l = f32(conv_ssd_w[l])
        cb_l = f32(conv_ssd_b[l])
        SEG = 2048
        hst = [np.zeros((128, 512), np.float32) for _ in range(NCORES)]
        ys_parts = [[] for _ in range(NCORES)]
        for sg in range(S // SEG):
            maps = []
            for c, (b, g) in enumerate(cores):
                ch = np.concatenate([np.arange(g * 512, (g + 1) * 512), 2048 + np.arange(g * 128, (g + 1) * 128),
                                     2560 + np.arange(g * 128, (g + 1) * 128)])
                cw = np.ascontiguousarray(cw_l[:, ch].T.reshape(6, 128, 4).transpose(1, 0, 2).reshape(128, 24))
                seg = xbc_f[b][ch][:, sg * SEG:(sg + 1) * SEG]
                halo = xbc_f[b][ch][:, sg * SEG - 4:sg * SEG] if sg > 0 else np.zeros((768, 4), seg.dtype)
                maps.append(dict(xbc=np.ascontiguousarray(np.concatenate([halo, seg], axis=1)), h_in=hst[c],
                                 zT=np.ascontiguousarray(z_f[b][g * 512:(g + 1) * 512, sg * SEG:(sg + 1) * SEG]),
                                 dtT=np.ascontiguousarray(dt_f[b][g * 8:(g + 1) * 8, sg * SEG:(sg + 1) * SEG]), cw=cw, cb=_lay(cb_l[ch]),
                                 dtb=pad8(dt_bias[l][g * 8:(g + 1) * 8]), alog=pad8(a_log[l][g * 8:(g + 1) * 8]),
                                 dsk=_lay(np.repeat(f32(d_skip[l][g * 8:(g + 1) * 8]), 64)),
                                 nw=_lay(f32(ssd_norm[l])[g * 512:(g + 1) * 512]), **stab))
            rr = _run(p2b, maps)
            for c in range(NCORES):
                hst[c] = np.ascontiguousarray(rr[c]["o_h"])
                ys_parts[c].append(rr[c]["o_ys"])
        r2b = [dict(o_ys=np.concatenate(ys_parts[c], axis=1)) for c in range(NCORES)]
        ya_f = [np.concatenate([r2a[b * J + g]["o_ya"] for g in range(J)], axis=0) for b in range(B)]
        ys_f = [np.concatenate([r2b[b * J + g]["o_ys"] for g in range(J)], axis=0) for b in range(B)]
        del qk_f, z_f, xbc_f, dt_f, v_f
        w_br_t = np.concatenate([_tile_cols(w_br_attn[l], 256), _tile_cols(w_br_ssd[l], 256), _tile_cols(w_br_mem[l], 256)], axis=2)
        wts = dict(w_kv=np.ascontiguousarray(_tile_cols(w_mem_kv[l], 512)).reshape(4, 128, KC * 512),
                   w_br=np.ascontiguousarray(w_br_t).reshape(8, 128, 32 * 256),
                   w_o=np.ascontiguousarray(_tile_cols(w_out[l], 256)).reshape(8, 128, KC * 256),
                   nw_mem=_lay(norm_mem[l]), nw_post=_lay(norm_mix_post[l]), nw_pre=_lay(norm_ffn_pre[l]))
        maps = []
        for c, (b, j) in enumerate(cores):
            ts = slice(j * T, (j + 1) * T)
            maps.append(dict(xT=xT[c], yaT=np.ascontiguousarray(ya_f[b][:, ts]), ysT=np.ascontiguousarray(ys_f[b][:, ts]),
                             qmT=r1[c]["o_qm"], gT=r1[c]["o_g"], memT=memT[b], **wts))
        r3a = _run(p3a, maps)
        del r1, r2a, r2b, ya_f, ys_f, wts, maps
        cwf = f32(conv_ffn_w[l])
        cw = np.ascontiguousarray(cwf.T.reshape(2 * NF, 128, 3).transpose(1, 0, 2).reshape(128, 2 * NF * 3))
        wu = f32(w_up[l])
        w_up_t = np.stack([_tile_cols(wu[:, :DFF], 256), _tile_cols(wu[:, DFF:], 256)], axis=2)
        wts = dict(w_up=np.ascontiguousarray(w_up_t).reshape(NF // 2, 128, 2 * KC * 256),
                   w_dn=np.ascontiguousarray(_tile_cols(w_down[l], 256)).reshape(8, 128, NF * 256),
                   cw=cw, cb=_lay(conv_ffn_b[l]), nw=_lay(norm_ffn_post[l]))
        del wu, w_up_t
        maps = []
        for c, (b, j) in enumerate(cores):
            h2 = r3a[c]["o_h2"]
            halo = r3a[c - 1]["o_h2"][:, -2:] if j > 0 else np.zeros((D, 2), h2.dtype)
            maps.append(dict(h2T=np.ascontiguousarray(np.concatenate([halo, h2], axis=1)), xmT=r3a[c]["o_xm"], **wts))
        r3b = _run(p3b, maps)
        xT = [np.ascontiguousarray(r3b[c]["o_x"]) for c in range(NCORES)]
        del r3a, r3b, wts, maps
    out = np.empty((B, S, D), np.float32)
    for c, (b, j) in enumerate(cores):
        out[b, j * T:(j + 1) * T, :] = xT[c].T
    return out
```
